# Optimizing a Trainium2 kernel written in Bass

```python
import math
import jax
import jax.numpy as jnp
from jax import lax
import numpy as np

D_MODEL = 2048
BATCH = 4
SEQ = 4096
DEPTH = 4

BLOCK = 128
N_BRANCH = 4
BRANCH_W = D_MODEL // 4
A_HEADS = 4
A_HD = BRANCH_W // (2 * A_HEADS)
B_WIDTH = BRANCH_W
B_BLOCKS = 8
B_CONV = 4
B_C = 8.0
C_HEADS = 4
C_HD = BRANCH_W // C_HEADS
D_HEADS = 8
D_KV = 2
D_GROUP = D_HEADS // D_KV
D_HD = BRANCH_W // D_HEADS
WINDOW = 128
D_FF = -(-8 * D_MODEL // (3 * 256)) * 256

IN_SIZES = (
    2 * A_HEADS * A_HD, 2 * A_HEADS * A_HD, 2 * A_HEADS * A_HD,
    B_WIDTH, B_WIDTH,
    C_HEADS * C_HD, C_HEADS * C_HD, C_HEADS * C_HD, C_HEADS,
    D_HEADS * D_HD, D_KV * D_HD, D_KV * D_HD,
    N_BRANCH * D_MODEL,
)
IN_COLS = sum(IN_SIZES)
RMS_EPS = 1e-6
NEG_INF = -1e30

kernel_name = 'hybrid_gated_diff_rglru_fox_swa_block'


def rms_norm(x, g):
    xf = x.astype(jnp.float32)
    y = xf * lax.rsqrt(jnp.mean(xf * xf, axis=-1, keepdims=True) + RMS_EPS)
    return (y * g.astype(jnp.float32)).astype(x.dtype)


def alibi_slopes(n):
    return jnp.exp2(-8.0 * jnp.arange(1, n + 1, dtype=jnp.float32) / n)


def diff_attention(q, k, v, lam, slopes):
    B, S, H, _, dh = q.shape
    nb = S // BLOCK
    scale = dh ** -0.5
    qb = q.reshape(B, nb, BLOCK, H, 2, dh).transpose(1, 0, 2, 3, 4, 5)
    kpos = jnp.arange(S)

    def one_block(args):
        qblk, i = args
        qpos = i * BLOCK + jnp.arange(BLOCK)
        dist = (qpos[:, None] - kpos[None, :]).astype(jnp.float32)
        logits = jnp.einsum('bqhmd,bshmd->bmhqs', qblk, k).astype(jnp.float32) * scale
        logits = logits - slopes[:, None, None] * dist
        logits = jnp.where(dist >= 0, logits, NEG_INF)
        p = jax.nn.softmax(logits, axis=-1)
        w = p[:, 0] - lam * p[:, 1]
        return jnp.einsum('bhqs,bshe->bqhe', w.astype(v.dtype), v)

    out = lax.map(one_block, (qb, jnp.arange(nb)))
    return out.transpose(1, 0, 2, 3, 4).reshape(B, S, H, 2 * dh)


def rg_lru_branch(xb, gb, conv_w, conv_b, wa, ba, wx, bx, lam):
    B, S, C = xb.shape
    xc = lax.conv_general_dilated(xb, conv_w[:, None, :], window_strides=(1,),
                                  padding=[(B_CONV - 1, 0)],
                                  dimension_numbers=('NWC', 'WIO', 'NWC'),
                                  feature_group_count=C) + conv_b
    xg = xc.reshape(B, S, B_BLOCKS, C // B_BLOCKS)
    r = jax.nn.sigmoid(jnp.einsum('bsnc,ncd->bsnd', xg, wa).reshape(B, S, C) + ba)
    i = jax.nn.sigmoid(jnp.einsum('bsnc,ncd->bsnd', xg, wx).reshape(B, S, C) + bx)
    log_a = -B_C * r.astype(jnp.float32) * jax.nn.softplus(-lam.astype(jnp.float32))
    a = jnp.exp(log_a)
    u = jnp.sqrt(-jnp.expm1(2.0 * log_a)) * (i * xc).astype(jnp.float32)

    def combine(left, right):
        a1, b1 = left
        a2, b2 = right
        return a1 * a2, a2 * b1 + b2

    _, h = lax.associative_scan(combine, (a, u), axis=1)
    return jax.nn.gelu(gb) * h.astype(xb.dtype)


def forgetting_attention(q, k, v, logf):
    B, S, H, dh = q.shape
    nb = S // BLOCK
    scale = dh ** -0.5
    cum = jnp.cumsum(logf, axis=1).transpose(0, 2, 1)
    qb = q.reshape(B, nb, BLOCK, H, dh).transpose(1, 0, 2, 3, 4)
    cqb = cum.reshape(B, H, nb, BLOCK).transpose(2, 0, 1, 3)
    kpos = jnp.arange(S)

    def one_block(args):
        qblk, cq, i = args
        qpos = i * BLOCK + jnp.arange(BLOCK)
        causal = qpos[:, None] >= kpos[None, :]
        logits = jnp.einsum('bqhd,bshd->bhqs', qblk, k).astype(jnp.float32) * scale
        logits = logits + cq[..., None] - cum[:, :, None, :]
        logits = jnp.where(causal, logits, NEG_INF)
        p = jax.nn.softmax(logits, axis=-1)
        return jnp.einsum('bhqs,bshd->bqhd', p.astype(v.dtype), v)

    out = lax.map(one_block, (qb, cqb, jnp.arange(nb)))
    return out.transpose(1, 0, 2, 3, 4).reshape(B, S, H * dh)


def sliding_window_sink_attention(q, k, v, sinks, slopes):
    B, S, KV, G, dh = q.shape
    nb = S // BLOCK
    scale = dh ** -0.5
    pad = ((0, 0), (BLOCK, 0), (0, 0), (0, 0))

    def band(t):
        tp = jnp.pad(t, pad)
        prev = tp[:, :S].reshape(B, nb, BLOCK, KV, dh)
        cur = tp[:, BLOCK:].reshape(B, nb, BLOCK, KV, dh)
        return jnp.concatenate([prev, cur], axis=2)

    kb, vb = band(k), band(v)
    qb = q.reshape(B, nb, BLOCK, KV, G, dh)
    qi = jnp.arange(BLOCK)
    kj = jnp.arange(2 * BLOCK) - BLOCK
    dist = (qi[:, None] - kj[None, :]).astype(jnp.float32)
    key_pos = jnp.arange(nb)[:, None] * BLOCK + kj[None, :]
    valid = ((dist >= 0) & (dist < WINDOW))[None] & (key_pos >= 0)[:, None, :]
    logits = jnp.einsum('bnqkgd,bnskd->bnkgqs', qb, kb).astype(jnp.float32) * scale
    logits = logits - slopes.reshape(KV, G)[:, :, None, None] * dist
    logits = jnp.where(valid[None, :, None, None], logits, NEG_INF)
    sink = sinks.astype(jnp.float32).reshape(KV, G)[None, None, :, :, None, None]
    m = jnp.maximum(jnp.max(logits, axis=-1, keepdims=True), sink)
    e = jnp.exp(logits - m)
    p = e / (jnp.sum(e, axis=-1, keepdims=True) + jnp.exp(sink - m))
    out = jnp.einsum('bnkgqs,bnskd->bnqkgd', p.astype(v.dtype), vb)
    return out.reshape(B, S, KV * G * dh)


def setup_inputs(seed: int = 0) -> dict:
    key = jax.random.key(seed)
    ks = jax.random.split(key, 26)
    f32 = jnp.float32

    def nrm(k, shape, scale):
        return jax.random.normal(k, shape, f32) * scale

    bw = B_WIDTH // B_BLOCKS
    a8 = jax.random.uniform(ks[15], (DEPTH, B_WIDTH), f32, 0.9, 0.999)
    a0 = a8 ** (1.0 / B_C)
    return {
        'x': nrm(ks[0], (BATCH, SEQ, D_MODEL), 1.0),
        'norm_mix': 1.0 + nrm(ks[1], (DEPTH, D_MODEL), 0.02),
        'w_in': nrm(ks[2], (DEPTH, D_MODEL, IN_COLS), D_MODEL ** -0.5),
        'b_gate': nrm(ks[3], (DEPTH, N_BRANCH * D_MODEL), 0.02),
        'diff_lq1': nrm(ks[4], (DEPTH, A_HD), 0.1),
        'diff_lk1': nrm(ks[5], (DEPTH, A_HD), 0.1),
        'diff_lq2': nrm(ks[6], (DEPTH, A_HD), 0.1),
        'diff_lk2': nrm(ks[7], (DEPTH, A_HD), 0.1),
        'diff_subln': 1.0 + nrm(ks[8], (DEPTH, 2 * A_HD), 0.02),
        'lru_conv_w': nrm(ks[9], (DEPTH, B_CONV, B_WIDTH), B_CONV ** -0.5),
        'lru_conv_b': nrm(ks[10], (DEPTH, B_WIDTH), 0.02),
        'lru_wa': nrm(ks[11], (DEPTH, B_BLOCKS, bw, bw), bw ** -0.5),
        'lru_ba': nrm(ks[12], (DEPTH, B_WIDTH), 0.02),
        'lru_wx': nrm(ks[13], (DEPTH, B_BLOCKS, bw, bw), bw ** -0.5),
        'lru_bx': nrm(ks[14], (DEPTH, B_WIDTH), 0.02),
        'lru_lambda': jnp.log(a0) - jnp.log1p(-a0),
        'fox_b_f': 2.0 + nrm(ks[16], (DEPTH, C_HEADS), 0.5),
        'swa_sinks': nrm(ks[17], (DEPTH, D_HEADS), 0.5),
        'w_branch': nrm(ks[18], (DEPTH, N_BRANCH, BRANCH_W, D_MODEL), BRANCH_W ** -0.5),
        'w_out': nrm(ks[19], (DEPTH, D_MODEL, D_MODEL), D_MODEL ** -0.5),
        'norm_ffn': 1.0 + nrm(ks[20], (DEPTH, D_MODEL), 0.02),
        'w_ffn_gate': nrm(ks[21], (DEPTH, D_MODEL, D_FF), D_MODEL ** -0.5),
        'w_ffn_up': nrm(ks[22], (DEPTH, D_MODEL, D_FF), D_MODEL ** -0.5),
        'w_ffn_down': nrm(ks[23], (DEPTH, D_FF, D_MODEL), D_FF ** -0.5),
        'norm_final': 1.0 + nrm(ks[24], (D_MODEL,), 0.02),
    }


def reference(x, norm_mix, w_in, b_gate, diff_lq1, diff_lk1, diff_lq2, diff_lk2, diff_subln,
              lru_conv_w, lru_conv_b, lru_wa, lru_ba, lru_wx, lru_bx, lru_lambda,
              fox_b_f, swa_sinks, w_branch, w_out, norm_ffn, w_ffn_gate, w_ffn_up, w_ffn_down,
              norm_final):
    B, S, _ = x.shape
    slopes_a = alibi_slopes(A_HEADS)
    slopes_d = alibi_slopes(D_HEADS)
    split_points = []
    acc = 0
    for size in IN_SIZES[:-1]:
        acc += size
        split_points.append(acc)

    h = x
    for l in range(DEPTH):
        u = rms_norm(h, norm_mix[l])
        z = u @ w_in[l]
        (aq, ak, av, bxin, bgate, cq, ck, cv, cf, dq, dk, dv, gz) = jnp.split(z, split_points, axis=-1)

        lam_init = 0.8 - 0.6 * math.exp(-0.3 * l)
        lam = (jnp.exp(jnp.sum(diff_lq1[l].astype(jnp.float32) * diff_lk1[l].astype(jnp.float32)))
               - jnp.exp(jnp.sum(diff_lq2[l].astype(jnp.float32) * diff_lk2[l].astype(jnp.float32)))
               + lam_init)
        ya = diff_attention(aq.reshape(B, S, A_HEADS, 2, A_HD), ak.reshape(B, S, A_HEADS, 2, A_HD),
                            av.reshape(B, S, A_HEADS, 2 * A_HD), lam, slopes_a)
        ya = (rms_norm(ya, diff_subln[l]) * (1.0 - lam_init)).reshape(B, S, BRANCH_W)

        yb = rg_lru_branch(bxin, bgate, lru_conv_w[l], lru_conv_b[l], lru_wa[l], lru_ba[l],
                           lru_wx[l], lru_bx[l], lru_lambda[l])

        logf = jax.nn.log_sigmoid((cf + fox_b_f[l]).astype(jnp.float32))
        yc = forgetting_attention(cq.reshape(B, S, C_HEADS, C_HD), ck.reshape(B, S, C_HEADS, C_HD),
                                  cv.reshape(B, S, C_HEADS, C_HD), logf)

        yd = sliding_window_sink_attention(dq.reshape(B, S, D_KV, D_GROUP, D_HD),
                                           dk.reshape(B, S, D_KV, D_HD), dv.reshape(B, S, D_KV, D_HD),
                                           swa_sinks[l], slopes_d)

        gates = jax.nn.sigmoid(gz + b_gate[l]).reshape(B, S, N_BRANCH, D_MODEL)
        branches = (ya, yb, yc, yd)
        mixed = gates[:, :, 0] * (branches[0] @ w_branch[l, 0])
        for n in range(1, N_BRANCH):
            mixed = mixed + gates[:, :, n] * (branches[n] @ w_branch[l, n])
        h = h + mixed @ w_out[l]

        v = rms_norm(h, norm_ffn[l])
        h = h + (jax.nn.silu(v @ w_ffn_gate[l]) * (v @ w_ffn_up[l])) @ w_ffn_down[l]

    return rms_norm(h, norm_final)
```

```python
import math
from contextlib import ExitStack

import numpy as np
import concourse.bass as bass
import concourse.mybir as mybir
from concourse.bass_utils import run_bass_kernel_spmd

F32 = mybir.dt.float32
BF16 = mybir.dt.bfloat16
U8 = mybir.dt.uint8
AF = mybir.ActivationFunctionType
ALU = mybir.AluOpType
AX = mybir.AxisListType

D = 2048
KC = D // 128
DFF = 5632
FC = DFF // 128
BW = 512
NCOLS_IN = 13060
RMS_EPS = 1e-6
NEG = -1.0e30
SLOPES_A = [2.0 ** (-8.0 * i / 4) for i in range(1, 5)]
SLOPES_D = [2.0 ** (-8.0 * i / 8) for i in range(1, 9)]
C_AQ, C_AK, C_AV, C_BX, C_BG, C_CQ, C_CK, C_CV, C_CF, C_DQ, C_DK, C_DV, C_GZ = (
    0, 512, 1024, 1536, 2048, 2560, 3072, 3584, 4096, 4100, 4612, 4740, 4868)
PV_BG, PV_CW, PV_CB, PV_BA, PV_BX, PV_LAM, PV_SUB = 0, 64, 80, 84, 88, 92, 96
NPV = 97
RV_LQ1, RV_LK1, RV_LQ2, RV_LK2, RV_SINK, RV_BF = 0, 64, 128, 192, 256, 264
NRV = 268

SAME_ENGINE_SYNC = True


class Prog:
    ENGS = ["pe", "act", "dve", "pool", "sp"]

    def __init__(self, nc, n_sp=40, n_pool=16):
        self.nc = nc
        self.ops = {e: [] for e in self.ENGS}
        self.sem_names = []
        self.eng_sem = {}
        for e in self.ENGS:
            self.eng_sem[e] = self._new_sem("c_" + e)
        self.count = {e: 0 for e in self.ENGS}
        self.dma_slots = {"sp": [self._new_sem(f"d_sp{i}") for i in range(n_sp)],
                          "pool": [self._new_sem(f"d_pl{i}") for i in range(n_pool)]}
        self.dma_next = {"sp": 0, "pool": 0}
        self.sem_val = {}
        self.last_write = {}
        self.readers = {}

    def _new_sem(self, name):
        self.sem_names.append(name)
        return len(self.sem_names) - 1

    def add(self, eng, fn, reads=(), writes=(), dma=False):
        deps = {}

        def dep(tok):
            s, v = tok
            if deps.get(s, 0) < v:
                deps[s] = v
        for r in reads:
            t = self.last_write.get(r)
            if t is not None:
                dep(t)
        for w in writes:
            t = self.last_write.get(w)
            if t is not None:
                dep(t)
            for s, v in self.readers.get(w, {}).items():
                dep((s, v))
        if dma:
            slots = self.dma_slots[eng]
            j = self.dma_next[eng]
            self.dma_next[eng] = (j + 1) % len(slots)
            sem = slots[j]
            prev = self.sem_val.get(sem, 0)
            if prev:
                dep((sem, prev))
            tok = (sem, prev + 16)
            inc = 16
        else:
            sem = self.eng_sem[eng]
            self.count[eng] += 1
            tok = (sem, self.count[eng])
            inc = 1
        self.sem_val[sem] = tok[1]
        for r in reads:
            d = self.readers.setdefault(r, {})
            if d.get(tok[0], 0) < tok[1]:
                d[tok[0]] = tok[1]
        for w in writes:
            self.last_write[w] = tok
            self.readers[w] = {}
        self.ops[eng].append((fn, deps, tok, inc))
        return tok

    def wait_all(self, eng, toks):
        deps = {}
        for s, v in toks:
            if deps.get(s, 0) < v:
                deps[s] = v
        self.ops[eng].append((None, deps, None, 0))

    def barrier(self):
        frontier = [(s, v) for s, v in self.sem_val.items()]
        for e in self.ENGS:
            self.wait_all(e, frontier)

    def emit(self, final_toks):
        nc = self.nc
        with ExitStack() as st:
            sems = [st.enter_context(nc.semaphore(n)) for n in self.sem_names]
            self.wait_all("sp", final_toks)
            block = st.enter_context(nc.Block())

            def run(engname):
                def body(eng):
                    known = {}
                    own = self.eng_sem[engname]
                    for fn, deps, tok, inc in self.ops[engname]:
                        for s in sorted(deps):
                            v = deps[s]
                            if s == own and (engname == "pe" or not SAME_ENGINE_SYNC):
                                continue
                            if known.get(s, 0) >= v:
                                continue
                            eng.wait_ge(sems[s], v)
                            known[s] = v
                        if fn is not None:
                            inst = fn(eng)
                            inst.then_inc(sems[tok[0]], inc)
                return body
            block.tensor(run("pe"))
            block.scalar(run("act"))
            block.vector(run("dve"))
            block.gpsimd(run("pool"))
            block.sync(run("sp"))


class Arena:
    def __init__(self, nc, nbytes):
        self.t = nc.alloc_sbuf_tensor("arena", [128, nbytes], U8)
        self.ap = self.t.ap() if hasattr(self.t, "ap") else self.t[:]
        self.nbytes = nbytes
        self.off = 0
        self.base = 0

    def set_base(self):
        self.base = self.off

    def reset(self):
        self.off = self.base

    def alloc(self, shape, dt, parts=128):
        esz = 4 if dt == F32 else 2
        n = int(np.prod(shape)) * esz
        assert self.off + n <= self.nbytes, f"SBUF arena overflow {self.off}+{n}>{self.nbytes}"
        a = self.ap[0:parts, self.off:self.off + n].bitcast(dt)
        self.off += (n + 63) // 64 * 64
        if len(shape) == 2:
            a = a.rearrange("p (a b) -> p a b", b=shape[1])
        elif len(shape) == 3:
            a = a.rearrange("p (a b c) -> p a b c", b=shape[1], c=shape[2])
        return a


def build_program(S, depth):
    assert S % 512 == 0
    NT = S // 128
    NR = S // 512
    TGI = min(2048, S)
    TGF = min(1024, S)
    TL = min(1024, S)
    nc = bass.Bass("TRN2", target_bir_lowering=False)

    def din(name, shape):
        return nc.dram_tensor(name, shape, F32, kind="ExternalInput").ap()
    x_in = din("x", [S, D])
    w_in = din("w_in", [depth, D, NCOLS_IN])
    w_branch = din("w_branch", [depth, 4 * BW, D])
    w_out = din("w_out", [depth, D, D])
    w_fg = din("w_ffn_gate", [depth, D, DFF])
    w_fu = din("w_ffn_up", [depth, D, DFF])
    w_fd = din("w_ffn_down", [depth, DFF, D])
    lru_wa = din("lru_wa", [depth, 8, 64, 64])
    lru_wx = din("lru_wx", [depth, 8, 64, 64])
    norms = din("norms", [2 * depth + 1, D])
    pvec = din("pvec", [depth, 128, NPV])
    rvec = din("rvec", [depth, NRV])
    out = nc.dram_tensor("out", [S, D], F32, kind="ExternalOutput").ap()

    def dscr(name, shape, dt):
        return nc.dram_tensor(name, shape, dt).ap()
    hres = dscr("hres", [S, D], F32)
    qkA = dscr("qkA", [8, 128, S], BF16)
    vA = dscr("vA", [S, 512], BF16)
    qkC = dscr("qkC", [8, 128, S], BF16)
    vC = dscr("vC", [S, 512], BF16)
    cfT = dscr("cfT", [4, S], F32)
    cumD = dscr("cumD", [4, S], F32)
    qDd = dscr("qD", [8, 64, S], BF16)
    kDd = dscr("kD", [2, 64, S], BF16)
    vDd = dscr("vD", [S, 128], BF16)
    xbT = dscr("xbT", [4, 128, S], F32)
    gbT = dscr("gbT", [4, 128, S], F32)
    gatesD = dscr("gatesD", [64, 128, S], BF16)
    yall = dscr("yall", [16, 128, S], BF16)
    mixD = dscr("mixD", [16, 128, S], BF16)

    P = Prog(nc)
    AR = Arena(nc, 207 * 1024)
    ps = [nc.alloc_psum_tensor(f"ps{i}", [128, 512], F32) for i in range(8)]

    bank_ctr = {}

    def next_bank(lo=0, hi=8):
        c = bank_ctr.get((lo, hi), 0)
        bank_ctr[(lo, hi)] = c + 1
        return lo + c % (hi - lo)

    def OP(eng, method, reads, writes, **kw):
        return P.add(eng, lambda e: getattr(e, method)(**kw), reads=reads, writes=writes)

    def dma(q, out_ap, in_ap, reads, writes):
        return P.add(q, lambda e: e.dma_start(out=out_ap, in_=in_ap), reads=reads, writes=writes, dma=True)

    def mm_group(outp, pairs, bank_res, reads, first=True, last=True):
        def fn(e):
            n = len(pairs)
            for i, (l_, r_) in enumerate(pairs):
                ins = e.matmul(outp, lhsT=l_, rhs=r_, start=(first and i == 0), stop=(last and i == n - 1))
            return ins
        return P.add("pe", fn, reads=reads, writes=[bank_res])

    evac_rr = [0]

    def evac_copy(out_ap, in_ap, reads, writes):
        evac_rr[0] ^= 1
        if evac_rr[0]:
            return OP("act", "copy", reads, writes, out=out_ap, in_=in_ap)
        return OP("dve", "tensor_copy", reads, writes, out=out_ap, in_=in_ap)

    ident_f = AR.alloc([128], F32)
    ones_f = AR.alloc([128], F32)
    ident = AR.alloc([128], BF16)
    ones_b = AR.alloc([128], BF16)
    maskneg = AR.alloc([128], F32)
    rowbA = AR.alloc([4, 512], F32)
    colbA = AR.alloc([4, NT + 4], F32)
    mbD = AR.alloc([8, 256], F32)
    iot = AR.alloc([512], F32)
    iotc = AR.alloc([NT + 4], F32)
    tmpc = AR.alloc([256], F32)
    eps_col = AR.alloc([1], F32)
    AR.set_base()

    OP("pool", "memset", [], ["ones_f"], ap=ones_f, constant=1.0)
    OP("pool", "memset", [], ["eps_col"], ap=eps_col, constant=RMS_EPS)
    OP("pool", "affine_select", ["ones_f"], ["ident_f"], out=ident_f, in_=ones_f, pattern=[[-1, 128]],
       compare_op=ALU.is_equal, fill=0.0, base=0, channel_multiplier=1)
    OP("pool", "memset", [], ["tmpc"], ap=tmpc[:, 0:128], constant=0.0)
    OP("pool", "affine_select", ["tmpc"], ["maskneg"], out=maskneg, in_=tmpc[:, 0:128], pattern=[[1, 128]],
       compare_op=ALU.is_ge, fill=NEG, base=0, channel_multiplier=-1)
    OP("dve", "tensor_copy", ["ident_f"], ["ident"], out=ident, in_=ident_f)
    OP("dve", "tensor_copy", ["ones_f"], ["ones_b"], out=ones_b, in_=ones_f)
    OP("pool", "iota", [], ["iot"], out=iot, pattern=[[1, 512]], base=0, channel_multiplier=0,
       allow_small_or_imprecise_dtypes=True)
    OP("pool", "iota", [], ["iotc"], out=iotc, pattern=[[128, NT + 4]], base=-128 * NT, channel_multiplier=1,
       allow_small_or_imprecise_dtypes=True)
    for h in range(4):
        OP("dve", "tensor_scalar", ["iot"], [("rowbA", h)], out=rowbA[:, h, :], in0=iot, scalar1=-SLOPES_A[h],
           scalar2=None, op0=ALU.mult)
        OP("dve", "tensor_scalar", ["iotc"], [("colbA", h)], out=colbA[:, h, :], in0=iotc, scalar1=SLOPES_A[h],
           scalar2=None, op0=ALU.mult)
    OP("pool", "iota", ["maskneg"], ["tmpc"], out=tmpc, pattern=[[1, 256]], base=0, channel_multiplier=-1,
       allow_small_or_imprecise_dtypes=True)
    for h in range(8):
        OP("dve", "tensor_scalar", ["tmpc"], [("mbD", h)], out=mbD[:, h, :], in0=tmpc, scalar1=-SLOPES_D[h],
           scalar2=None, op0=ALU.mult)
        OP("pool", "affine_select", [("mbD", h)], [("mbD", h)], out=mbD[:, h, :], in_=mbD[:, h, :],
           pattern=[[1, 256]], compare_op=ALU.is_ge, fill=NEG, base=0, channel_multiplier=-1)
        OP("pool", "affine_select", [("mbD", h)], [("mbD", h)], out=mbD[:, h, :], in_=mbD[:, h, :],
           pattern=[[-1, 256]], compare_op=ALU.is_ge, fill=NEG, base=127, channel_multiplier=1)
    P.barrier()

    def norm_transpose(src, row0, ntok, gain_row, uT, uT_res):
        gbc = AR.alloc([D], F32)
        dma("sp", gbc, norms[gain_row:gain_row + 1, :].partition_broadcast(128), [], ["gbc"])
        hx = [AR.alloc([D], F32) for _ in range(2)]
        un = [AR.alloc([D], BF16) for _ in range(2)]
        junk = AR.alloc([D], BF16)
        st = [AR.alloc([4], F32) for _ in range(2)]
        for i in range(ntok // 128):
            b = i % 2
            r0 = row0 + i * 128
            dma("sp", hx[b], src[r0:r0 + 128, :], [], [("hx", b)])
            OP("act", "activation", [("hx", b)], ["junk", ("st", b, 0)], out=junk, in_=hx[b], func=AF.Square,
               accum_out=st[b][:, 0:1])
            OP("act", "activation", [("st", b, 0), "eps_col"], [("st", b, 1)], out=st[b][:, 1:2], in_=st[b][:, 0:1],
               func=AF.Sqrt, scale=1.0 / D, bias=eps_col)
            OP("dve", "reciprocal", [("st", b, 1)], [("st", b, 2)], out=st[b][:, 2:3], in_=st[b][:, 1:2])
            OP("dve", "scalar_tensor_tensor", [("hx", b), ("st", b, 2), "gbc"], [("un", b)], out=un[b], in0=hx[b],
               scalar=st[b][:, 2:3], in1=gbc, op0=ALU.mult, op1=ALU.mult)
            for half in range(2):
                bk = next_bank()
                pvw = ps[bk][:, :].bitcast(BF16)

                def fn(e, b=b, half=half, pvw=pvw):
                    for j in range(8):
                        k = half * 8 + j
                        ins = e.transpose(out=pvw[:, 128 * j:128 * j + 128], in_=un[b][:, 128 * k:128 * k + 128],
                                          identity=ident)
                    return ins
                P.add("pe", fn, reads=[("un", b), "ident"], writes=[("ps", bk)])
                evac_copy(uT[:, half * 8:half * 8 + 8, i * 128:i * 128 + 128],
                          pvw.rearrange("p (k t) -> p k t", t=128),
                          [("ps", bk)], [(uT_res, i, half)])

    for l in range(depth):
        src = x_in if l == 0 else hres
        lam_init = 0.8 - 0.6 * math.exp(-0.3 * l)
        AR.reset()
        pv = AR.alloc([NPV], F32)
        rv = AR.alloc([NRV], F32)
        lw = AR.alloc([16], F32)
        wab = AR.alloc([4, 128], BF16)
        wxb = AR.alloc([4, 128], BF16)
        prod = AR.alloc([128], F32)
        nsp = AR.alloc([12], F32)
        bfc = AR.alloc([1], F32)
        layer_base = AR.off
        dma("sp", pv, pvec[l], [], ["pv"])
        dma("sp", rv, rvec[l:l + 1, :].partition_broadcast(128), [], ["rv"])
        dma("sp", bfc[0:4, :], rvec[l, RV_BF:RV_BF + 4].rearrange("(a b) -> a b", b=1), [], ["bfc"])
        OP("pool", "memset", [], ["wab"], ap=wab, constant=0.0)
        OP("pool", "memset", [], ["wxb"], ap=wxb, constant=0.0)
        for bi in range(8):
            cc, hb = bi // 2, bi % 2
            dma("pool", wab[64 * hb:64 * hb + 64, cc, 64 * hb:64 * hb + 64], lru_wa[l, bi], ["wab"], [("wab", bi)])
            dma("pool", wxb[64 * hb:64 * hb + 64, cc, 64 * hb:64 * hb + 64], lru_wx[l, bi], ["wxb"], [("wxb", bi)])
        OP("dve", "tensor_tensor", ["rv"], ["prod0"], out=prod[:, 0:64], in0=rv[:, RV_LQ1:RV_LQ1 + 64],
           in1=rv[:, RV_LK1:RV_LK1 + 64], op=ALU.mult)
        OP("dve", "tensor_tensor", ["rv"], ["prod1"], out=prod[:, 64:128], in0=rv[:, RV_LQ2:RV_LQ2 + 64],
           in1=rv[:, RV_LK2:RV_LK2 + 64], op=ALU.mult)
        OP("dve", "reduce_sum", ["prod0"], [("lw", 2)], out=lw[:, 2:3], in_=prod[:, 0:64], axis=AX.X)
        OP("dve", "reduce_sum", ["prod1"], [("lw", 3)], out=lw[:, 3:4], in_=prod[:, 64:128], axis=AX.X)
        OP("act", "activation", [("lw", 2), ("lw", 3)], [("lw", 4)], out=lw[:, 4:6], in_=lw[:, 2:4], func=AF.Exp)
        OP("dve", "scalar_tensor_tensor", [("lw", 4)], [("lw", 0)], out=lw[:, 0:1], in0=lw[:, 5:6], scalar=-lam_init,
           in1=lw[:, 4:5], op0=ALU.add, op1=ALU.subtract)
        OP("dve", "tensor_scalar", ["pv"], [("lw", 1)], out=lw[:, 1:2], in0=pv[:, PV_SUB:PV_SUB + 1],
           scalar1=1.0 - lam_init, scalar2=None, op0=ALU.mult)
        OP("act", "activation", ["rv"], [("lw", 8)], out=lw[:, 8:16], in_=rv[:, RV_SINK:RV_SINK + 8], func=AF.Exp)
        OP("act", "activation", ["pv"], ["nsp0"], out=nsp[:, 0:4], in_=pv[:, PV_LAM:PV_LAM + 4], func=AF.Exp, scale=-1.0)
        OP("dve", "tensor_scalar", ["nsp0"], ["nsp0"], out=nsp[:, 0:4], in0=nsp[:, 0:4], scalar1=1.0, scalar2=None,
           op0=ALU.add)
        OP("act", "activation", ["nsp0"], ["nsp0"], out=nsp[:, 0:4], in_=nsp[:, 0:4], func=AF.Ln)
        OP("dve", "tensor_scalar", ["nsp0"], ["nsp8"], out=nsp[:, 4:8], in0=nsp[:, 0:4], scalar1=-8.0, scalar2=None,
           op0=ALU.mult)
        OP("dve", "tensor_scalar", ["nsp0"], ["nsp16"], out=nsp[:, 8:12], in0=nsp[:, 0:4], scalar1=-16.0, scalar2=None,
           op0=ALU.mult)
        P.barrier()

        blocks = [("aq", C_AQ, 512), ("ak", C_AK, 512), ("av", C_AV, 512), ("bx", C_BX, 512), ("bg", C_BG, 512),
                  ("cq", C_CQ, 512), ("ck", C_CK, 512), ("cvf", C_CV, 516), ("dq", C_DQ, 512), ("dkv", C_DK, 256)]
        blocks += [("gz", C_GZ + 512 * i, 512) for i in range(16)]
        wv_in = w_in[l].rearrange("(k p) n -> p k n", p=128)
        for g in range(S // TGI):
            AR.off = layer_base
            uT = AR.alloc([KC, TGI], BF16)
            mark = AR.off
            norm_transpose(src, g * TGI, TGI, 2 * l, uT, "uT")
            P.barrier()
            AR.off = mark
            wblk = [AR.alloc([KC, 516], BF16) for _ in range(2)]
            stg = [AR.alloc([TGI], F32) for _ in range(3)]
            stg_i = [0]
            tok0 = g * TGI
            uT_reads = [("uT", i, hf) for i in range(TGI // 128) for hf in range(2)]

            def fm_chunk(wb, wres, c0, m, dst, dst_res, dt, func=None, bias=None, bias_res=None):
                si = stg_i[0] % 3
                stg_i[0] += 1
                sview = stg[si] if dt == F32 else stg[si].bitcast(BF16)[:, 0:TGI]
                for r in range(TGI // 512):
                    bk = next_bank()
                    pairs = [(wb[:, k, c0:c0 + m], uT[:, k, r * 512:(r + 1) * 512]) for k in range(KC)]
                    mm_group(ps[bk][0:m, :], pairs, ("ps", bk), [wres] + uT_reads)
                    o = sview[0:m, r * 512:(r + 1) * 512]
                    if func is not None:
                        OP("act", "activation", [("ps", bk)] + ([bias_res] if bias_res else []), [("stg", si, r)],
                           out=o, in_=ps[bk][0:m, :], func=func, bias=bias)
                    else:
                        evac_copy(o, ps[bk][0:m, :], [("ps", bk)], [("stg", si, r)])
                dma("sp", dst[:, tok0:tok0 + TGI], sview[0:m, :], [("stg", si, r) for r in range(TGI // 512)],
                    [dst_res])

            def tm_block(wb, wres, c0, n, dst, dst_res):
                for i in range(TGI // 128):
                    si = stg_i[0] % 3
                    stg_i[0] += 1
                    sview = stg[si].bitcast(BF16)[:, 0:n]
                    bk = next_bank()
                    pairs = [(uT[:, k, i * 128:(i + 1) * 128], wb[:, k, c0:c0 + n]) for k in range(KC)]
                    mm_group(ps[bk][:, 0:n], pairs, ("ps", bk), [wres] + uT_reads)
                    evac_copy(sview, ps[bk][:, 0:n], [("ps", bk)], [("stg", si, 0)])
                    dma("sp", dst[tok0 + i * 128:tok0 + (i + 1) * 128, :], sview, [("stg", si, 0)], [(dst_res, i)])

            for bi, (name, c0, wcols) in enumerate(blocks):
                wb = wblk[bi % 2]
                wres = ("wblk", bi % 2)
                dma("pool", wb[:, :, 0:wcols], wv_in[:, :, c0:c0 + wcols], [], [wres])
                if name in ("aq", "ak"):
                    base = 0 if name == "aq" else 4
                    for c in range(4):
                        fm_chunk(wb, wres, 128 * c, 128, qkA[base + c], ("qkA", base + c, g), BF16)
                elif name in ("cq", "ck"):
                    base = 0 if name == "cq" else 4
                    for c in range(4):
                        fm_chunk(wb, wres, 128 * c, 128, qkC[base + c], ("qkC", base + c, g), BF16)
                elif name == "av":
                    tm_block(wb, wres, 0, 512, vA, ("vA", g))
                elif name == "cvf":
                    tm_block(wb, wres, 0, 512, vC, ("vC", g))
                    fm_chunk(wb, wres, 512, 4, cfT, ("cfT", g), F32)
                elif name == "bx":
                    for c in range(4):
                        fm_chunk(wb, wres, 128 * c, 128, xbT[c], ("xbT", c, g), F32)
                elif name == "bg":
                    for c in range(4):
                        fm_chunk(wb, wres, 128 * c, 128, gbT[c], ("gbT", c, g), F32)
                elif name == "dq":
                    for h in range(8):
                        fm_chunk(wb, wres, 64 * h, 64, qDd[h], ("qD", h, g), BF16)
                elif name == "dkv":
                    for kv in range(2):
                        fm_chunk(wb, wres, 64 * kv, 64, kDd[kv], ("kD", kv, g), BF16)
                    tm_block(wb, wres, 128, 128, vDd, ("vD", g))
                else:
                    gi = (c0 - C_GZ) // 128
                    for c in range(4):
                        fm_chunk(wb, wres, 128 * c, 128, gatesD[gi + c], ("gates", gi + c, g), BF16,
                                 func=AF.Sigmoid, bias=pv[:, PV_BG + gi + c:PV_BG + gi + c + 1], bias_res="pv")
            P.barrier()

        AR.off = layer_base
        cf_sb = AR.alloc([S], F32)
        cum_sb = AR.alloc([S], F32)
        ones_row = AR.alloc([S], F32)
        dma("sp", cf_sb[0:4, :], cfT[:, :], [], ["cf_sb"])
        OP("act", "activation", ["cf_sb", "bfc"], ["cf_sb"], out=cf_sb[0:4, :], in_=cf_sb[0:4, :], func=AF.Sigmoid,
           bias=bfc[0:4, :])
        OP("act", "activation", ["cf_sb"], ["cf_sb"], out=cf_sb[0:4, :], in_=cf_sb[0:4, :], func=AF.Ln)
        OP("pool", "memset", [], ["ones_row"], ap=ones_row[0:4, :], constant=1.0)
        OP("dve", "tensor_tensor_scan", ["cf_sb", "ones_row"], ["cum_sb"], out=cum_sb[0:4, :], data0=ones_row[0:4, :],
           data1=cf_sb[0:4, :], initial=0.0, op0=ALU.mult, op1=ALU.add)
        dma("sp", cumD[:, :], cum_sb[0:4, :], ["cum_sb"], ["cumD"])
        P.barrier()

        def attention(kind):
            AR.off = layer_base
            qk = qkA if kind == "A" else qkC
            vsrc = vA if kind == "A" else vC
            V = AR.alloc([NT, 512], BF16)
            dma("sp", V, vsrc.rearrange("(c p) n -> p c n", p=128), [], ["V"])
            qT = [AR.alloc([S], BF16) for _ in range(2)]
            kT = [AR.alloc([S], BF16) for _ in range(2)]
            tmp = [AR.alloc([512], F32) for _ in range(3)]
            eT = [AR.alloc([512], BF16) for _ in range(3)]
            fin = [AR.alloc([512], F32) for _ in range(6)]
            sqb = AR.alloc([512], BF16)
            yst = [AR.alloc([512], BF16) for _ in range(2)]
            if kind == "C":
                cumT = AR.alloc([NT, 4], F32)
                Rb = [AR.alloc([512], F32) for _ in range(2)]
                rowbC = [AR.alloc([512], F32) for _ in range(2)]
                colbC = [AR.alloc([NT], F32) for _ in range(2)]
                cum4 = AR.alloc([S], F32)
                dma("sp", cum4[0:4, :], cumD[:, :], [], ["cum4"])
                bkc = 4

                def fnc(e):
                    for c in range(NT):
                        ins = e.transpose(out=ps[bkc][:, 4 * c:4 * c + 4], in_=cum4[0:4, 128 * c:128 * c + 128],
                                          identity=ident_f[0:4, 0:4])
                    return ins
                P.add("pe", fnc, reads=["cum4", "ident_f"], writes=[("ps", bkc)])
                OP("dve", "tensor_copy", [("ps", bkc)], ["cumT"], out=cumT,
                   in_=ps[bkc][:, 0:4 * NT].rearrange("p (c h) -> p c h", h=4))
            it = 0
            yi = 0
            ri = 0
            for h in range(4):
                hb = h % 2
                dma("sp", qT[hb], qk[h], [], [("qT", hb)])
                dma("sp", kT[hb], qk[4 + h], [], [("kT", hb)])
                for tr in range(NR):
                    t0 = tr * 512
                    nsc = 4 * tr + 4
                    if kind == "C":
                        rb = ri % 2
                        ri += 1
                        dma("sp", Rb[rb], cumD[h:h + 1, t0:t0 + 512].partition_broadcast(128), [], [("Rb", rb)])
                        OP("dve", "tensor_scalar", [("Rb", rb)], [("rowbC", rb)], out=rowbC[rb], in0=Rb[rb],
                           scalar1=Rb[rb][:, 0:1], scalar2=None, op0=ALU.subtract)
                        OP("dve", "tensor_scalar", [("Rb", rb), "cumT"], [("colbC", rb)], out=colbC[rb][:, 0:nsc],
                           in0=cumT[:, 0:nsc, h], scalar1=-1.0, scalar2=Rb[rb][:, 0:1], op0=ALU.mult, op1=ALU.add)
                    maps = 2 if kind == "A" else 1
                    banks = [(0, 1), (2, 3)]
                    for m in range(maps):
                        bo, bn = banks[m]
                        for sc in range(nsc):
                            di = sc - 4 * tr
                            ta = t0 + 128 * di if di > 0 else t0
                            n = t0 + 512 - ta
                            off = ta - t0
                            bs = next_bank(5, 8)
                            ti = it % 3
                            it += 1
                            if kind == "A":
                                lhsT = kT[hb][64 * m:64 * m + 64, sc * 128:(sc + 1) * 128]
                                rhs = qT[hb][64 * m:64 * m + 64, ta:t0 + 512]
                                scale = 0.125
                                rowb = rowbA[:, h, off:512]
                                rowres = ("rowbA", h)
                                ci = NT + sc - 4 * tr
                                colb = colbA[:, h, ci:ci + 1]
                                colres = ("colbA", h)
                            else:
                                lhsT = kT[hb][:, sc * 128:(sc + 1) * 128]
                                rhs = qT[hb][:, ta:t0 + 512]
                                scale = 128.0 ** -0.5
                                rowb = rowbC[rb][:, off:512]
                                rowres = ("rowbC", rb)
                                colb = colbC[rb][:, sc:sc + 1]
                                colres = ("colbC", rb)
                            mm_group(ps[bs][:, 0:n], [(lhsT, rhs)], ("ps", bs), [("qT", hb), ("kT", hb)])
                            OP("dve", "scalar_tensor_tensor", [("ps", bs), rowres], [("tmp", ti)], out=tmp[ti][:, 0:n],
                               in0=ps[bs][:, 0:n], scalar=scale, in1=rowb, op0=ALU.mult, op1=ALU.add)
                            if di >= 0:
                                OP("pool", "tensor_tensor", [("tmp", ti), "maskneg"], [("tmp", ti)],
                                   out=tmp[ti][:, 0:128], in0=tmp[ti][:, 0:128], in1=maskneg, op=ALU.add)
                            OP("act", "activation", [("tmp", ti), colres], [("eT", ti)], out=eT[ti][:, 0:n],
                               in_=tmp[ti][:, 0:n], func=AF.Exp, bias=colb)
                            first, last = sc == 0, sc == nsc - 1
                            mm_group(ps[bo][:, off:512], [(V[:, sc, 128 * h:128 * h + 128], eT[ti][:, 0:n])],
                                     ("ps", bo), [("eT", ti), "V"], first=first, last=last)
                            mm_group(ps[bn][:, off:512], [(ones_b, eT[ti][:, 0:n])],
                                     ("ps", bn), [("eT", ti), "ones_b"], first=first, last=last)
                    ysi = yi % 2
                    yi += 1
                    bo, bn = banks[0]
                    OP("dve", "reciprocal", [("ps", bn)], [("fin", 0)], out=fin[0], in_=ps[bn][:, :])
                    if kind == "C":
                        OP("dve", "tensor_tensor", [("ps", bo), ("fin", 0)], [("yst", ysi)], out=yst[ysi],
                           in0=ps[bo][:, :], in1=fin[0], op=ALU.mult)
                        dma("sp", yall[8 + h][:, t0:t0 + 512], yst[ysi], [("yst", ysi)], [("yall", 8 + h, tr)])
                        continue
                    OP("dve", "tensor_tensor", [("ps", bo), ("fin", 0)], [("fin", 1)], out=fin[1], in0=ps[bo][:, :],
                       in1=fin[0], op=ALU.mult)
                    bo1, bn1 = banks[1]
                    OP("dve", "reciprocal", [("ps", bn1)], [("fin", 2)], out=fin[2], in_=ps[bn1][:, :])
                    OP("dve", "tensor_tensor", [("ps", bo1), ("fin", 2)], [("fin", 3)], out=fin[3], in0=ps[bo1][:, :],
                       in1=fin[2], op=ALU.mult)
                    OP("dve", "scalar_tensor_tensor", [("fin", 3), ("fin", 1), ("lw", 0)], [("fin", 4)], out=fin[4],
                       in0=fin[3], scalar=lw[:, 0:1], in1=fin[1], op0=ALU.mult, op1=ALU.add)
                    OP("act", "activation", [("fin", 4)], ["sqb"], out=sqb, in_=fin[4], func=AF.Square)
                    bq = 4
                    mm_group(ps[bq][:, :], [(ones_b, sqb)], ("ps", bq), ["sqb", "ones_b"])
                    OP("act", "activation", [("ps", bq), "eps_col"], [("fin", 5)], out=fin[5], in_=ps[bq][:, :],
                       func=AF.Sqrt, scale=1.0 / 128, bias=eps_col)
                    OP("dve", "reciprocal", [("fin", 5)], [("fin", 5)], out=fin[5], in_=fin[5])
                    OP("dve", "scalar_tensor_tensor", [("fin", 4), ("fin", 5), ("lw", 1)], [("yst", ysi)], out=yst[ysi],
                       in0=fin[4], scalar=lw[:, 1:2], in1=fin[5], op0=ALU.mult, op1=ALU.mult)
                    dma("sp", yall[h][:, t0:t0 + 512], yst[ysi], [("yst", ysi)], [("yall", h, tr)])
            P.barrier()

        attention("A")
        attention("C")

        AR.off = layer_base
        Vd = AR.alloc([NT, 128], BF16)
        dma("sp", Vd, vDd.rearrange("(c p) n -> p c n", p=128), [], ["Vd"])
        kTd = AR.alloc([2, S], BF16)
        for kv in range(2):
            dma("sp", kTd[0:64, kv, :], kDd[kv], [], [("kTd", kv)])
        qTd = [AR.alloc([S], BF16) for _ in range(2)]
        tmpd = [AR.alloc([256], F32) for _ in range(5)]
        eTd = [AR.alloc([256], BF16) for _ in range(5)]
        find = [AR.alloc([512], F32) for _ in range(2)]
        ystd = [AR.alloc([512], BF16) for _ in range(2)]
        it = 0
        for h in range(8):
            hb = h % 2
            kv = h // 4
            dma("sp", qTd[hb][0:64, :], qDd[h], [], [("qTd", hb)])
            for tr in range(NR):
                t0 = tr * 512
                bo, bn = 0, 1
                etiles = {}
                for c in range(4 * tr - 1, 4 * tr + 4):
                    if c < 0:
                        continue
                    tca = max(c, 4 * tr)
                    tcb = min(c + 1, 4 * tr + 3)
                    ta, n = tca * 128, (tcb - tca + 1) * 128
                    moff = 0 if tca == c else 128
                    bs = next_bank(5, 8)
                    ti = it % 5
                    it += 1
                    mm_group(ps[bs][:, 0:n], [(kTd[0:64, kv, c * 128:(c + 1) * 128], qTd[hb][0:64, ta:ta + n])],
                             ("ps", bs), [("qTd", hb), ("kTd", kv)])
                    OP("dve", "scalar_tensor_tensor", [("ps", bs), ("mbD", h)], [("tmpd", ti)], out=tmpd[ti][:, 0:n],
                       in0=ps[bs][:, 0:n], scalar=0.125, in1=mbD[:, h, moff:moff + n], op0=ALU.mult, op1=ALU.add)
                    OP("act", "activation", [("tmpd", ti)], [("eTd", ti)], out=eTd[ti][:, 0:n], in_=tmpd[ti][:, 0:n],
                       func=AF.Exp)
                    etiles[c] = (ti, tca)
                for bank, is_o in ((bo, True), (bn, False)):
                    for tc in range(4 * tr, 4 * tr + 4):
                        srcs = [c for c in (tc - 1, tc) if c >= 0]
                        for j, c in enumerate(srcs):
                            ti, tca = etiles[c]
                            eo = (tc - tca) * 128
                            lhs = Vd[:, c, 64 * kv:64 * kv + 64] if is_o else ones_b[:, 0:64]
                            col = (tc - 4 * tr) * 128
                            mm_group(ps[bank][0:64, col:col + 128], [(lhs, eTd[ti][:, eo:eo + 128])], ("ps", bank),
                                     [("eTd", ti), "Vd", "ones_b"], first=(j == 0), last=(j == len(srcs) - 1))
                fi = (h * NR + tr) % 2
                OP("dve", "tensor_scalar", [("ps", bn), ("lw", 8)], [("find", fi)], out=find[fi][0:64, :],
                   in0=ps[bn][0:64, :], scalar1=lw[0:64, 8 + h:9 + h], scalar2=None, op0=ALU.add)
                OP("dve", "reciprocal", [("find", fi)], [("find", fi)], out=find[fi][0:64, :], in_=find[fi][0:64, :])
                OP("dve", "tensor_tensor", [("ps", bo), ("find", fi)], [("ystd", fi)], out=ystd[fi][0:64, :],
                   in0=ps[bo][0:64, :], in1=find[fi][0:64, :], op=ALU.mult)
                dma("sp", yall[12 + h // 2][64 * (h % 2):64 * (h % 2) + 64, t0:t0 + 512], ystd[fi][0:64, :],
                    [("ystd", fi)], [("yall", 12 + h // 2, tr, h % 2)])
        P.barrier()

        AR.off = layer_base
        NLR = S // TL
        NSB = TL // 512
        xb = [AR.alloc([TL + 3], F32) for _ in range(2)]
        gb = [AR.alloc([TL], F32) for _ in range(2)]
        xc = AR.alloc([TL], F32)
        xcb = AR.alloc([TL], BF16)
        rr = AR.alloc([TL], F32)
        ii = AR.alloc([TL], F32)
        aa = AR.alloc([TL], F32)
        a2 = AR.alloc([TL], F32)
        hh = [AR.alloc([TL], F32) for _ in range(2)]
        gq = AR.alloc([TL], F32)
        ybs = [AR.alloc([TL], BF16) for _ in range(2)]
        li = 0
        for cc in range(4):
            for rg in range(NLR):
                t0 = rg * TL
                b = li % 2
                hp, hc = hh[li % 2], hh[(li + 1) % 2]
                hres_p, hres_c = ("hh", li % 2), ("hh", (li + 1) % 2)
                li += 1
                if rg == 0:
                    OP("pool", "memset", [], [("xb", b, "h")], ap=xb[b][:, 0:3], constant=0.0)
                    dma("sp", xb[b][:, 3:3 + TL], xbT[cc][:, 0:TL], [], [("xb", b)])
                else:
                    dma("sp", xb[b][:, 0:3 + TL], xbT[cc][:, t0 - 3:t0 + TL], [], [("xb", b), ("xb", b, "h")])
                dma("sp", gb[b], gbT[cc][:, t0:t0 + TL], [], [("gb", b)])
                xr = [("xb", b), ("xb", b, "h"), "pv"]
                cwc = PV_CW + 4 * cc
                OP("dve", "tensor_scalar", xr, ["xc"], out=xc, in0=xb[b][:, 3:3 + TL], scalar1=pv[:, cwc + 3:cwc + 4],
                   scalar2=pv[:, PV_CB + cc:PV_CB + cc + 1], op0=ALU.mult, op1=ALU.add)
                for j in range(3):
                    OP("dve", "scalar_tensor_tensor", xr + ["xc"], ["xc"], out=xc, in0=xb[b][:, j:j + TL],
                       scalar=pv[:, cwc + j:cwc + j + 1], in1=xc, op0=ALU.mult, op1=ALU.add)
                OP("act", "copy", ["xc"], ["xcb"], out=xcb, in_=xc)
                for sb in range(NSB):
                    sl = slice(sb * 512, (sb + 1) * 512)
                    b1, b2 = next_bank(), next_bank()
                    mm_group(ps[b1][:, :], [(wab[:, cc, :], xcb[:, sl])], ("ps", b1),
                             ["xcb", ("wab", 2 * cc), ("wab", 2 * cc + 1), "wab"])
                    mm_group(ps[b2][:, :], [(wxb[:, cc, :], xcb[:, sl])], ("ps", b2),
                             ["xcb", ("wxb", 2 * cc), ("wxb", 2 * cc + 1), "wxb"])
                    OP("act", "activation", [("ps", b1), "pv"], [("rr", sb)], out=rr[:, sl], in_=ps[b1][:, :],
                       func=AF.Sigmoid, bias=pv[:, PV_BA + cc:PV_BA + cc + 1])
                    OP("act", "activation", [("ps", b2), "pv"], [("ii", sb)], out=ii[:, sl], in_=ps[b2][:, :],
                       func=AF.Sigmoid, bias=pv[:, PV_BX + cc:PV_BX + cc + 1])
                rrr = [("rr", sb) for sb in range(NSB)]
                iir = [("ii", sb) for sb in range(NSB)]
                OP("act", "activation", rrr + ["nsp8"], ["aa"], out=aa, in_=rr, func=AF.Exp, scale=nsp[:, 4 + cc:5 + cc])
                OP("act", "activation", rrr + ["nsp16"], ["a2"], out=a2, in_=rr, func=AF.Exp, scale=nsp[:, 8 + cc:9 + cc])
                OP("pool", "tensor_scalar", ["a2"], ["a2"], out=a2, in0=a2, scalar1=-1.0, scalar2=1.0, op0=ALU.mult,
                   op1=ALU.add)
                OP("act", "activation", ["a2"], ["a2"], out=a2, in_=a2, func=AF.Sqrt)
                OP("pool", "tensor_tensor", iir + ["xc"], iir, out=ii, in0=ii, in1=xc, op=ALU.mult)
                OP("pool", "tensor_tensor", iir + ["a2"], iir, out=ii, in0=ii, in1=a2, op=ALU.mult)
                if rg == 0:
                    OP("dve", "tensor_tensor_scan", iir + ["aa"], [hres_c], out=hc, data0=aa, data1=ii, initial=0.0,
                       op0=ALU.mult, op1=ALU.add)
                else:
                    OP("dve", "tensor_tensor_scan", iir + ["aa", hres_p], [hres_c], out=hc, data0=aa, data1=ii,
                       initial=hp[:, TL - 1:TL], op0=ALU.mult, op1=ALU.add)
                OP("act", "activation", [("gb", b)], ["gq"], out=gq, in_=gb[b], func=AF.Square)
                OP("pool", "tensor_scalar", ["gq"], ["gq"], out=gq, in0=gq, scalar1=0.044715, scalar2=1.0,
                   op0=ALU.mult, op1=ALU.add)
                OP("pool", "tensor_tensor", ["gq", ("gb", b)], ["gq"], out=gq, in0=gq, in1=gb[b], op=ALU.mult)
                OP("act", "activation", ["gq"], ["gq"], out=gq, in_=gq, func=AF.Sigmoid, scale=1.5957691216057308)
                OP("pool", "tensor_tensor", ["gq", ("gb", b)], ["gq"], out=gq, in0=gq, in1=gb[b], op=ALU.mult)
                OP("dve", "tensor_tensor", ["gq", hres_c], [("ybs", b)], out=ybs[b], in0=gq, in1=hc, op=ALU.mult)
                dma("sp", yall[4 + cc][:, t0:t0 + TL], ybs[b], [("ybs", b)], [("yall", 4 + cc, rg)])
        P.barrier()

        AR.off = layer_base
        Wb = AR.alloc([16, D], BF16)
        wbv = w_branch[l].rearrange("(k p) n -> p k n", p=128)
        for q in range(4):
            dma("pool", Wb[:, 4 * q:4 * q + 4, :], wbv[:, 4 * q:4 * q + 4, :], [], [("Wb", q)])
        Wb_r = [("Wb", q) for q in range(4)]
        ysb = [AR.alloc([16, 512], BF16) for _ in range(2)]
        gsb = [AR.alloc([4, 512], BF16) for _ in range(2)]
        mixT = [AR.alloc([KC, 512], BF16) for _ in range(2)]
        mtmp = [AR.alloc([512], F32) for _ in range(3)]
        macc = AR.alloc([512], F32)
        gview = gatesD.rearrange("(n c) p s -> c p n s", n=4)
        gi = 0
        for tr in range(NR):
            t0 = tr * 512
            yb_ = tr % 2
            dma("sp", ysb[yb_], yall[:, :, t0:t0 + 512].rearrange("k p t -> p k t"), [], [("ysb", yb_)])
            for cc in range(16):
                gbf = gi % 2
                gi += 1
                dma("sp", gsb[gbf], gview[cc][:, :, t0:t0 + 512], [], [("gsb", gbf)])
                for n in range(4):
                    bk = next_bank()
                    pairs = [(Wb[:, 4 * n + k, cc * 128:(cc + 1) * 128], ysb[yb_][:, 4 * n + k, :]) for k in range(4)]
                    mm_group(ps[bk][:, :], pairs, ("ps", bk), [("ysb", yb_)] + Wb_r)
                    if n == 0:
                        OP("dve", "tensor_tensor", [("ps", bk), ("gsb", gbf)], ["macc"], out=macc, in0=ps[bk][:, :],
                           in1=gsb[gbf][:, 0, :], op=ALU.mult)
                    else:
                        mi = n - 1
                        OP("dve", "tensor_tensor", [("ps", bk), ("gsb", gbf)], [("mtmp", mi)], out=mtmp[mi],
                           in0=ps[bk][:, :], in1=gsb[gbf][:, n, :], op=ALU.mult)
                        if n < 3:
                            OP("pool", "tensor_tensor", ["macc", ("mtmp", mi)], ["macc"], out=macc, in0=macc,
                               in1=mtmp[mi], op=ALU.add)
                        else:
                            OP("pool", "tensor_tensor", ["macc", ("mtmp", mi)], [("mixT", yb_, cc)],
                               out=mixT[yb_][:, cc, :], in0=macc, in1=mtmp[mi], op=ALU.add)
            dma("sp", mixD[:, :, t0:t0 + 512].rearrange("k p t -> p k t"), mixT[yb_],
                [("mixT", yb_, cc) for cc in range(16)], [("mixD", tr)])
        P.barrier()

        AR.off = layer_base
        Wo = AR.alloc([KC, D], BF16)
        wov = w_out[l].rearrange("(k p) n -> p k n", p=128)
        for q in range(4):
            dma("pool", Wo[:, 4 * q:4 * q + 4, :], wov[:, 4 * q:4 * q + 4, :], [], [("Wo", q)])
        Wo_r = [("Wo", q) for q in range(4)]
        mx = [AR.alloc([KC, 512], BF16) for _ in range(2)]
        hxm = [AR.alloc([D], F32) for _ in range(2)]
        hi = 0
        for tr in range(NR):
            t0 = tr * 512
            mb_ = tr % 2
            dma("sp", mx[mb_], mixD[:, :, t0:t0 + 512].rearrange("k p t -> p k t"), [], [("mx", mb_)])
            for ts in range(4):
                r0 = t0 + ts * 128
                hb = hi % 2
                hi += 1
                dma("sp", hxm[hb], src[r0:r0 + 128, :], [], [("hxm", hb)])
                for cb in range(4):
                    bk = next_bank()
                    pairs = [(mx[mb_][:, k, ts * 128:(ts + 1) * 128], Wo[:, k, cb * 512:(cb + 1) * 512]) for k in range(KC)]
                    mm_group(ps[bk][:, :], pairs, ("ps", bk), [("mx", mb_)] + Wo_r)
                    OP("dve", "tensor_tensor", [("ps", bk), ("hxm", hb)], [("hxm", hb)],
                       out=hxm[hb][:, cb * 512:(cb + 1) * 512], in0=hxm[hb][:, cb * 512:(cb + 1) * 512],
                       in1=ps[bk][:, :], op=ALU.add)
                dma("sp", hres[r0:r0 + 128, :], hxm[hb], [("hxm", hb)], [("hres", r0)])
        P.barrier()

        wgv = w_fg[l].rearrange("(k p) n -> p k n", p=128)
        wuv = w_fu[l].rearrange("(k p) n -> p k n", p=128)
        wdv = w_fd[l].rearrange("(f p) n -> p f n", p=128)
        for g in range(S // TGF):
            tok0 = g * TGF
            AR.off = layer_base
            aT = AR.alloc([FC, TGF], BF16)
            mark = AR.off
            uT2 = AR.alloc([KC, TGF], BF16)
            mark2 = AR.off
            norm_transpose(hres, tok0, TGF, 2 * l + 1, uT2, "uT2")
            P.barrier()
            AR.off = mark2
            wg = [AR.alloc([KC, 256], BF16) for _ in range(2)]
            wu = [AR.alloc([KC, 256], BF16) for _ in range(2)]
            sg = [AR.alloc([512], F32) for _ in range(2)]
            u_reads = [("uT2", i, hf) for i in range(TGF // 128) for hf in range(2)]
            si = 0
            for fb in range(DFF // 256):
                wbuf = fb % 2
                dma("pool", wg[wbuf], wgv[:, :, fb * 256:(fb + 1) * 256], [], [("wg", wbuf)])
                dma("pool", wu[wbuf], wuv[:, :, fb * 256:(fb + 1) * 256], [], [("wu", wbuf)])
                for sub in range(2):
                    f = fb * 2 + sub
                    for r in range(TGF // 512):
                        b1, b2 = next_bank(), next_bank()
                        rs = slice(r * 512, (r + 1) * 512)
                        mm_group(ps[b1][:, :], [(wg[wbuf][:, k, sub * 128:(sub + 1) * 128], uT2[:, k, rs]) for k in range(KC)],
                                 ("ps", b1), [("wg", wbuf)] + u_reads)
                        mm_group(ps[b2][:, :], [(wu[wbuf][:, k, sub * 128:(sub + 1) * 128], uT2[:, k, rs]) for k in range(KC)],
                                 ("ps", b2), [("wu", wbuf)] + u_reads)
                        sj = si % 2
                        si += 1
                        OP("act", "activation", [("ps", b1)], [("sg", sj)], out=sg[sj], in_=ps[b1][:, :], func=AF.Silu)
                        OP("dve", "tensor_tensor", [("ps", b2), ("sg", sj)], [("aT", f, r)], out=aT[:, f, rs],
                           in0=sg[sj], in1=ps[b2][:, :], op=ALU.mult)
            P.barrier()
            AR.off = mark
            wd = [AR.alloc([FC, 256], BF16) for _ in range(2)]
            hxs = [AR.alloc([256], F32) for _ in range(3)]
            a_reads = [("aT", f, r) for f in range(FC) for r in range(TGF // 512)]
            hi = 0
            for cb in range(8):
                wbuf = cb % 2
                cs = slice(cb * 256, (cb + 1) * 256)
                for q in range(4):
                    dma("pool", wd[wbuf][:, 11 * q:11 * q + 11, :], wdv[:, 11 * q:11 * q + 11, cs],
                        [], [("wd", wbuf, q)])
                wd_r = [("wd", wbuf, q) for q in range(4)]
                for i in range(TGF // 128):
                    r0 = tok0 + i * 128
                    hb = hi % 3
                    hi += 1
                    dma("sp", hxs[hb], hres[r0:r0 + 128, cs], [("hres", r0)], [("hxs", hb)])
                    bk = next_bank()
                    pairs = [(aT[:, f, i * 128:(i + 1) * 128], wd[wbuf][:, f, :]) for f in range(FC)]
                    mm_group(ps[bk][:, 0:256], pairs, ("ps", bk), a_reads + wd_r)
                    OP("dve", "tensor_tensor", [("ps", bk), ("hxs", hb)], [("hxs", hb)], out=hxs[hb], in0=hxs[hb],
                       in1=ps[bk][:, 0:256], op=ALU.add)
                    dma("sp", hres[r0:r0 + 128, cs], hxs[hb], [("hxs", hb)], [("hres", r0)])
            P.barrier()

    AR.reset()
    gbcf = AR.alloc([D], F32)
    dma("sp", gbcf, norms[2 * depth:2 * depth + 1, :].partition_broadcast(128), [], ["gbcf"])
    hxf = [AR.alloc([D], F32) for _ in range(2)]
    junkf = AR.alloc([D], BF16)
    stf = [AR.alloc([4], F32) for _ in range(2)]
    final_toks = []
    for i in range(NT):
        b = i % 2
        r0 = i * 128
        dma("sp", hxf[b], hres[r0:r0 + 128, :], [("hres", r0)], [("hxf", b)])
        OP("act", "activation", [("hxf", b)], ["junkf", ("stf", b, 0)], out=junkf, in_=hxf[b], func=AF.Square,
           accum_out=stf[b][:, 0:1])
        OP("act", "activation", [("stf", b, 0), "eps_col"], [("stf", b, 1)], out=stf[b][:, 1:2], in_=stf[b][:, 0:1],
           func=AF.Sqrt, scale=1.0 / D, bias=eps_col)
        OP("dve", "reciprocal", [("stf", b, 1)], [("stf", b, 2)], out=stf[b][:, 2:3], in_=stf[b][:, 1:2])
        OP("dve", "scalar_tensor_tensor", [("hxf", b), ("stf", b, 2), "gbcf"], [("hxf", b)], out=hxf[b], in0=hxf[b],
           scalar=stf[b][:, 2:3], in1=gbcf, op0=ALU.mult, op1=ALU.mult)
        final_toks.append(dma("sp", out[r0:r0 + 128, :], hxf[b], [("hxf", b)], [("out", i)]))
    P.emit(final_toks)
    return nc


_NC_CACHE = {}


def _pack_small(inp, depth):
    f = np.float32
    pvec = np.zeros((depth, 128, NPV), f)
    rvec = np.zeros((depth, NRV), f)
    for l in range(depth):
        pvec[l, :, PV_BG:PV_BG + 64] = inp["b_gate"][l].reshape(64, 128).T
        pvec[l, :, PV_CW:PV_CW + 16] = inp["lru_conv_w"][l].reshape(4, 4, 128).transpose(2, 1, 0).reshape(128, 16)
        pvec[l, :, PV_CB:PV_CB + 4] = inp["lru_conv_b"][l].reshape(4, 128).T
        pvec[l, :, PV_BA:PV_BA + 4] = inp["lru_ba"][l].reshape(4, 128).T
        pvec[l, :, PV_BX:PV_BX + 4] = inp["lru_bx"][l].reshape(4, 128).T
        pvec[l, :, PV_LAM:PV_LAM + 4] = inp["lru_lambda"][l].reshape(4, 128).T
        pvec[l, :, PV_SUB] = inp["diff_subln"][l]
        rvec[l, RV_LQ1:RV_LQ1 + 64] = inp["diff_lq1"][l]
        rvec[l, RV_LK1:RV_LK1 + 64] = inp["diff_lk1"][l]
        rvec[l, RV_LQ2:RV_LQ2 + 64] = inp["diff_lq2"][l]
        rvec[l, RV_LK2:RV_LK2 + 64] = inp["diff_lk2"][l]
        rvec[l, RV_SINK:RV_SINK + 8] = inp["swa_sinks"][l]
        rvec[l, RV_BF:RV_BF + 4] = inp["fox_b_f"][l]
    norms = np.zeros((2 * depth + 1, D), f)
    for l in range(depth):
        norms[2 * l] = inp["norm_mix"][l]
        norms[2 * l + 1] = inp["norm_ffn"][l]
    norms[2 * depth] = inp["norm_final"]
    return pvec, rvec, norms


def run(inputs, S, depth, n_cores=8):
    inp = {k: np.asarray(v, dtype=np.float32) for k, v in inputs.items()}
    B = inp["x"].shape[0]
    key = (S, depth)
    if key not in _NC_CACHE:
        _NC_CACHE[key] = build_program(S, depth)
    nc = _NC_CACHE[key]
    pvec, rvec, norms = _pack_small(inp, depth)
    shared = {
        "w_in": np.ascontiguousarray(inp["w_in"]),
        "w_branch": np.ascontiguousarray(inp["w_branch"].reshape(depth, 4 * BW, D)),
        "w_out": np.ascontiguousarray(inp["w_out"]),
        "w_ffn_gate": np.ascontiguousarray(inp["w_ffn_gate"]),
        "w_ffn_up": np.ascontiguousarray(inp["w_ffn_up"]),
        "w_ffn_down": np.ascontiguousarray(inp["w_ffn_down"]),
        "lru_wa": np.ascontiguousarray(inp["lru_wa"]),
        "lru_wx": np.ascontiguousarray(inp["lru_wx"]),
        "norms": norms, "pvec": pvec, "rvec": rvec,
    }
    in_maps = []
    for c in range(n_cores):
        m = dict(shared)
        m["x"] = np.ascontiguousarray(inp["x"][c]) if c < B else np.zeros((S, D), np.float32)
        in_maps.append(m)
    res = run_bass_kernel_spmd(nc, in_maps, core_ids=list(range(n_cores)))
    return np.stack([res.results[c]["out"] for c in range(B)], axis=0)


def kernel(**inputs):
    return run(inputs, 4096, 4)
```

```python
import math
from contextlib import ExitStack

import numpy as np
import concourse.bass as bass
import concourse.mybir as mybir
from concourse.bass_utils import run_bass_kernel_spmd

F32 = mybir.dt.float32
BF16 = mybir.dt.bfloat16
U8 = mybir.dt.uint8
AF = mybir.ActivationFunctionType
ALU = mybir.AluOpType
AX = mybir.AxisListType

D = 2048
KC = D // 128
DFF = 5632
FC = DFF // 128
BW = 512
NCOLS_IN = 13060
RMS_EPS = 1e-6
NEG = -1.0e30
SLOPES_A = [2.0 ** (-8.0 * i / 4) for i in range(1, 5)]
SLOPES_D = [2.0 ** (-8.0 * i / 8) for i in range(1, 9)]
C_AQ, C_AK, C_AV, C_BX, C_BG, C_CQ, C_CK, C_CV, C_CF, C_DQ, C_DK, C_DV, C_GZ = (
    0, 512, 1024, 1536, 2048, 2560, 3072, 3584, 4096, 4100, 4612, 4740, 4868)
PV_BG, PV_CW, PV_CB, PV_BA, PV_BX, PV_LAM, PV_SUB = 0, 64, 80, 84, 88, 92, 96
NPV = 97
RV_LQ1, RV_LK1, RV_LQ2, RV_LK2, RV_SINK, RV_BF = 0, 64, 128, 192, 256, 264
NRV = 268

SAME_ENGINE_SYNC = True


class Prog:
    ENGS = ["pe", "act", "dve", "pool", "sp"]

    def __init__(self, nc, n_sp=40, n_pool=16):
        self.nc = nc
        self.ops = {e: [] for e in self.ENGS}
        self.sem_names = []
        self.eng_sem = {}
        for e in self.ENGS:
            self.eng_sem[e] = self._new_sem("c_" + e)
        self.count = {e: 0 for e in self.ENGS}
        self.dma_slots = {"sp": [self._new_sem(f"d_sp{i}") for i in range(n_sp)],
                          "pool": [self._new_sem(f"d_pl{i}") for i in range(n_pool)]}
        self.dma_next = {"sp": 0, "pool": 0}
        self.sem_val = {}
        self.last_write = {}
        self.readers = {}

    def _new_sem(self, name):
        self.sem_names.append(name)
        return len(self.sem_names) - 1

    def add(self, eng, fn, reads=(), writes=(), dma=False):
        deps = {}

        def dep(tok):
            s, v = tok
            if deps.get(s, 0) < v:
                deps[s] = v
        for r in reads:
            t = self.last_write.get(r)
            if t is not None:
                dep(t)
        for w in writes:
            t = self.last_write.get(w)
            if t is not None:
                dep(t)
            for s, v in self.readers.get(w, {}).items():
                dep((s, v))
        if dma:
            slots = self.dma_slots[eng]
            j = self.dma_next[eng]
            self.dma_next[eng] = (j + 1) % len(slots)
            sem = slots[j]
            prev = self.sem_val.get(sem, 0)
            if prev:
                dep((sem, prev))
            tok = (sem, prev + 16)
            inc = 16
        else:
            sem = self.eng_sem[eng]
            self.count[eng] += 1
            tok = (sem, self.count[eng])
            inc = 1
        self.sem_val[sem] = tok[1]
        for r in reads:
            d = self.readers.setdefault(r, {})
            if d.get(tok[0], 0) < tok[1]:
                d[tok[0]] = tok[1]
        for w in writes:
            self.last_write[w] = tok
            self.readers[w] = {}
        self.ops[eng].append((fn, deps, tok, inc))
        return tok

    def wait_all(self, eng, toks):
        deps = {}
        for s, v in toks:
            if deps.get(s, 0) < v:
                deps[s] = v
        self.ops[eng].append((None, deps, None, 0))

    def barrier(self):
        frontier = [(s, v) for s, v in self.sem_val.items()]
        for e in self.ENGS:
            self.wait_all(e, frontier)

    def emit(self, final_toks):
        nc = self.nc
        with ExitStack() as st:
            sems = [st.enter_context(nc.semaphore(n)) for n in self.sem_names]
            self.wait_all("sp", final_toks)
            block = st.enter_context(nc.Block())

            def run(engname):
                def body(eng):
                    known = {}
                    own = self.eng_sem[engname]
                    for fn, deps, tok, inc in self.ops[engname]:
                        for s in sorted(deps):
                            v = deps[s]
                            if s == own and (engname == "pe" or not SAME_ENGINE_SYNC):
                                continue
                            if known.get(s, 0) >= v:
                                continue
                            eng.wait_ge(sems[s], v)
                            known[s] = v
                        if fn is not None:
                            inst = fn(eng)
                            inst.then_inc(sems[tok[0]], inc)
                return body
            block.tensor(run("pe"))
            block.scalar(run("act"))
            block.vector(run("dve"))
            block.gpsimd(run("pool"))
            block.sync(run("sp"))


class Arena:
    def __init__(self, nc, nbytes):
        self.t = nc.alloc_sbuf_tensor("arena", [128, nbytes], U8)
        self.ap = self.t.ap() if hasattr(self.t, "ap") else self.t[:]
        self.nbytes = nbytes
        self.off = 0
        self.base = 0

    def set_base(self):
        self.base = self.off

    def reset(self):
        self.off = self.base

    def alloc(self, shape, dt, parts=128):
        esz = 4 if dt == F32 else 2
        n = int(np.prod(shape)) * esz
        assert self.off + n <= self.nbytes, f"SBUF arena overflow {self.off}+{n}>{self.nbytes}"
        a = self.ap[0:parts, self.off:self.off + n].bitcast(dt)
        self.off += (n + 63) // 64 * 64
        if len(shape) == 2:
            a = a.rearrange("p (a b) -> p a b", b=shape[1])
        elif len(shape) == 3:
            a = a.rearrange("p (a b c) -> p a b c", b=shape[1], c=shape[2])
        return a


def build_program(S, depth):
    assert S % 512 == 0
    NT = S // 128
    NR = S // 512
    TGI = min(2048, S)
    TGF = min(1024, S)
    TL = min(1024, S)
    nc = bass.Bass("TRN2", target_bir_lowering=False)

    def din(name, shape):
        return nc.dram_tensor(name, shape, F32, kind="ExternalInput").ap()
    x_in = din("x", [S, D])
    w_in = din("w_in", [depth, D, NCOLS_IN])
    w_branch = din("w_branch", [depth, 4 * BW, D])
    w_out = din("w_out", [depth, D, D])
    w_fg = din("w_ffn_gate", [depth, D, DFF])
    w_fu = din("w_ffn_up", [depth, D, DFF])
    w_fd = din("w_ffn_down", [depth, DFF, D])
    lru_wa = din("lru_wa", [depth, 8, 64, 64])
    lru_wx = din("lru_wx", [depth, 8, 64, 64])
    norms = din("norms", [2 * depth + 1, D])
    pvec = din("pvec", [depth, 128, NPV])
    rvec = din("rvec", [depth, NRV])
    out = nc.dram_tensor("out", [S, D], F32, kind="ExternalOutput").ap()

    def dscr(name, shape, dt):
        return nc.dram_tensor(name, shape, dt).ap()
    hres = dscr("hres", [S, D], F32)
    qkA = dscr("qkA", [8, 128, S], BF16)
    vA = dscr("vA", [S, 512], BF16)
    qkC = dscr("qkC", [8, 128, S], BF16)
    vC = dscr("vC", [S, 512], BF16)
    cfT = dscr("cfT", [4, S], F32)
    cumD = dscr("cumD", [4, S], F32)
    qDd = dscr("qD", [8, 64, S], BF16)
    kDd = dscr("kD", [2, 64, S], BF16)
    vDd = dscr("vD", [S, 128], BF16)
    xbT = dscr("xbT", [4, 128, S], F32)
    gbT = dscr("gbT", [4, 128, S], F32)
    gatesD = dscr("gatesD", [64, 128, S], BF16)
    yall = dscr("yall", [16, 128, S], BF16)
    mixD = dscr("mixD", [16, 128, S], BF16)

    P = Prog(nc)
    AR = Arena(nc, 207 * 1024)
    ps = [nc.alloc_psum_tensor(f"ps{i}", [128, 512], F32) for i in range(8)]

    bank_ctr = {}

    def next_bank(lo=0, hi=8):
        c = bank_ctr.get((lo, hi), 0)
        bank_ctr[(lo, hi)] = c + 1
        return lo + c % (hi - lo)

    def OP(eng, method, reads, writes, **kw):
        return P.add(eng, lambda e: getattr(e, method)(**kw), reads=reads, writes=writes)

    def dma(q, out_ap, in_ap, reads, writes):
        return P.add(q, lambda e: e.dma_start(out=out_ap, in_=in_ap), reads=reads, writes=writes, dma=True)

    def mm_group(outp, pairs, bank_res, reads, first=True, last=True):
        def fn(e):
            n = len(pairs)
            for i, (l_, r_) in enumerate(pairs):
                ins = e.matmul(outp, lhsT=l_, rhs=r_, start=(first and i == 0), stop=(last and i == n - 1))
            return ins
        return P.add("pe", fn, reads=reads, writes=[bank_res])

    evac_rr = [0]

    def evac_copy(out_ap, in_ap, reads, writes):
        evac_rr[0] ^= 1
        if evac_rr[0]:
            return OP("act", "copy", reads, writes, out=out_ap, in_=in_ap)
        return OP("dve", "tensor_copy", reads, writes, out=out_ap, in_=in_ap)

    ident_f = AR.alloc([128], F32)
    ones_f = AR.alloc([128], F32)
    ident = AR.alloc([128], BF16)
    ones_b = AR.alloc([128], BF16)
    maskneg = AR.alloc([128], F32)
    rowbA = AR.alloc([4, 512], F32)
    colbA = AR.alloc([4, NT + 4], F32)
    mbD = AR.alloc([8, 256], F32)
    iot = AR.alloc([512], F32)
    iotc = AR.alloc([NT + 4], F32)
    tmpc = AR.alloc([256], F32)
    eps_col = AR.alloc([1], F32)
    AR.set_base()

    OP("pool", "memset", [], ["ones_f"], ap=ones_f, constant=1.0)
    OP("pool", "memset", [], ["eps_col"], ap=eps_col, constant=RMS_EPS)
    OP("pool", "affine_select", ["ones_f"], ["ident_f"], out=ident_f, in_=ones_f, pattern=[[-1, 128]],
       compare_op=ALU.is_equal, fill=0.0, base=0, channel_multiplier=1)
    OP("pool", "memset", [], ["tmpc"], ap=tmpc[:, 0:128], constant=0.0)
    OP("pool", "affine_select", ["tmpc"], ["maskneg"], out=maskneg, in_=tmpc[:, 0:128], pattern=[[1, 128]],
       compare_op=ALU.is_ge, fill=NEG, base=0, channel_multiplier=-1)
    OP("dve", "tensor_copy", ["ident_f"], ["ident"], out=ident, in_=ident_f)
    OP("dve", "tensor_copy", ["ones_f"], ["ones_b"], out=ones_b, in_=ones_f)
    OP("pool", "iota", [], ["iot"], out=iot, pattern=[[1, 512]], base=0, channel_multiplier=0,
       allow_small_or_imprecise_dtypes=True)
    OP("pool", "iota", [], ["iotc"], out=iotc, pattern=[[128, NT + 4]], base=-128 * NT, channel_multiplier=1,
       allow_small_or_imprecise_dtypes=True)
    for h in range(4):
        OP("dve", "tensor_scalar", ["iot"], [("rowbA", h)], out=rowbA[:, h, :], in0=iot, scalar1=-SLOPES_A[h],
           scalar2=None, op0=ALU.mult)
        OP("dve", "tensor_scalar", ["iotc"], [("colbA", h)], out=colbA[:, h, :], in0=iotc, scalar1=SLOPES_A[h],
           scalar2=None, op0=ALU.mult)
    OP("pool", "iota", ["maskneg"], ["tmpc"], out=tmpc, pattern=[[1, 256]], base=0, channel_multiplier=-1,
       allow_small_or_imprecise_dtypes=True)
    for h in range(8):
        OP("dve", "tensor_scalar", ["tmpc"], [("mbD", h)], out=mbD[:, h, :], in0=tmpc, scalar1=-SLOPES_D[h],
           scalar2=None, op0=ALU.mult)
        OP("pool", "affine_select", [("mbD", h)], [("mbD", h)], out=mbD[:, h, :], in_=mbD[:, h, :],
           pattern=[[1, 256]], compare_op=ALU.is_ge, fill=NEG, base=0, channel_multiplier=-1)
        OP("pool", "affine_select", [("mbD", h)], [("mbD", h)], out=mbD[:, h, :], in_=mbD[:, h, :],
           pattern=[[-1, 256]], compare_op=ALU.is_ge, fill=NEG, base=127, channel_multiplier=1)
    P.barrier()

    def norm_transpose(src, row0, ntok, gain_row, uT, uT_res):
        gbc = AR.alloc([D], F32)
        dma("sp", gbc, norms[gain_row:gain_row + 1, :].partition_broadcast(128), [], ["gbc"])
        hx = [AR.alloc([D], F32) for _ in range(2)]
        un = [AR.alloc([D], BF16) for _ in range(2)]
        junk = AR.alloc([D], BF16)
        st = [AR.alloc([4], F32) for _ in range(2)]
        for i in range(ntok // 128):
            b = i % 2
            r0 = row0 + i * 128
            dma("sp", hx[b], src[r0:r0 + 128, :], [], [("hx", b)])
            OP("act", "activation", [("hx", b)], ["junk", ("st", b, 0)], out=junk, in_=hx[b], func=AF.Square,
               accum_out=st[b][:, 0:1])
            OP("act", "activation", [("st", b, 0), "eps_col"], [("st", b, 1)], out=st[b][:, 1:2], in_=st[b][:, 0:1],
               func=AF.Sqrt, scale=1.0 / D, bias=eps_col)
            OP("dve", "reciprocal", [("st", b, 1)], [("st", b, 2)], out=st[b][:, 2:3], in_=st[b][:, 1:2])
            OP("dve", "scalar_tensor_tensor", [("hx", b), ("st", b, 2), "gbc"], [("un", b)], out=un[b], in0=hx[b],
               scalar=st[b][:, 2:3], in1=gbc, op0=ALU.mult, op1=ALU.mult)
            for half in range(2):
                bk = next_bank()
                pvw = ps[bk][:, :].bitcast(BF16)

                def fn(e, b=b, half=half, pvw=pvw):
                    for j in range(8):
                        k = half * 8 + j
                        ins = e.transpose(out=pvw[:, 128 * j:128 * j + 128], in_=un[b][:, 128 * k:128 * k + 128],
                                          identity=ident)
                    return ins
                P.add("pe", fn, reads=[("un", b), "ident"], writes=[("ps", bk)])
                evac_copy(uT[:, half * 8:half * 8 + 8, i * 128:i * 128 + 128],
                          pvw.rearrange("p (k t) -> p k t", t=128),
                          [("ps", bk)], [(uT_res, i, half)])

    for l in range(depth):
        src = x_in if l == 0 else hres
        lam_init = 0.8 - 0.6 * math.exp(-0.3 * l)
        AR.reset()
        pv = AR.alloc([NPV], F32)
        rv = AR.alloc([NRV], F32)
        lw = AR.alloc([16], F32)
        wab = AR.alloc([4, 128], BF16)
        wxb = AR.alloc([4, 128], BF16)
        prod = AR.alloc([128], F32)
        nsp = AR.alloc([12], F32)
        bfc = AR.alloc([1], F32)
        layer_base = AR.off
        dma("sp", pv, pvec[l], [], ["pv"])
        dma("sp", rv, rvec[l:l + 1, :].partition_broadcast(128), [], ["rv"])
        dma("sp", bfc[0:4, :], rvec[l, RV_BF:RV_BF + 4].rearrange("(a b) -> a b", b=1), [], ["bfc"])
        OP("pool", "memset", [], ["wab"], ap=wab, constant=0.0)
        OP("pool", "memset", [], ["wxb"], ap=wxb, constant=0.0)
        for bi in range(8):
            cc, hb = bi // 2, bi % 2
            dma("pool", wab[64 * hb:64 * hb + 64, cc, 64 * hb:64 * hb + 64], lru_wa[l, bi], ["wab"], [("wab", bi)])
            dma("pool", wxb[64 * hb:64 * hb + 64, cc, 64 * hb:64 * hb + 64], lru_wx[l, bi], ["wxb"], [("wxb", bi)])
        OP("dve", "tensor_tensor", ["rv"], ["prod0"], out=prod[:, 0:64], in0=rv[:, RV_LQ1:RV_LQ1 + 64],
           in1=rv[:, RV_LK1:RV_LK1 + 64], op=ALU.mult)
        OP("dve", "tensor_tensor", ["rv"], ["prod1"], out=prod[:, 64:128], in0=rv[:, RV_LQ2:RV_LQ2 + 64],
           in1=rv[:, RV_LK2:RV_LK2 + 64], op=ALU.mult)
        OP("dve", "reduce_sum", ["prod0"], [("lw", 2)], out=lw[:, 2:3], in_=prod[:, 0:64], axis=AX.X)
        OP("dve", "reduce_sum", ["prod1"], [("lw", 3)], out=lw[:, 3:4], in_=prod[:, 64:128], axis=AX.X)
        OP("act", "activation", [("lw", 2), ("lw", 3)], [("lw", 4)], out=lw[:, 4:6], in_=lw[:, 2:4], func=AF.Exp)
        OP("dve", "scalar_tensor_tensor", [("lw", 4)], [("lw", 0)], out=lw[:, 0:1], in0=lw[:, 5:6], scalar=-lam_init,
           in1=lw[:, 4:5], op0=ALU.add, op1=ALU.subtract)
        OP("dve", "tensor_scalar", ["pv"], [("lw", 1)], out=lw[:, 1:2], in0=pv[:, PV_SUB:PV_SUB + 1],
           scalar1=1.0 - lam_init, scalar2=None, op0=ALU.mult)
        OP("act", "activation", ["rv"], [("lw", 8)], out=lw[:, 8:16], in_=rv[:, RV_SINK:RV_SINK + 8], func=AF.Exp)
        OP("act", "activation", ["pv"], ["nsp0"], out=nsp[:, 0:4], in_=pv[:, PV_LAM:PV_LAM + 4], func=AF.Exp, scale=-1.0)
        OP("dve", "tensor_scalar", ["nsp0"], ["nsp0"], out=nsp[:, 0:4], in0=nsp[:, 0:4], scalar1=1.0, scalar2=None,
           op0=ALU.add)
        OP("act", "activation", ["nsp0"], ["nsp0"], out=nsp[:, 0:4], in_=nsp[:, 0:4], func=AF.Ln)
        OP("dve", "tensor_scalar", ["nsp0"], ["nsp8"], out=nsp[:, 4:8], in0=nsp[:, 0:4], scalar1=-8.0, scalar2=None,
           op0=ALU.mult)
        OP("dve", "tensor_scalar", ["nsp0"], ["nsp16"], out=nsp[:, 8:12], in0=nsp[:, 0:4], scalar1=-16.0, scalar2=None,
           op0=ALU.mult)
        P.barrier()

        blocks = [("aq", C_AQ, 512), ("ak", C_AK, 512), ("av", C_AV, 512), ("bx", C_BX, 512), ("bg", C_BG, 512),
                  ("cq", C_CQ, 512), ("ck", C_CK, 512), ("cvf", C_CV, 516), ("dq", C_DQ, 512), ("dkv", C_DK, 256)]
        blocks += [("gz", C_GZ + 512 * i, 512) for i in range(16)]
        wv_in = w_in[l].rearrange("(k p) n -> p k n", p=128)
        for g in range(S // TGI):
            AR.off = layer_base
            uT = AR.alloc([KC, TGI], BF16)
            mark = AR.off
            norm_transpose(src, g * TGI, TGI, 2 * l, uT, "uT")
            P.barrier()
            AR.off = mark
            wblk = [AR.alloc([KC, 516], BF16) for _ in range(2)]
            stg = [AR.alloc([TGI], F32) for _ in range(3)]
            stg_i = [0]
            tok0 = g * TGI
            uT_reads = [("uT", i, hf) for i in range(TGI // 128) for hf in range(2)]

            def fm_chunk(wb, wres, c0, m, dst, dst_res, dt, func=None, bias=None, bias_res=None):
                si = stg_i[0] % 3
                stg_i[0] += 1
                sview = stg[si] if dt == F32 else stg[si].bitcast(BF16)[:, 0:TGI]
                for r in range(TGI // 512):
                    bk = next_bank()
                    pairs = [(wb[:, k, c0:c0 + m], uT[:, k, r * 512:(r + 1) * 512]) for k in range(KC)]
                    mm_group(ps[bk][0:m, :], pairs, ("ps", bk), [wres] + uT_reads)
                    o = sview[0:m, r * 512:(r + 1) * 512]
                    if func is not None:
                        OP("act", "activation", [("ps", bk)] + ([bias_res] if bias_res else []), [("stg", si, r)],
                           out=o, in_=ps[bk][0:m, :], func=func, bias=bias)
                    else:
                        evac_copy(o, ps[bk][0:m, :], [("ps", bk)], [("stg", si, r)])
                dma("sp", dst[:, tok0:tok0 + TGI], sview[0:m, :], [("stg", si, r) for r in range(TGI // 512)],
                    [dst_res])

            def tm_block(wb, wres, c0, n, dst, dst_res):
                for i in range(TGI // 128):
                    si = stg_i[0] % 3
                    stg_i[0] += 1
                    sview = stg[si].bitcast(BF16)[:, 0:n]
                    bk = next_bank()
                    pairs = [(uT[:, k, i * 128:(i + 1) * 128], wb[:, k, c0:c0 + n]) for k in range(KC)]
                    mm_group(ps[bk][:, 0:n], pairs, ("ps", bk), [wres] + uT_reads)
                    evac_copy(sview, ps[bk][:, 0:n], [("ps", bk)], [("stg", si, 0)])
                    dma("sp", dst[tok0 + i * 128:tok0 + (i + 1) * 128, :], sview, [("stg", si, 0)], [(dst_res, i)])

            for bi, (name, c0, wcols) in enumerate(blocks):
                wb = wblk[bi % 2]
                wres = ("wblk", bi % 2)
                dma("pool", wb[:, :, 0:wcols], wv_in[:, :, c0:c0 + wcols], [], [wres])
                if name in ("aq", "ak"):
                    base = 0 if name == "aq" else 4
                    for c in range(4):
                        fm_chunk(wb, wres, 128 * c, 128, qkA[base + c], ("qkA", base + c, g), BF16)
                elif name in ("cq", "ck"):
                    base = 0 if name == "cq" else 4
                    for c in range(4):
                        fm_chunk(wb, wres, 128 * c, 128, qkC[base + c], ("qkC", base + c, g), BF16)
                elif name == "av":
                    tm_block(wb, wres, 0, 512, vA, ("vA", g))
                elif name == "cvf":
                    tm_block(wb, wres, 0, 512, vC, ("vC", g))
                    fm_chunk(wb, wres, 512, 4, cfT, ("cfT", g), F32)
                elif name == "bx":
                    for c in range(4):
                        fm_chunk(wb, wres, 128 * c, 128, xbT[c], ("xbT", c, g), F32)
                elif name == "bg":
                    for c in range(4):
                        fm_chunk(wb, wres, 128 * c, 128, gbT[c], ("gbT", c, g), F32)
                elif name == "dq":
                    for h in range(8):
                        fm_chunk(wb, wres, 64 * h, 64, qDd[h], ("qD", h, g), BF16)
                elif name == "dkv":
                    for kv in range(2):
                        fm_chunk(wb, wres, 64 * kv, 64, kDd[kv], ("kD", kv, g), BF16)
                    tm_block(wb, wres, 128, 128, vDd, ("vD", g))
                else:
                    gi = (c0 - C_GZ) // 128
                    for c in range(4):
                        fm_chunk(wb, wres, 128 * c, 128, gatesD[gi + c], ("gates", gi + c, g), BF16,
                                 func=AF.Sigmoid, bias=pv[:, PV_BG + gi + c:PV_BG + gi + c + 1], bias_res="pv")
            P.barrier()

        AR.off = layer_base
        cf_sb = AR.alloc([S], F32)
        cum_sb = AR.alloc([S], F32)
        ones_row = AR.alloc([S], F32)
        dma("sp", cf_sb[0:4, :], cfT[:, :], [], ["cf_sb"])
        OP("act", "activation", ["cf_sb", "bfc"], ["cf_sb"], out=cf_sb[0:4, :], in_=cf_sb[0:4, :], func=AF.Sigmoid,
           bias=bfc[0:4, :])
        OP("act", "activation", ["cf_sb"], ["cf_sb"], out=cf_sb[0:4, :], in_=cf_sb[0:4, :], func=AF.Ln)
        OP("pool", "memset", [], ["ones_row"], ap=ones_row[0:4, :], constant=1.0)
        OP("dve", "tensor_tensor_scan", ["cf_sb", "ones_row"], ["cum_sb"], out=cum_sb[0:4, :], data0=ones_row[0:4, :],
           data1=cf_sb[0:4, :], initial=0.0, op0=ALU.mult, op1=ALU.add)
        dma("sp", cumD[:, :], cum_sb[0:4, :], ["cum_sb"], ["cumD"])
        P.barrier()

        def attention(kind):
            AR.off = layer_base
            qk = qkA if kind == "A" else qkC
            vsrc = vA if kind == "A" else vC
            NB = 4
            LA = 3
            V = AR.alloc([NT, 512], BF16)
            dma("sp", V, vsrc.rearrange("(c p) n -> p c n", p=128), [], ["V"])
            qT = [AR.alloc([S], BF16) for _ in range(2)]
            kT = [AR.alloc([S], BF16) for _ in range(2)]
            tmp = [AR.alloc([512], F32) for _ in range(NB)]
            eT = [AR.alloc([512], BF16) for _ in range(NB)]
            fin = [AR.alloc([512], F32) for _ in range(6)]
            sqb = AR.alloc([512], BF16)
            yst = [AR.alloc([512], BF16) for _ in range(2)]
            if kind == "C":
                cumT = AR.alloc([NT, 4], F32)
                Rb = [AR.alloc([512], F32) for _ in range(2)]
                rowbC = [AR.alloc([512], F32) for _ in range(2)]
                colbC = [AR.alloc([NT], F32) for _ in range(2)]
                cum4 = AR.alloc([S], F32)
                dma("sp", cum4[0:4, :], cumD[:, :], [], ["cum4"])
                bkc = next_bank(4, 8)

                def fnc(e):
                    for c in range(NT):
                        ins = e.transpose(out=ps[bkc][:, 4 * c:4 * c + 4], in_=cum4[0:4, 128 * c:128 * c + 128],
                                          identity=ident_f[0:4, 0:4])
                    return ins
                P.add("pe", fnc, reads=["cum4", "ident_f"], writes=[("ps", bkc)])
                OP("dve", "tensor_copy", [("ps", bkc)], ["cumT"], out=cumT,
                   in_=ps[bkc][:, 0:4 * NT].rearrange("p (c h) -> p c h", h=4))
            pend = []

            def push(front, back):
                front()
                pend.append(back)
                if len(pend) > LA:
                    pend.pop(0)()

            def finalize(h, tr, banks, ysi):
                t0 = tr * 512
                bo, bn = banks[0]
                OP("dve", "reciprocal", [("ps", bn)], [("fin", 0)], out=fin[0], in_=ps[bn][:, :])
                if kind == "C":
                    OP("dve", "tensor_tensor", [("ps", bo), ("fin", 0)], [("yst", ysi)], out=yst[ysi],
                       in0=ps[bo][:, :], in1=fin[0], op=ALU.mult)
                    dma("sp", yall[8 + h][:, t0:t0 + 512], yst[ysi], [("yst", ysi)], [("yall", 8 + h, tr)])
                    return
                OP("dve", "tensor_tensor", [("ps", bo), ("fin", 0)], [("fin", 1)], out=fin[1], in0=ps[bo][:, :],
                   in1=fin[0], op=ALU.mult)
                bo1, bn1 = banks[1]
                OP("dve", "reciprocal", [("ps", bn1)], [("fin", 2)], out=fin[2], in_=ps[bn1][:, :])
                OP("dve", "tensor_tensor", [("ps", bo1), ("fin", 2)], [("fin", 3)], out=fin[3], in0=ps[bo1][:, :],
                   in1=fin[2], op=ALU.mult)
                OP("dve", "scalar_tensor_tensor", [("fin", 3), ("fin", 1), ("lw", 0)], [("fin", 4)], out=fin[4],
                   in0=fin[3], scalar=lw[:, 0:1], in1=fin[1], op0=ALU.mult, op1=ALU.add)
                OP("act", "activation", [("fin", 4)], ["sqb"], out=sqb, in_=fin[4], func=AF.Square)
                bq = next_bank(4, 8)
                mm_group(ps[bq][:, :], [(ones_b, sqb)], ("ps", bq), ["sqb", "ones_b"])
                OP("act", "activation", [("ps", bq), "eps_col"], [("fin", 5)], out=fin[5], in_=ps[bq][:, :],
                   func=AF.Sqrt, scale=1.0 / 128, bias=eps_col)
                OP("dve", "reciprocal", [("fin", 5)], [("fin", 5)], out=fin[5], in_=fin[5])
                OP("dve", "scalar_tensor_tensor", [("fin", 4), ("fin", 5), ("lw", 1)], [("yst", ysi)], out=yst[ysi],
                   in0=fin[4], scalar=lw[:, 1:2], in1=fin[5], op0=ALU.mult, op1=ALU.mult)
                dma("sp", yall[h][:, t0:t0 + 512], yst[ysi], [("yst", ysi)], [("yall", h, tr)])

            def make_tile(h, hb, m, tr, sc, nsc, rb, bo, bn, ti, fin_args):
                t0 = tr * 512
                di = sc - 4 * tr
                ta = t0 + 128 * di if di > 0 else t0
                n = t0 + 512 - ta
                off = ta - t0
                if kind == "A":
                    lhsT = kT[hb][64 * m:64 * m + 64, sc * 128:(sc + 1) * 128]
                    rhs = qT[hb][64 * m:64 * m + 64, ta:t0 + 512]
                    scale = 0.125
                    rowb = rowbA[:, h, off:512]
                    rowres = ("rowbA", h)
                    ci = NT + sc - 4 * tr
                    colb = colbA[:, h, ci:ci + 1]
                    colres = ("colbA", h)
                else:
                    lhsT = kT[hb][:, sc * 128:(sc + 1) * 128]
                    rhs = qT[hb][:, ta:t0 + 512]
                    scale = 128.0 ** -0.5
                    rowb = rowbC[rb][:, off:512]
                    rowres = ("rowbC", rb)
                    colb = colbC[rb][:, sc:sc + 1]
                    colres = ("colbC", rb)

                def front():
                    bs = next_bank(4, 8)
                    mm_group(ps[bs][:, 0:n], [(lhsT, rhs)], ("ps", bs), [("qT", hb), ("kT", hb)])
                    OP("dve", "scalar_tensor_tensor", [("ps", bs), rowres], [("tmp", ti)], out=tmp[ti][:, 0:n],
                       in0=ps[bs][:, 0:n], scalar=scale, in1=rowb, op0=ALU.mult, op1=ALU.add)
                    if di >= 0:
                        OP("pool", "tensor_tensor", [("tmp", ti), "maskneg"], [("tmp", ti)],
                           out=tmp[ti][:, 0:128], in0=tmp[ti][:, 0:128], in1=maskneg, op=ALU.add)
                    OP("act", "activation", [("tmp", ti), colres], [("eT", ti)], out=eT[ti][:, 0:n],
                       in_=tmp[ti][:, 0:n], func=AF.Exp, bias=colb)

                def back():
                    first, last = sc == 0, sc == nsc - 1
                    mm_group(ps[bo][:, off:512], [(V[:, sc, 128 * h:128 * h + 128], eT[ti][:, 0:n])],
                             ("ps", bo), [("eT", ti), "V"], first=first, last=last)
                    mm_group(ps[bn][:, off:512], [(ones_b, eT[ti][:, 0:n])],
                             ("ps", bn), [("eT", ti), "ones_b"], first=first, last=last)
                    if fin_args is not None:
                        finalize(*fin_args)
                return front, back

            it = 0
            yi = 0
            ri = 0
            for h in range(4):
                hb = h % 2
                dma("sp", qT[hb], qk[h], [], [("qT", hb)])
                dma("sp", kT[hb], qk[4 + h], [], [("kT", hb)])
                for tr in range(NR):
                    t0 = tr * 512
                    nsc = 4 * tr + 4
                    rb = 0
                    if kind == "C":
                        rb = ri % 2
                        dma("sp", Rb[rb], cumD[h:h + 1, t0:t0 + 512].partition_broadcast(128), [], [("Rb", rb)])
                        OP("dve", "tensor_scalar", [("Rb", rb)], [("rowbC", rb)], out=rowbC[rb], in0=Rb[rb],
                           scalar1=Rb[rb][:, 0:1], scalar2=None, op0=ALU.subtract)
                        OP("dve", "tensor_scalar", [("Rb", rb), "cumT"], [("colbC", rb)], out=colbC[rb][:, 0:nsc],
                           in0=cumT[:, 0:nsc, h], scalar1=-1.0, scalar2=Rb[rb][:, 0:1], op0=ALU.mult, op1=ALU.add)
                    if kind == "A":
                        maps = 2
                        banks = [(0, 1), (2, 3)]
                    else:
                        maps = 1
                        banks = [(0, 1)] if ri % 2 == 0 else [(2, 3)]
                    ri += 1
                    ysi = yi % 2
                    yi += 1
                    for m in range(maps):
                        bo, bn = banks[m]
                        for sc in range(nsc):
                            lastt = (m == maps - 1 and sc == nsc - 1)
                            fr, bk_ = make_tile(h, hb, m, tr, sc, nsc, rb, bo, bn, it % NB,
                                                (h, tr, banks, ysi) if lastt else None)
                            it += 1
                            push(fr, bk_)
            while pend:
                pend.pop(0)()
            P.barrier()

        attention("A")
        attention("C")

        AR.off = layer_base
        Vd = AR.alloc([NT, 128], BF16)
        dma("sp", Vd, vDd.rearrange("(c p) n -> p c n", p=128), [], ["Vd"])
        kTd = AR.alloc([2, S], BF16)
        for kv in range(2):
            dma("sp", kTd[0:64, kv, :], kDd[kv], [], [("kTd", kv)])
        qTd = [AR.alloc([S], BF16) for _ in range(2)]
        NBD = 10
        tmpd = [AR.alloc([256], F32) for _ in range(NBD)]
        eTd = [AR.alloc([256], BF16) for _ in range(NBD)]
        find = [AR.alloc([512], F32) for _ in range(2)]
        ystd = [AR.alloc([512], BF16) for _ in range(2)]
        itd = [0]
        pendd = []

        def make_group(h, hb, kv, tr, gi_):
            t0 = tr * 512
            bo, bn = (0, 1) if gi_ % 2 == 0 else (2, 3)
            etiles = {}

            def front():
                for c in range(4 * tr - 1, 4 * tr + 4):
                    if c < 0:
                        continue
                    tca = max(c, 4 * tr)
                    tcb = min(c + 1, 4 * tr + 3)
                    ta, n = tca * 128, (tcb - tca + 1) * 128
                    moff = 0 if tca == c else 128
                    bs = next_bank(4, 8)
                    ti = itd[0] % NBD
                    itd[0] += 1
                    mm_group(ps[bs][:, 0:n], [(kTd[0:64, kv, c * 128:(c + 1) * 128], qTd[hb][0:64, ta:ta + n])],
                             ("ps", bs), [("qTd", hb), ("kTd", kv)])
                    OP("dve", "scalar_tensor_tensor", [("ps", bs), ("mbD", h)], [("tmpd", ti)], out=tmpd[ti][:, 0:n],
                       in0=ps[bs][:, 0:n], scalar=0.125, in1=mbD[:, h, moff:moff + n], op0=ALU.mult, op1=ALU.add)
                    OP("act", "activation", [("tmpd", ti)], [("eTd", ti)], out=eTd[ti][:, 0:n], in_=tmpd[ti][:, 0:n],
                       func=AF.Exp)
                    etiles[c] = (ti, tca)

            def back():
                for bank, is_o in ((bo, True), (bn, False)):
                    for tc in range(4 * tr, 4 * tr + 4):
                        srcs = [c for c in (tc - 1, tc) if c >= 0]
                        for j, c in enumerate(srcs):
                            ti, tca = etiles[c]
                            eo = (tc - tca) * 128
                            lhs = Vd[:, c, 64 * kv:64 * kv + 64] if is_o else ones_b[:, 0:64]
                            col = (tc - 4 * tr) * 128
                            mm_group(ps[bank][0:64, col:col + 128], [(lhs, eTd[ti][:, eo:eo + 128])], ("ps", bank),
                                     [("eTd", ti), "Vd", "ones_b"], first=(j == 0), last=(j == len(srcs) - 1))
                fi = gi_ % 2
                OP("dve", "tensor_scalar", [("ps", bn), ("lw", 8)], [("find", fi)], out=find[fi][0:64, :],
                   in0=ps[bn][0:64, :], scalar1=lw[0:64, 8 + h:9 + h], scalar2=None, op0=ALU.add)
                OP("dve", "reciprocal", [("find", fi)], [("find", fi)], out=find[fi][0:64, :], in_=find[fi][0:64, :])
                OP("dve", "tensor_tensor", [("ps", bo), ("find", fi)], [("ystd", fi)], out=ystd[fi][0:64, :],
                   in0=ps[bo][0:64, :], in1=find[fi][0:64, :], op=ALU.mult)
                dma("sp", yall[12 + h // 2][64 * (h % 2):64 * (h % 2) + 64, t0:t0 + 512], ystd[fi][0:64, :],
                    [("ystd", fi)], [("yall", 12 + h // 2, tr, h % 2)])
            return front, back

        gi_ = 0
        for h in range(8):
            hb = h % 2
            kv = h // 4
            dma("sp", qTd[hb][0:64, :], qDd[h], [], [("qTd", hb)])
            for tr in range(NR):
                fr, bk_ = make_group(h, hb, kv, tr, gi_)
                gi_ += 1
                fr()
                pendd.append(bk_)
                if len(pendd) > 1:
                    pendd.pop(0)()
        while pendd:
            pendd.pop(0)()
        P.barrier()

        AR.off = layer_base
        NLR = S // TL
        NSB = TL // 512
        xb = [AR.alloc([TL + 3], F32) for _ in range(2)]
        gb = [AR.alloc([TL], F32) for _ in range(2)]
        xc_l = [AR.alloc([TL], F32) for _ in range(2)]
        xcb_l = [AR.alloc([TL], BF16) for _ in range(2)]
        rr_l = [AR.alloc([TL], F32) for _ in range(2)]
        ii_l = [AR.alloc([TL], F32) for _ in range(2)]
        aa_l = [AR.alloc([TL], F32) for _ in range(2)]
        a2_l = [AR.alloc([TL], F32) for _ in range(2)]
        hh = [AR.alloc([TL], F32) for _ in range(2)]
        gq_l = [AR.alloc([TL], F32) for _ in range(2)]
        ybs = [AR.alloc([TL], BF16) for _ in range(2)]
        li = 0
        for cc in range(4):
            for rg in range(NLR):
                t0 = rg * TL
                b = li % 2
                hp, hc = hh[li % 2], hh[(li + 1) % 2]
                hres_p, hres_c = ("hh", li % 2), ("hh", (li + 1) % 2)
                li += 1
                xc, xcb, rr, ii, aa, a2, gq = xc_l[b], xcb_l[b], rr_l[b], ii_l[b], aa_l[b], a2_l[b], gq_l[b]
                if rg == 0:
                    OP("pool", "memset", [], [("xb", b, "h")], ap=xb[b][:, 0:3], constant=0.0)
                    dma("sp", xb[b][:, 3:3 + TL], xbT[cc][:, 0:TL], [], [("xb", b)])
                else:
                    dma("sp", xb[b][:, 0:3 + TL], xbT[cc][:, t0 - 3:t0 + TL], [], [("xb", b), ("xb", b, "h")])
                dma("sp", gb[b], gbT[cc][:, t0:t0 + TL], [], [("gb", b)])
                xr = [("xb", b), ("xb", b, "h"), "pv"]
                cwc = PV_CW + 4 * cc
                OP("dve", "tensor_scalar", xr, [("xc", b)], out=xc, in0=xb[b][:, 3:3 + TL], scalar1=pv[:, cwc + 3:cwc + 4],
                   scalar2=pv[:, PV_CB + cc:PV_CB + cc + 1], op0=ALU.mult, op1=ALU.add)
                for j in range(3):
                    OP("dve", "scalar_tensor_tensor", xr + [("xc", b)], [("xc", b)], out=xc, in0=xb[b][:, j:j + TL],
                       scalar=pv[:, cwc + j:cwc + j + 1], in1=xc, op0=ALU.mult, op1=ALU.add)
                OP("act", "copy", [("xc", b)], [("xcb", b)], out=xcb, in_=xc)
                for sb in range(NSB):
                    sl = slice(sb * 512, (sb + 1) * 512)
                    b1, b2 = next_bank(), next_bank()
                    mm_group(ps[b1][:, :], [(wab[:, cc, :], xcb[:, sl])], ("ps", b1),
                             [("xcb", b), ("wab", 2 * cc), ("wab", 2 * cc + 1), "wab"])
                    mm_group(ps[b2][:, :], [(wxb[:, cc, :], xcb[:, sl])], ("ps", b2),
                             [("xcb", b), ("wxb", 2 * cc), ("wxb", 2 * cc + 1), "wxb"])
                    OP("act", "activation", [("ps", b1), "pv"], [("rr", b, sb)], out=rr[:, sl], in_=ps[b1][:, :],
                       func=AF.Sigmoid, bias=pv[:, PV_BA + cc:PV_BA + cc + 1])
                    OP("act", "activation", [("ps", b2), "pv"], [("ii", b, sb)], out=ii[:, sl], in_=ps[b2][:, :],
                       func=AF.Sigmoid, bias=pv[:, PV_BX + cc:PV_BX + cc + 1])
                rrr = [("rr", b, sb) for sb in range(NSB)]
                iir = [("ii", b, sb) for sb in range(NSB)]
                OP("act", "activation", rrr + ["nsp8"], [("aa", b)], out=aa, in_=rr, func=AF.Exp, scale=nsp[:, 4 + cc:5 + cc])
                OP("act", "activation", rrr + ["nsp16"], [("a2", b)], out=a2, in_=rr, func=AF.Exp, scale=nsp[:, 8 + cc:9 + cc])
                OP("pool", "tensor_scalar", [("a2", b)], [("a2", b)], out=a2, in0=a2, scalar1=-1.0, scalar2=1.0, op0=ALU.mult,
                   op1=ALU.add)
                OP("act", "activation", [("a2", b)], [("a2", b)], out=a2, in_=a2, func=AF.Sqrt)
                OP("pool", "tensor_tensor", iir + [("xc", b)], iir, out=ii, in0=ii, in1=xc, op=ALU.mult)
                OP("pool", "tensor_tensor", iir + [("a2", b)], iir, out=ii, in0=ii, in1=a2, op=ALU.mult)
                if rg == 0:
                    OP("dve", "tensor_tensor_scan", iir + [("aa", b)], [hres_c], out=hc, data0=aa, data1=ii, initial=0.0,
                       op0=ALU.mult, op1=ALU.add)
                else:
                    OP("dve", "tensor_tensor_scan", iir + [("aa", b), hres_p], [hres_c], out=hc, data0=aa, data1=ii,
                       initial=hp[:, TL - 1:TL], op0=ALU.mult, op1=ALU.add)
                OP("act", "activation", [("gb", b)], [("gq", b)], out=gq, in_=gb[b], func=AF.Square)
                OP("pool", "tensor_scalar", [("gq", b)], [("gq", b)], out=gq, in0=gq, scalar1=0.044715, scalar2=1.0,
                   op0=ALU.mult, op1=ALU.add)
                OP("pool", "tensor_tensor", [("gq", b), ("gb", b)], [("gq", b)], out=gq, in0=gq, in1=gb[b], op=ALU.mult)
                OP("act", "activation", [("gq", b)], [("gq", b)], out=gq, in_=gq, func=AF.Sigmoid, scale=1.5957691216057308)
                OP("pool", "tensor_tensor", [("gq", b), ("gb", b)], [("gq", b)], out=gq, in0=gq, in1=gb[b], op=ALU.mult)
                OP("dve", "tensor_tensor", [("gq", b), hres_c], [("ybs", b)], out=ybs[b], in0=gq, in1=hc, op=ALU.mult)
                dma("sp", yall[4 + cc][:, t0:t0 + TL], ybs[b], [("ybs", b)], [("yall", 4 + cc, rg)])
        P.barrier()

        AR.off = layer_base
        Wb = AR.alloc([16, D], BF16)
        wbv = w_branch[l].rearrange("(k p) n -> p k n", p=128)
        for q in range(4):
            dma("pool", Wb[:, 4 * q:4 * q + 4, :], wbv[:, 4 * q:4 * q + 4, :], [], [("Wb", q)])
        Wb_r = [("Wb", q) for q in range(4)]
        ysb = [AR.alloc([16, 512], BF16) for _ in range(2)]
        gsb = [AR.alloc([4, 512], BF16) for _ in range(2)]
        mixT = [AR.alloc([KC, 512], BF16) for _ in range(2)]
        mtmp = [AR.alloc([512], F32) for _ in range(3)]
        macc = AR.alloc([512], F32)
        gview = gatesD.rearrange("(n c) p s -> c p n s", n=4)
        gi = 0
        for tr in range(NR):
            t0 = tr * 512
            yb_ = tr % 2
            dma("sp", ysb[yb_], yall[:, :, t0:t0 + 512].rearrange("k p t -> p k t"), [], [("ysb", yb_)])
            for cc in range(16):
                gbf = gi % 2
                gi += 1
                dma("sp", gsb[gbf], gview[cc][:, :, t0:t0 + 512], [], [("gsb", gbf)])
                for n in range(4):
                    bk = next_bank()
                    pairs = [(Wb[:, 4 * n + k, cc * 128:(cc + 1) * 128], ysb[yb_][:, 4 * n + k, :]) for k in range(4)]
                    mm_group(ps[bk][:, :], pairs, ("ps", bk), [("ysb", yb_)] + Wb_r)
                    if n == 0:
                        OP("dve", "tensor_tensor", [("ps", bk), ("gsb", gbf)], ["macc"], out=macc, in0=ps[bk][:, :],
                           in1=gsb[gbf][:, 0, :], op=ALU.mult)
                    else:
                        mi = n - 1
                        OP("dve", "tensor_tensor", [("ps", bk), ("gsb", gbf)], [("mtmp", mi)], out=mtmp[mi],
                           in0=ps[bk][:, :], in1=gsb[gbf][:, n, :], op=ALU.mult)
                        if n < 3:
                            OP("pool", "tensor_tensor", ["macc", ("mtmp", mi)], ["macc"], out=macc, in0=macc,
                               in1=mtmp[mi], op=ALU.add)
                        else:
                            OP("pool", "tensor_tensor", ["macc", ("mtmp", mi)], [("mixT", yb_, cc)],
                               out=mixT[yb_][:, cc, :], in0=macc, in1=mtmp[mi], op=ALU.add)
            dma("sp", mixD[:, :, t0:t0 + 512].rearrange("k p t -> p k t"), mixT[yb_],
                [("mixT", yb_, cc) for cc in range(16)], [("mixD", tr)])
        P.barrier()

        AR.off = layer_base
        Wo = AR.alloc([KC, D], BF16)
        wov = w_out[l].rearrange("(k p) n -> p k n", p=128)
        for q in range(4):
            dma("pool", Wo[:, 4 * q:4 * q + 4, :], wov[:, 4 * q:4 * q + 4, :], [], [("Wo", q)])
        Wo_r = [("Wo", q) for q in range(4)]
        mx = [AR.alloc([KC, 512], BF16) for _ in range(2)]
        hxm = [AR.alloc([D], F32) for _ in range(2)]
        hi = 0
        for tr in range(NR):
            t0 = tr * 512
            mb_ = tr % 2
            dma("sp", mx[mb_], mixD[:, :, t0:t0 + 512].rearrange("k p t -> p k t"), [], [("mx", mb_)])
            for ts in range(4):
                r0 = t0 + ts * 128
                hb = hi % 2
                hi += 1
                dma("sp", hxm[hb], src[r0:r0 + 128, :], [], [("hxm", hb)])
                for cb in range(4):
                    bk = next_bank()
                    pairs = [(mx[mb_][:, k, ts * 128:(ts + 1) * 128], Wo[:, k, cb * 512:(cb + 1) * 512]) for k in range(KC)]
                    mm_group(ps[bk][:, :], pairs, ("ps", bk), [("mx", mb_)] + Wo_r)
                    OP("dve", "tensor_tensor", [("ps", bk), ("hxm", hb)], [("hxm", hb)],
                       out=hxm[hb][:, cb * 512:(cb + 1) * 512], in0=hxm[hb][:, cb * 512:(cb + 1) * 512],
                       in1=ps[bk][:, :], op=ALU.add)
                dma("sp", hres[r0:r0 + 128, :], hxm[hb], [("hxm", hb)], [("hres", r0)])
        P.barrier()

        wgv = w_fg[l].rearrange("(k p) n -> p k n", p=128)
        wuv = w_fu[l].rearrange("(k p) n -> p k n", p=128)
        wdv = w_fd[l].rearrange("(f p) n -> p f n", p=128)
        for g in range(S // TGF):
            tok0 = g * TGF
            AR.off = layer_base
            aT = AR.alloc([FC, TGF], BF16)
            mark = AR.off
            uT2 = AR.alloc([KC, TGF], BF16)
            mark2 = AR.off
            norm_transpose(hres, tok0, TGF, 2 * l + 1, uT2, "uT2")
            P.barrier()
            AR.off = mark2
            wg = [AR.alloc([KC, 256], BF16) for _ in range(2)]
            wu = [AR.alloc([KC, 256], BF16) for _ in range(2)]
            sg = [AR.alloc([512], F32) for _ in range(2)]
            u_reads = [("uT2", i, hf) for i in range(TGF // 128) for hf in range(2)]
            si = 0
            for fb in range(DFF // 256):
                wbuf = fb % 2
                dma("pool", wg[wbuf], wgv[:, :, fb * 256:(fb + 1) * 256], [], [("wg", wbuf)])
                dma("pool", wu[wbuf], wuv[:, :, fb * 256:(fb + 1) * 256], [], [("wu", wbuf)])
                for sub in range(2):
                    f = fb * 2 + sub
                    for r in range(TGF // 512):
                        b1, b2 = next_bank(), next_bank()
                        rs = slice(r * 512, (r + 1) * 512)
                        mm_group(ps[b1][:, :], [(wg[wbuf][:, k, sub * 128:(sub + 1) * 128], uT2[:, k, rs]) for k in range(KC)],
                                 ("ps", b1), [("wg", wbuf)] + u_reads)
                        mm_group(ps[b2][:, :], [(wu[wbuf][:, k, sub * 128:(sub + 1) * 128], uT2[:, k, rs]) for k in range(KC)],
                                 ("ps", b2), [("wu", wbuf)] + u_reads)
                        sj = si % 2
                        si += 1
                        OP("act", "activation", [("ps", b1)], [("sg", sj)], out=sg[sj], in_=ps[b1][:, :], func=AF.Silu)
                        OP("dve", "tensor_tensor", [("ps", b2), ("sg", sj)], [("aT", f, r)], out=aT[:, f, rs],
                           in0=sg[sj], in1=ps[b2][:, :], op=ALU.mult)
            P.barrier()
            AR.off = mark
            wd = [AR.alloc([FC, 256], BF16) for _ in range(2)]
            hxs = [AR.alloc([256], F32) for _ in range(3)]
            a_reads = [("aT", f, r) for f in range(FC) for r in range(TGF // 512)]
            hi = 0
            for cb in range(8):
                wbuf = cb % 2
                cs = slice(cb * 256, (cb + 1) * 256)
                for q in range(4):
                    dma("pool", wd[wbuf][:, 11 * q:11 * q + 11, :], wdv[:, 11 * q:11 * q + 11, cs],
                        [], [("wd", wbuf, q)])
                wd_r = [("wd", wbuf, q) for q in range(4)]
                for i in range(TGF // 128):
                    r0 = tok0 + i * 128
                    hb = hi % 3
                    hi += 1
                    dma("sp", hxs[hb], hres[r0:r0 + 128, cs], [("hres", r0)], [("hxs", hb)])
                    bk = next_bank()
                    pairs = [(aT[:, f, i * 128:(i + 1) * 128], wd[wbuf][:, f, :]) for f in range(FC)]
                    mm_group(ps[bk][:, 0:256], pairs, ("ps", bk), a_reads + wd_r)
                    OP("dve", "tensor_tensor", [("ps", bk), ("hxs", hb)], [("hxs", hb)], out=hxs[hb], in0=hxs[hb],
                       in1=ps[bk][:, 0:256], op=ALU.add)
                    dma("sp", hres[r0:r0 + 128, cs], hxs[hb], [("hxs", hb)], [("hres", r0)])
            P.barrier()

    AR.reset()
    gbcf = AR.alloc([D], F32)
    dma("sp", gbcf, norms[2 * depth:2 * depth + 1, :].partition_broadcast(128), [], ["gbcf"])
    hxf = [AR.alloc([D], F32) for _ in range(2)]
    junkf = AR.alloc([D], BF16)
    stf = [AR.alloc([4], F32) for _ in range(2)]
    final_toks = []
    for i in range(NT):
        b = i % 2
        r0 = i * 128
        dma("sp", hxf[b], hres[r0:r0 + 128, :], [("hres", r0)], [("hxf", b)])
        OP("act", "activation", [("hxf", b)], ["junkf", ("stf", b, 0)], out=junkf, in_=hxf[b], func=AF.Square,
           accum_out=stf[b][:, 0:1])
        OP("act", "activation", [("stf", b, 0), "eps_col"], [("stf", b, 1)], out=stf[b][:, 1:2], in_=stf[b][:, 0:1],
           func=AF.Sqrt, scale=1.0 / D, bias=eps_col)
        OP("dve", "reciprocal", [("stf", b, 1)], [("stf", b, 2)], out=stf[b][:, 2:3], in_=stf[b][:, 1:2])
        OP("dve", "scalar_tensor_tensor", [("hxf", b), ("stf", b, 2), "gbcf"], [("hxf", b)], out=hxf[b], in0=hxf[b],
           scalar=stf[b][:, 2:3], in1=gbcf, op0=ALU.mult, op1=ALU.mult)
        final_toks.append(dma("sp", out[r0:r0 + 128, :], hxf[b], [("hxf", b)], [("out", i)]))
    P.emit(final_toks)
    return nc


_NC_CACHE = {}


def _pack_small(inp, depth):
    f = np.float32
    pvec = np.zeros((depth, 128, NPV), f)
    rvec = np.zeros((depth, NRV), f)
    for l in range(depth):
        pvec[l, :, PV_BG:PV_BG + 64] = inp["b_gate"][l].reshape(64, 128).T
        pvec[l, :, PV_CW:PV_CW + 16] = inp["lru_conv_w"][l].reshape(4, 4, 128).transpose(2, 1, 0).reshape(128, 16)
        pvec[l, :, PV_CB:PV_CB + 4] = inp["lru_conv_b"][l].reshape(4, 128).T
        pvec[l, :, PV_BA:PV_BA + 4] = inp["lru_ba"][l].reshape(4, 128).T
        pvec[l, :, PV_BX:PV_BX + 4] = inp["lru_bx"][l].reshape(4, 128).T
        pvec[l, :, PV_LAM:PV_LAM + 4] = inp["lru_lambda"][l].reshape(4, 128).T
        pvec[l, :, PV_SUB] = inp["diff_subln"][l]
        rvec[l, RV_LQ1:RV_LQ1 + 64] = inp["diff_lq1"][l]
        rvec[l, RV_LK1:RV_LK1 + 64] = inp["diff_lk1"][l]
        rvec[l, RV_LQ2:RV_LQ2 + 64] = inp["diff_lq2"][l]
        rvec[l, RV_LK2:RV_LK2 + 64] = inp["diff_lk2"][l]
        rvec[l, RV_SINK:RV_SINK + 8] = inp["swa_sinks"][l]
        rvec[l, RV_BF:RV_BF + 4] = inp["fox_b_f"][l]
    norms = np.zeros((2 * depth + 1, D), f)
    for l in range(depth):
        norms[2 * l] = inp["norm_mix"][l]
        norms[2 * l + 1] = inp["norm_ffn"][l]
    norms[2 * depth] = inp["norm_final"]
    return pvec, rvec, norms


def run(inputs, S, depth, n_cores=8):
    inp = {k: np.asarray(v, dtype=np.float32) for k, v in inputs.items()}
    B = inp["x"].shape[0]
    key = (S, depth)
    if key not in _NC_CACHE:
        _NC_CACHE[key] = build_program(S, depth)
    nc = _NC_CACHE[key]
    pvec, rvec, norms = _pack_small(inp, depth)
    shared = {
        "w_in": np.ascontiguousarray(inp["w_in"]),
        "w_branch": np.ascontiguousarray(inp["w_branch"].reshape(depth, 4 * BW, D)),
        "w_out": np.ascontiguousarray(inp["w_out"]),
        "w_ffn_gate": np.ascontiguousarray(inp["w_ffn_gate"]),
        "w_ffn_up": np.ascontiguousarray(inp["w_ffn_up"]),
        "w_ffn_down": np.ascontiguousarray(inp["w_ffn_down"]),
        "lru_wa": np.ascontiguousarray(inp["lru_wa"]),
        "lru_wx": np.ascontiguousarray(inp["lru_wx"]),
        "norms": norms, "pvec": pvec, "rvec": rvec,
    }
    active = [0, 1, 4, 5]
    zeros = np.zeros((S, D), np.float32)
    in_maps = []
    for c in range(n_cores):
        m = dict(shared)
        m["x"] = np.ascontiguousarray(inp["x"][active.index(c)]) if (c in active and active.index(c) < B) else zeros
        in_maps.append(m)
    res = run_bass_kernel_spmd(nc, in_maps, core_ids=list(range(n_cores)))
    return np.stack([res.results[active[b]]["out"] for b in range(B)], axis=0)


def kernel(**inputs):
    return run(inputs, 4096, 4)
```

```python
import math
from contextlib import ExitStack

import numpy as np
import concourse.bass as bass
import concourse.mybir as mybir
from concourse.bass_utils import run_bass_kernel_spmd

F32 = mybir.dt.float32
BF16 = mybir.dt.bfloat16
U8 = mybir.dt.uint8
AF = mybir.ActivationFunctionType
ALU = mybir.AluOpType
AX = mybir.AxisListType

D = 2048
KC = D // 128
DFF = 5632
FC = DFF // 128
BW = 512
NCOLS_IN = 13060
RMS_EPS = 1e-6
NEG = -1.0e30
SLOPES_A = [2.0 ** (-8.0 * i / 4) for i in range(1, 5)]
SLOPES_D = [2.0 ** (-8.0 * i / 8) for i in range(1, 9)]
C_AQ, C_AK, C_AV, C_BX, C_BG, C_CQ, C_CK, C_CV, C_CF, C_DQ, C_DK, C_DV, C_GZ = (
    0, 512, 1024, 1536, 2048, 2560, 3072, 3584, 4096, 4100, 4612, 4740, 4868)
PV_BG, PV_CW, PV_CB, PV_BA, PV_BX, PV_LAM, PV_SUB = 0, 64, 80, 84, 88, 92, 96
NPV = 97
RV_LQ1, RV_LK1, RV_LQ2, RV_LK2, RV_SINK, RV_BF = 0, 64, 128, 192, 256, 264
NRV = 268

SAME_ENGINE_SYNC = True


class Prog:
    ENGS = ["pe", "act", "dve", "pool", "sp"]

    def __init__(self, nc, n_sp=40, n_pool=16):
        self.nc = nc
        self.ops = {e: [] for e in self.ENGS}
        self.sem_names = []
        self.eng_sem = {}
        for e in self.ENGS:
            self.eng_sem[e] = self._new_sem("c_" + e)
        self.count = {e: 0 for e in self.ENGS}
        self.dma_slots = {"sp": [self._new_sem(f"d_sp{i}") for i in range(n_sp)],
                          "pool": [self._new_sem(f"d_pl{i}") for i in range(n_pool)]}
        self.dma_next = {"sp": 0, "pool": 0}
        self.sem_val = {}
        self.last_write = {}
        self.readers = {}

    def _new_sem(self, name):
        self.sem_names.append(name)
        return len(self.sem_names) - 1

    def add(self, eng, fn, reads=(), writes=(), dma=False):
        deps = {}

        def dep(tok):
            s, v = tok
            if deps.get(s, 0) < v:
                deps[s] = v
        for r in reads:
            t = self.last_write.get(r)
            if t is not None:
                dep(t)
        for w in writes:
            t = self.last_write.get(w)
            if t is not None:
                dep(t)
            for s, v in self.readers.get(w, {}).items():
                dep((s, v))
        if dma:
            slots = self.dma_slots[eng]
            j = self.dma_next[eng]
            self.dma_next[eng] = (j + 1) % len(slots)
            sem = slots[j]
            prev = self.sem_val.get(sem, 0)
            if prev:
                dep((sem, prev))
            tok = (sem, prev + 16)
            inc = 16
        else:
            sem = self.eng_sem[eng]
            self.count[eng] += 1
            tok = (sem, self.count[eng])
            inc = 1
        self.sem_val[sem] = tok[1]
        for r in reads:
            d = self.readers.setdefault(r, {})
            if d.get(tok[0], 0) < tok[1]:
                d[tok[0]] = tok[1]
        for w in writes:
            self.last_write[w] = tok
            self.readers[w] = {}
        self.ops[eng].append((fn, deps, tok, inc))
        return tok

    def wait_all(self, eng, toks):
        deps = {}
        for s, v in toks:
            if deps.get(s, 0) < v:
                deps[s] = v
        self.ops[eng].append((None, deps, None, 0))

    def barrier(self):
        frontier = [(s, v) for s, v in self.sem_val.items()]
        for e in self.ENGS:
            self.wait_all(e, frontier)

    def emit(self, final_toks):
        nc = self.nc
        with ExitStack() as st:
            sems = [st.enter_context(nc.semaphore(n)) for n in self.sem_names]
            self.wait_all("sp", final_toks)
            block = st.enter_context(nc.Block())

            def run(engname):
                def body(eng):
                    known = {}
                    own = self.eng_sem[engname]
                    for fn, deps, tok, inc in self.ops[engname]:
                        for s in sorted(deps):
                            v = deps[s]
                            if s == own and (engname == "pe" or not SAME_ENGINE_SYNC):
                                continue
                            if known.get(s, 0) >= v:
                                continue
                            eng.wait_ge(sems[s], v)
                            known[s] = v
                        if fn is not None:
                            inst = fn(eng)
                            inst.then_inc(sems[tok[0]], inc)
                return body
            block.tensor(run("pe"))
            block.scalar(run("act"))
            block.vector(run("dve"))
            block.gpsimd(run("pool"))
            block.sync(run("sp"))


class Arena:
    def __init__(self, nc, nbytes):
        self.t = nc.alloc_sbuf_tensor("arena", [128, nbytes], U8)
        self.ap = self.t.ap() if hasattr(self.t, "ap") else self.t[:]
        self.nbytes = nbytes
        self.off = 0
        self.base = 0

    def set_base(self):
        self.base = self.off

    def reset(self):
        self.off = self.base

    def alloc(self, shape, dt, parts=128):
        esz = 4 if dt == F32 else 2
        n = int(np.prod(shape)) * esz
        assert self.off + n <= self.nbytes, f"SBUF arena overflow {self.off}+{n}>{self.nbytes}"
        a = self.ap[0:parts, self.off:self.off + n].bitcast(dt)
        self.off += (n + 63) // 64 * 64
        if len(shape) == 2:
            a = a.rearrange("p (a b) -> p a b", b=shape[1])
        elif len(shape) == 3:
            a = a.rearrange("p (a b c) -> p a b c", b=shape[1], c=shape[2])
        return a


def build_program(S, depth):
    assert S % 512 == 0
    NT = S // 128
    NR = S // 512
    TGI = min(2048, S)
    TGF = min(1024, S)
    TL = min(1024, S)
    nc = bass.Bass("TRN2", target_bir_lowering=False)

    def din(name, shape):
        return nc.dram_tensor(name, shape, F32, kind="ExternalInput").ap()
    x_in = din("x", [S, D])
    w_in = din("w_in", [depth, D, NCOLS_IN])
    w_branch = din("w_branch", [depth, 4 * BW, D])
    w_out = din("w_out", [depth, D, D])
    w_fg = din("w_ffn_gate", [depth, D, DFF])
    w_fu = din("w_ffn_up", [depth, D, DFF])
    w_fd = din("w_ffn_down", [depth, DFF, D])
    lru_wa = din("lru_wa", [depth, 8, 64, 64])
    lru_wx = din("lru_wx", [depth, 8, 64, 64])
    norms = din("norms", [2 * depth + 1, D])
    pvec = din("pvec", [depth, 128, NPV])
    rvec = din("rvec", [depth, NRV])
    out = nc.dram_tensor("out", [S, D], F32, kind="ExternalOutput").ap()

    def dscr(name, shape, dt):
        return nc.dram_tensor(name, shape, dt).ap()
    hres = dscr("hres", [S, D], F32)
    qkA = dscr("qkA", [8, 128, S], BF16)
    vA = dscr("vA", [S, 512], BF16)
    qkC = dscr("qkC", [8, 128, S], BF16)
    vC = dscr("vC", [S, 512], BF16)
    cfT = dscr("cfT", [4, S], F32)
    cumD = dscr("cumD", [4, S], F32)
    qDd = dscr("qD", [8, 64, S], BF16)
    kDd = dscr("kD", [2, 64, S], BF16)
    vDd = dscr("vD", [S, 128], BF16)
    xbT = dscr("xbT", [4, 128, S], F32)
    gbT = dscr("gbT", [4, 128, S], F32)
    gatesD = dscr("gatesD", [64, 128, S], BF16)
    yall = dscr("yall", [16, 128, S], BF16)
    mixD = dscr("mixD", [16, 128, S], BF16)

    P = Prog(nc)
    AR = Arena(nc, 207 * 1024)
    ps = [nc.alloc_psum_tensor(f"ps{i}", [128, 512], F32) for i in range(8)]

    bank_ctr = {}

    def next_bank(lo=0, hi=8):
        c = bank_ctr.get((lo, hi), 0)
        bank_ctr[(lo, hi)] = c + 1
        return lo + c % (hi - lo)

    def OP(eng, method, reads, writes, **kw):
        return P.add(eng, lambda e: getattr(e, method)(**kw), reads=reads, writes=writes)

    def dma(q, out_ap, in_ap, reads, writes):
        return P.add(q, lambda e: e.dma_start(out=out_ap, in_=in_ap), reads=reads, writes=writes, dma=True)

    def mm_group(outp, pairs, bank_res, reads, first=True, last=True):
        def fn(e):
            n = len(pairs)
            for i, (l_, r_) in enumerate(pairs):
                ins = e.matmul(outp, lhsT=l_, rhs=r_, start=(first and i == 0), stop=(last and i == n - 1))
            return ins
        return P.add("pe", fn, reads=reads, writes=[bank_res])

    evac_rr = [0]

    def evac_copy(out_ap, in_ap, reads, writes):
        evac_rr[0] ^= 1
        if evac_rr[0]:
            return OP("act", "copy", reads, writes, out=out_ap, in_=in_ap)
        return OP("dve", "tensor_copy", reads, writes, out=out_ap, in_=in_ap)

    ident_f = AR.alloc([128], F32)
    ones_f = AR.alloc([128], F32)
    ident = AR.alloc([128], BF16)
    ones_b = AR.alloc([128], BF16)
    maskneg = AR.alloc([128], F32)
    rowbA = AR.alloc([4, 512], F32)
    colbA = AR.alloc([4, NT + 4], F32)
    mbD = AR.alloc([8, 256], F32)
    iot = AR.alloc([512], F32)
    iotc = AR.alloc([NT + 4], F32)
    tmpc = AR.alloc([256], F32)
    eps_col = AR.alloc([1], F32)
    warm_rhs = AR.alloc([512], BF16)
    AR.set_base()

    OP("pool", "memset", [], ["ones_f"], ap=ones_f, constant=1.0)
    OP("pool", "memset", [], ["eps_col"], ap=eps_col, constant=RMS_EPS)
    OP("pool", "memset", [], ["warm_rhs"], ap=warm_rhs, constant=1.0)
    OP("pool", "affine_select", ["ones_f"], ["ident_f"], out=ident_f, in_=ones_f, pattern=[[-1, 128]],
       compare_op=ALU.is_equal, fill=0.0, base=0, channel_multiplier=1)
    OP("pool", "memset", [], ["tmpc"], ap=tmpc[:, 0:128], constant=0.0)
    OP("pool", "affine_select", ["tmpc"], ["maskneg"], out=maskneg, in_=tmpc[:, 0:128], pattern=[[1, 128]],
       compare_op=ALU.is_ge, fill=NEG, base=0, channel_multiplier=-1)
    OP("dve", "tensor_copy", ["ident_f"], ["ident"], out=ident, in_=ident_f)
    OP("dve", "tensor_copy", ["ones_f"], ["ones_b"], out=ones_b, in_=ones_f)
    OP("pool", "iota", [], ["iot"], out=iot, pattern=[[1, 512]], base=0, channel_multiplier=0,
       allow_small_or_imprecise_dtypes=True)
    OP("pool", "iota", [], ["iotc"], out=iotc, pattern=[[128, NT + 4]], base=-128 * NT, channel_multiplier=1,
       allow_small_or_imprecise_dtypes=True)
    for h in range(4):
        OP("dve", "tensor_scalar", ["iot"], [("rowbA", h)], out=rowbA[:, h, :], in0=iot, scalar1=-SLOPES_A[h],
           scalar2=None, op0=ALU.mult)
        OP("dve", "tensor_scalar", ["iotc"], [("colbA", h)], out=colbA[:, h, :], in0=iotc, scalar1=SLOPES_A[h],
           scalar2=None, op0=ALU.mult)
    OP("pool", "iota", ["maskneg"], ["tmpc"], out=tmpc, pattern=[[1, 256]], base=0, channel_multiplier=-1,
       allow_small_or_imprecise_dtypes=True)
    for h in range(8):
        OP("dve", "tensor_scalar", ["tmpc"], [("mbD", h)], out=mbD[:, h, :], in0=tmpc, scalar1=-SLOPES_D[h],
           scalar2=None, op0=ALU.mult)
        OP("pool", "affine_select", [("mbD", h)], [("mbD", h)], out=mbD[:, h, :], in_=mbD[:, h, :],
           pattern=[[1, 256]], compare_op=ALU.is_ge, fill=NEG, base=0, channel_multiplier=-1)
        OP("pool", "affine_select", [("mbD", h)], [("mbD", h)], out=mbD[:, h, :], in_=mbD[:, h, :],
           pattern=[[-1, 256]], compare_op=ALU.is_ge, fill=NEG, base=127, channel_multiplier=1)
    P.barrier()

    def warm_pe(n=20, lo=4, hi=8):
        bkw = next_bank(lo, hi)

        def fn(e):
            for _ in range(n):
                ins = e.matmul(ps[bkw][:, :], lhsT=ones_b, rhs=warm_rhs, start=True, stop=True)
            return ins
        P.add("pe", fn, reads=["ones_b", "warm_rhs"], writes=[("ps", bkw)])

    def norm_transpose(src, row0, ntok, gain_row, uT, uT_res):
        gbc = AR.alloc([D], F32)
        dma("sp", gbc, norms[gain_row:gain_row + 1, :].partition_broadcast(128), [], ["gbc"])
        hx = [AR.alloc([D], F32) for _ in range(2)]
        un = [AR.alloc([D], BF16) for _ in range(2)]
        junk = AR.alloc([D], BF16)
        st = [AR.alloc([4], F32) for _ in range(2)]
        for i in range(ntok // 128):
            b = i % 2
            r0 = row0 + i * 128
            dma("sp", hx[b], src[r0:r0 + 128, :], [], [("hx", b)])
            OP("act", "activation", [("hx", b)], ["junk", ("st", b, 0)], out=junk, in_=hx[b], func=AF.Square,
               accum_out=st[b][:, 0:1])
            OP("act", "activation", [("st", b, 0), "eps_col"], [("st", b, 1)], out=st[b][:, 1:2], in_=st[b][:, 0:1],
               func=AF.Sqrt, scale=1.0 / D, bias=eps_col)
            OP("dve", "reciprocal", [("st", b, 1)], [("st", b, 2)], out=st[b][:, 2:3], in_=st[b][:, 1:2])
            OP("dve", "scalar_tensor_tensor", [("hx", b), ("st", b, 2), "gbc"], [("un", b)], out=un[b], in0=hx[b],
               scalar=st[b][:, 2:3], in1=gbc, op0=ALU.mult, op1=ALU.mult)
            for half in range(2):
                bk = next_bank()
                pvw = ps[bk][:, :].bitcast(BF16)

                def fn(e, b=b, half=half, pvw=pvw):
                    for j in range(8):
                        k = half * 8 + j
                        ins = e.transpose(out=pvw[:, 128 * j:128 * j + 128], in_=un[b][:, 128 * k:128 * k + 128],
                                          identity=ident)
                    return ins
                P.add("pe", fn, reads=[("un", b), "ident"], writes=[("ps", bk)])
                evac_copy(uT[:, half * 8:half * 8 + 8, i * 128:i * 128 + 128],
                          pvw.rearrange("p (k t) -> p k t", t=128),
                          [("ps", bk)], [(uT_res, i, half)])

    for l in range(depth):
        src = x_in if l == 0 else hres
        lam_init = 0.8 - 0.6 * math.exp(-0.3 * l)
        AR.reset()
        pv = AR.alloc([NPV], F32)
        rv = AR.alloc([NRV], F32)
        lw = AR.alloc([16], F32)
        wab = AR.alloc([4, 128], BF16)
        wxb = AR.alloc([4, 128], BF16)
        prod = AR.alloc([128], F32)
        nsp = AR.alloc([12], F32)
        bfc = AR.alloc([1], F32)
        layer_base = AR.off
        dma("sp", pv, pvec[l], [], ["pv"])
        dma("sp", rv, rvec[l:l + 1, :].partition_broadcast(128), [], ["rv"])
        dma("sp", bfc[0:4, :], rvec[l, RV_BF:RV_BF + 4].rearrange("(a b) -> a b", b=1), [], ["bfc"])
        OP("pool", "memset", [], ["wab"], ap=wab, constant=0.0)
        OP("pool", "memset", [], ["wxb"], ap=wxb, constant=0.0)
        for bi in range(8):
            cc, hb = bi // 2, bi % 2
            dma("pool", wab[64 * hb:64 * hb + 64, cc, 64 * hb:64 * hb + 64], lru_wa[l, bi], ["wab"], [("wab", bi)])
            dma("pool", wxb[64 * hb:64 * hb + 64, cc, 64 * hb:64 * hb + 64], lru_wx[l, bi], ["wxb"], [("wxb", bi)])
        OP("dve", "tensor_tensor", ["rv"], ["prod0"], out=prod[:, 0:64], in0=rv[:, RV_LQ1:RV_LQ1 + 64],
           in1=rv[:, RV_LK1:RV_LK1 + 64], op=ALU.mult)
        OP("dve", "tensor_tensor", ["rv"], ["prod1"], out=prod[:, 64:128], in0=rv[:, RV_LQ2:RV_LQ2 + 64],
           in1=rv[:, RV_LK2:RV_LK2 + 64], op=ALU.mult)
        OP("dve", "reduce_sum", ["prod0"], [("lw", 2)], out=lw[:, 2:3], in_=prod[:, 0:64], axis=AX.X)
        OP("dve", "reduce_sum", ["prod1"], [("lw", 3)], out=lw[:, 3:4], in_=prod[:, 64:128], axis=AX.X)
        OP("act", "activation", [("lw", 2), ("lw", 3)], [("lw", 4)], out=lw[:, 4:6], in_=lw[:, 2:4], func=AF.Exp)
        OP("dve", "scalar_tensor_tensor", [("lw", 4)], [("lw", 0)], out=lw[:, 0:1], in0=lw[:, 5:6], scalar=-lam_init,
           in1=lw[:, 4:5], op0=ALU.add, op1=ALU.subtract)
        OP("dve", "tensor_scalar", ["pv"], [("lw", 1)], out=lw[:, 1:2], in0=pv[:, PV_SUB:PV_SUB + 1],
           scalar1=1.0 - lam_init, scalar2=None, op0=ALU.mult)
        OP("act", "activation", ["rv"], [("lw", 8)], out=lw[:, 8:16], in_=rv[:, RV_SINK:RV_SINK + 8], func=AF.Exp)
        OP("act", "activation", ["pv"], ["nsp0"], out=nsp[:, 0:4], in_=pv[:, PV_LAM:PV_LAM + 4], func=AF.Exp, scale=-1.0)
        OP("dve", "tensor_scalar", ["nsp0"], ["nsp0"], out=nsp[:, 0:4], in0=nsp[:, 0:4], scalar1=1.0, scalar2=None,
           op0=ALU.add)
        OP("act", "activation", ["nsp0"], ["nsp0"], out=nsp[:, 0:4], in_=nsp[:, 0:4], func=AF.Ln)
        OP("dve", "tensor_scalar", ["nsp0"], ["nsp8"], out=nsp[:, 4:8], in0=nsp[:, 0:4], scalar1=-8.0, scalar2=None,
           op0=ALU.mult)
        OP("dve", "tensor_scalar", ["nsp0"], ["nsp16"], out=nsp[:, 8:12], in0=nsp[:, 0:4], scalar1=-16.0, scalar2=None,
           op0=ALU.mult)
        P.barrier()

        blocks = [("aq", C_AQ, 512), ("ak", C_AK, 512), ("av", C_AV, 512), ("bx", C_BX, 512), ("bg", C_BG, 512),
                  ("cq", C_CQ, 512), ("ck", C_CK, 512), ("cvf", C_CV, 516), ("dq", C_DQ, 512), ("dkv", C_DK, 256)]
        blocks += [("gz", C_GZ + 512 * i, 512) for i in range(16)]
        wv_in = w_in[l].rearrange("(k p) n -> p k n", p=128)
        for g in range(S // TGI):
            AR.off = layer_base
            uT = AR.alloc([KC, TGI], BF16)
            mark = AR.off
            norm_transpose(src, g * TGI, TGI, 2 * l, uT, "uT")
            P.barrier()
            AR.off = mark
            wblk = [AR.alloc([KC, 516], BF16) for _ in range(2)]
            stg = [AR.alloc([TGI], F32) for _ in range(3)]
            stg_i = [0]
            tok0 = g * TGI
            uT_reads = [("uT", i, hf) for i in range(TGI // 128) for hf in range(2)]

            def fm_chunk(wb, wres, c0, m, dst, dst_res, dt, func=None, bias=None, bias_res=None):
                si = stg_i[0] % 3
                stg_i[0] += 1
                sview = stg[si] if dt == F32 else stg[si].bitcast(BF16)[:, 0:TGI]
                for r in range(TGI // 512):
                    bk = next_bank()
                    pairs = [(wb[:, k, c0:c0 + m], uT[:, k, r * 512:(r + 1) * 512]) for k in range(KC)]
                    mm_group(ps[bk][0:m, :], pairs, ("ps", bk), [wres] + uT_reads)
                    o = sview[0:m, r * 512:(r + 1) * 512]
                    if func is not None:
                        OP("act", "activation", [("ps", bk)] + ([bias_res] if bias_res else []), [("stg", si, r)],
                           out=o, in_=ps[bk][0:m, :], func=func, bias=bias)
                    else:
                        evac_copy(o, ps[bk][0:m, :], [("ps", bk)], [("stg", si, r)])
                dma("sp", dst[:, tok0:tok0 + TGI], sview[0:m, :], [("stg", si, r) for r in range(TGI // 512)],
                    [dst_res])

            def tm_block(wb, wres, c0, n, dst, dst_res):
                for i in range(TGI // 128):
                    si = stg_i[0] % 3
                    stg_i[0] += 1
                    sview = stg[si].bitcast(BF16)[:, 0:n]
                    bk = next_bank()
                    pairs = [(uT[:, k, i * 128:(i + 1) * 128], wb[:, k, c0:c0 + n]) for k in range(KC)]
                    mm_group(ps[bk][:, 0:n], pairs, ("ps", bk), [wres] + uT_reads)
                    evac_copy(sview, ps[bk][:, 0:n], [("ps", bk)], [("stg", si, 0)])
                    dma("sp", dst[tok0 + i * 128:tok0 + (i + 1) * 128, :], sview, [("stg", si, 0)], [(dst_res, i)])

            for bi, (name, c0, wcols) in enumerate(blocks):
                wb = wblk[bi % 2]
                wres = ("wblk", bi % 2)
                dma("pool", wb[:, :, 0:wcols], wv_in[:, :, c0:c0 + wcols], [], [wres])
                if name in ("aq", "ak"):
                    base = 0 if name == "aq" else 4
                    for c in range(4):
                        fm_chunk(wb, wres, 128 * c, 128, qkA[base + c], ("qkA", base + c, g), BF16)
                elif name in ("cq", "ck"):
                    base = 0 if name == "cq" else 4
                    for c in range(4):
                        fm_chunk(wb, wres, 128 * c, 128, qkC[base + c], ("qkC", base + c, g), BF16)
                elif name == "av":
                    tm_block(wb, wres, 0, 512, vA, ("vA", g))
                elif name == "cvf":
                    tm_block(wb, wres, 0, 512, vC, ("vC", g))
                    fm_chunk(wb, wres, 512, 4, cfT, ("cfT", g), F32)
                elif name == "bx":
                    for c in range(4):
                        fm_chunk(wb, wres, 128 * c, 128, xbT[c], ("xbT", c, g), F32)
                elif name == "bg":
                    for c in range(4):
                        fm_chunk(wb, wres, 128 * c, 128, gbT[c], ("gbT", c, g), F32)
                elif name == "dq":
                    for h in range(8):
                        fm_chunk(wb, wres, 64 * h, 64, qDd[h], ("qD", h, g), BF16)
                elif name == "dkv":
                    for kv in range(2):
                        fm_chunk(wb, wres, 64 * kv, 64, kDd[kv], ("kD", kv, g), BF16)
                    tm_block(wb, wres, 128, 128, vDd, ("vD", g))
                else:
                    gi = (c0 - C_GZ) // 128
                    for c in range(4):
                        fm_chunk(wb, wres, 128 * c, 128, gatesD[gi + c], ("gates", gi + c, g), BF16,
                                 func=AF.Sigmoid, bias=pv[:, PV_BG + gi + c:PV_BG + gi + c + 1], bias_res="pv")
            P.barrier()

        AR.off = layer_base
        cf_sb = AR.alloc([S], F32)
        cum_sb = AR.alloc([S], F32)
        ones_row = AR.alloc([S], F32)
        dma("sp", cf_sb[0:4, :], cfT[:, :], [], ["cf_sb"])
        OP("act", "activation", ["cf_sb", "bfc"], ["cf_sb"], out=cf_sb[0:4, :], in_=cf_sb[0:4, :], func=AF.Sigmoid,
           bias=bfc[0:4, :])
        OP("act", "activation", ["cf_sb"], ["cf_sb"], out=cf_sb[0:4, :], in_=cf_sb[0:4, :], func=AF.Ln)
        OP("pool", "memset", [], ["ones_row"], ap=ones_row[0:4, :], constant=1.0)
        OP("dve", "tensor_tensor_scan", ["cf_sb", "ones_row"], ["cum_sb"], out=cum_sb[0:4, :], data0=ones_row[0:4, :],
           data1=cf_sb[0:4, :], initial=0.0, op0=ALU.mult, op1=ALU.add)
        dma("sp", cumD[:, :], cum_sb[0:4, :], ["cum_sb"], ["cumD"])
        P.barrier()

        def attention(kind):
            AR.off = layer_base
            qk = qkA if kind == "A" else qkC
            vsrc = vA if kind == "A" else vC
            NB = 4
            LA = 3
            V = AR.alloc([NT, 512], BF16)
            dma("sp", V, vsrc.rearrange("(c p) n -> p c n", p=128), [], ["V"])
            qT = [AR.alloc([S], BF16) for _ in range(2)]
            kT = [AR.alloc([S], BF16) for _ in range(2)]
            tmp = [AR.alloc([512], F32) for _ in range(NB)]
            eT = [AR.alloc([512], BF16) for _ in range(NB)]
            fin = [AR.alloc([512], F32) for _ in range(6)]
            sqb = AR.alloc([512], BF16)
            yst = [AR.alloc([512], BF16) for _ in range(2)]
            if kind == "C":
                cumT = AR.alloc([NT, 4], F32)
                Rb = [AR.alloc([512], F32) for _ in range(2)]
                rowbC = [AR.alloc([512], F32) for _ in range(2)]
                colbC = [AR.alloc([NT], F32) for _ in range(2)]
                cum4 = AR.alloc([S], F32)
                dma("sp", cum4[0:4, :], cumD[:, :], [], ["cum4"])
                bkc = next_bank(4, 8)

                def fnc(e):
                    for c in range(NT):
                        ins = e.transpose(out=ps[bkc][:, 4 * c:4 * c + 4], in_=cum4[0:4, 128 * c:128 * c + 128],
                                          identity=ident_f[0:4, 0:4])
                    return ins
                P.add("pe", fnc, reads=["cum4", "ident_f"], writes=[("ps", bkc)])
                OP("dve", "tensor_copy", [("ps", bkc)], ["cumT"], out=cumT,
                   in_=ps[bkc][:, 0:4 * NT].rearrange("p (c h) -> p c h", h=4))
            pend = []

            def push(front, back):
                front()
                pend.append(back)
                if len(pend) > LA:
                    pend.pop(0)()

            def finalize(h, tr, banks, ysi):
                t0 = tr * 512
                bo, bn = banks[0]
                OP("act", "activation", [("ps", bn)], [("fin", 0)], out=fin[0], in_=ps[bn][:, :], func=AF.Ln)
                OP("act", "activation", [("fin", 0)], [("fin", 0)], out=fin[0], in_=fin[0], func=AF.Exp, scale=-1.0)
                if kind == "C":
                    OP("dve", "tensor_tensor", [("ps", bo), ("fin", 0)], [("yst", ysi)], out=yst[ysi],
                       in0=ps[bo][:, :], in1=fin[0], op=ALU.mult)
                    dma("sp", yall[8 + h][:, t0:t0 + 512], yst[ysi], [("yst", ysi)], [("yall", 8 + h, tr)])
                    return
                OP("dve", "tensor_tensor", [("ps", bo), ("fin", 0)], [("fin", 1)], out=fin[1], in0=ps[bo][:, :],
                   in1=fin[0], op=ALU.mult)
                bo1, bn1 = banks[1]
                OP("act", "activation", [("ps", bn1)], [("fin", 2)], out=fin[2], in_=ps[bn1][:, :], func=AF.Ln)
                OP("act", "activation", [("fin", 2)], [("fin", 2)], out=fin[2], in_=fin[2], func=AF.Exp, scale=-1.0)
                OP("dve", "tensor_tensor", [("ps", bo1), ("fin", 2)], [("fin", 3)], out=fin[3], in0=ps[bo1][:, :],
                   in1=fin[2], op=ALU.mult)
                OP("dve", "scalar_tensor_tensor", [("fin", 3), ("fin", 1), ("lw", 0)], [("fin", 4)], out=fin[4],
                   in0=fin[3], scalar=lw[:, 0:1], in1=fin[1], op0=ALU.mult, op1=ALU.add)
                OP("act", "activation", [("fin", 4)], ["sqb"], out=sqb, in_=fin[4], func=AF.Square)
                bq = next_bank(4, 8)
                mm_group(ps[bq][:, :], [(ones_b, sqb)], ("ps", bq), ["sqb", "ones_b"])
                OP("act", "activation", [("ps", bq), "eps_col"], [("fin", 5)], out=fin[5], in_=ps[bq][:, :],
                   func=AF.Ln, scale=1.0 / 128, bias=eps_col)
                OP("act", "activation", [("fin", 5)], [("fin", 5)], out=fin[5], in_=fin[5], func=AF.Exp, scale=-0.5)
                OP("dve", "scalar_tensor_tensor", [("fin", 4), ("fin", 5), ("lw", 1)], [("yst", ysi)], out=yst[ysi],
                   in0=fin[4], scalar=lw[:, 1:2], in1=fin[5], op0=ALU.mult, op1=ALU.mult)
                dma("sp", yall[h][:, t0:t0 + 512], yst[ysi], [("yst", ysi)], [("yall", h, tr)])

            def make_tile(h, hb, m, tr, sc, nsc, rb, bo, bn, ti, fin_args):
                t0 = tr * 512
                di = sc - 4 * tr
                ta = t0 + 128 * di if di > 0 else t0
                n = t0 + 512 - ta
                off = ta - t0
                if kind == "A":
                    lhsT = kT[hb][64 * m:64 * m + 64, sc * 128:(sc + 1) * 128]
                    rhs = qT[hb][64 * m:64 * m + 64, ta:t0 + 512]
                    scale = 0.125
                    rowb = rowbA[:, h, off:512]
                    rowres = ("rowbA", h)
                    ci = NT + sc - 4 * tr
                    colb = colbA[:, h, ci:ci + 1]
                    colres = ("colbA", h)
                else:
                    lhsT = kT[hb][:, sc * 128:(sc + 1) * 128]
                    rhs = qT[hb][:, ta:t0 + 512]
                    scale = 128.0 ** -0.5
                    rowb = rowbC[rb][:, off:512]
                    rowres = ("rowbC", rb)
                    colb = colbC[rb][:, sc:sc + 1]
                    colres = ("colbC", rb)

                def front():
                    bs = next_bank(4, 8)
                    mm_group(ps[bs][:, 0:n], [(lhsT, rhs)], ("ps", bs), [("qT", hb), ("kT", hb)])
                    OP("dve", "scalar_tensor_tensor", [("ps", bs), rowres], [("tmp", ti)], out=tmp[ti][:, 0:n],
                       in0=ps[bs][:, 0:n], scalar=scale, in1=rowb, op0=ALU.mult, op1=ALU.add)
                    if di >= 0:
                        OP("pool", "tensor_tensor", [("tmp", ti), "maskneg"], [("tmp", ti)],
                           out=tmp[ti][:, 0:128], in0=tmp[ti][:, 0:128], in1=maskneg, op=ALU.add)
                    OP("act", "activation", [("tmp", ti), colres], [("eT", ti)], out=eT[ti][:, 0:n],
                       in_=tmp[ti][:, 0:n], func=AF.Exp, bias=colb)

                def back():
                    first, last = sc == 0, sc == nsc - 1
                    mm_group(ps[bo][:, off:512], [(V[:, sc, 128 * h:128 * h + 128], eT[ti][:, 0:n])],
                             ("ps", bo), [("eT", ti), "V"], first=first, last=last)
                    mm_group(ps[bn][:, off:512], [(ones_b, eT[ti][:, 0:n])],
                             ("ps", bn), [("eT", ti), "ones_b"], first=first, last=last)
                    if fin_args is not None:
                        finalize(*fin_args)
                return front, back

            it = 0
            yi = 0
            ri = 0
            for h in range(4):
                hb = h % 2
                dma("sp", qT[hb], qk[h], [], [("qT", hb)])
                dma("sp", kT[hb], qk[4 + h], [], [("kT", hb)])
                warm_pe()
                for tr in range(NR):
                    t0 = tr * 512
                    nsc = 4 * tr + 4
                    rb = 0
                    if kind == "C":
                        rb = ri % 2
                        dma("sp", Rb[rb], cumD[h:h + 1, t0:t0 + 512].partition_broadcast(128), [], [("Rb", rb)])
                        OP("dve", "tensor_scalar", [("Rb", rb)], [("rowbC", rb)], out=rowbC[rb], in0=Rb[rb],
                           scalar1=Rb[rb][:, 0:1], scalar2=None, op0=ALU.subtract)
                        OP("dve", "tensor_scalar", [("Rb", rb), "cumT"], [("colbC", rb)], out=colbC[rb][:, 0:nsc],
                           in0=cumT[:, 0:nsc, h], scalar1=-1.0, scalar2=Rb[rb][:, 0:1], op0=ALU.mult, op1=ALU.add)
                    if kind == "A":
                        maps = 2
                        banks = [(0, 1), (2, 3)]
                    else:
                        maps = 1
                        banks = [(0, 1)] if ri % 2 == 0 else [(2, 3)]
                    ri += 1
                    ysi = yi % 2
                    yi += 1
                    for m in range(maps):
                        bo, bn = banks[m]
                        for sc in range(nsc):
                            lastt = (m == maps - 1 and sc == nsc - 1)
                            fr, bk_ = make_tile(h, hb, m, tr, sc, nsc, rb, bo, bn, it % NB,
                                                (h, tr, banks, ysi) if lastt else None)
                            it += 1
                            push(fr, bk_)
            while pend:
                pend.pop(0)()
            P.barrier()

        attention("A")
        attention("C")

        AR.off = layer_base
        Vd = AR.alloc([NT, 128], BF16)
        dma("sp", Vd, vDd.rearrange("(c p) n -> p c n", p=128), [], ["Vd"])
        kTd = AR.alloc([2, S], BF16)
        for kv in range(2):
            dma("sp", kTd[0:64, kv, :], kDd[kv], [], [("kTd", kv)])
        qTd = [AR.alloc([S], BF16) for _ in range(2)]
        NBD = 10
        tmpd = [AR.alloc([256], F32) for _ in range(NBD)]
        eTd = [AR.alloc([256], BF16) for _ in range(NBD)]
        find = [AR.alloc([512], F32) for _ in range(2)]
        ystd = [AR.alloc([512], BF16) for _ in range(2)]
        itd = [0]
        pendd = []

        def make_group(h, hb, kv, tr, gi_):
            t0 = tr * 512
            bo, bn = (0, 1) if gi_ % 2 == 0 else (2, 3)
            etiles = {}

            def front():
                for c in range(4 * tr - 1, 4 * tr + 4):
                    if c < 0:
                        continue
                    tca = max(c, 4 * tr)
                    tcb = min(c + 1, 4 * tr + 3)
                    ta, n = tca * 128, (tcb - tca + 1) * 128
                    moff = 0 if tca == c else 128
                    bs = next_bank(4, 8)
                    ti = itd[0] % NBD
                    itd[0] += 1
                    mm_group(ps[bs][:, 0:n], [(kTd[0:64, kv, c * 128:(c + 1) * 128], qTd[hb][0:64, ta:ta + n])],
                             ("ps", bs), [("qTd", hb), ("kTd", kv)])
                    OP("dve", "scalar_tensor_tensor", [("ps", bs), ("mbD", h)], [("tmpd", ti)], out=tmpd[ti][:, 0:n],
                       in0=ps[bs][:, 0:n], scalar=0.125, in1=mbD[:, h, moff:moff + n], op0=ALU.mult, op1=ALU.add)
                    OP("act", "activation", [("tmpd", ti)], [("eTd", ti)], out=eTd[ti][:, 0:n], in_=tmpd[ti][:, 0:n],
                       func=AF.Exp)
                    etiles[c] = (ti, tca)

            def back():
                for bank, is_o in ((bo, True), (bn, False)):
                    for tc in range(4 * tr, 4 * tr + 4):
                        srcs = [c for c in (tc - 1, tc) if c >= 0]
                        for j, c in enumerate(srcs):
                            ti, tca = etiles[c]
                            eo = (tc - tca) * 128
                            lhs = Vd[:, c, 64 * kv:64 * kv + 64] if is_o else ones_b[:, 0:64]
                            col = (tc - 4 * tr) * 128
                            mm_group(ps[bank][0:64, col:col + 128], [(lhs, eTd[ti][:, eo:eo + 128])], ("ps", bank),
                                     [("eTd", ti), "Vd", "ones_b"], first=(j == 0), last=(j == len(srcs) - 1))
                fi = gi_ % 2
                OP("act", "activation", [("ps", bn), ("lw", 8)], [("find", fi)], out=find[fi][0:64, :],
                   in_=ps[bn][0:64, :], func=AF.Ln, bias=lw[0:64, 8 + h:9 + h])
                OP("act", "activation", [("find", fi)], [("find", fi)], out=find[fi][0:64, :], in_=find[fi][0:64, :],
                   func=AF.Exp, scale=-1.0)
                OP("dve", "tensor_tensor", [("ps", bo), ("find", fi)], [("ystd", fi)], out=ystd[fi][0:64, :],
                   in0=ps[bo][0:64, :], in1=find[fi][0:64, :], op=ALU.mult)
                dma("sp", yall[12 + h // 2][64 * (h % 2):64 * (h % 2) + 64, t0:t0 + 512], ystd[fi][0:64, :],
                    [("ystd", fi)], [("yall", 12 + h // 2, tr, h % 2)])
            return front, back

        gi_ = 0
        for h in range(8):
            hb = h % 2
            kv = h // 4
            dma("sp", qTd[hb][0:64, :], qDd[h], [], [("qTd", hb)])
            if h % 2 == 0:
                warm_pe()
            for tr in range(NR):
                fr, bk_ = make_group(h, hb, kv, tr, gi_)
                gi_ += 1
                fr()
                pendd.append(bk_)
                if len(pendd) > 1:
                    pendd.pop(0)()
        while pendd:
            pendd.pop(0)()
        P.barrier()

        AR.off = layer_base
        NLR = S // TL
        NSB = TL // 512
        xb = [AR.alloc([TL + 3], F32) for _ in range(2)]
        gb = [AR.alloc([TL], F32) for _ in range(2)]
        xc_l = [AR.alloc([TL], F32) for _ in range(2)]
        xcb_l = [AR.alloc([TL], BF16) for _ in range(2)]
        rr_l = [AR.alloc([TL], F32) for _ in range(2)]
        ii_l = [AR.alloc([TL], F32) for _ in range(2)]
        aa_l = [AR.alloc([TL], F32) for _ in range(2)]
        a2_l = [AR.alloc([TL], F32) for _ in range(2)]
        hh = [AR.alloc([TL], F32) for _ in range(2)]
        gq_l = [AR.alloc([TL], F32) for _ in range(2)]
        ybs = [AR.alloc([TL], BF16) for _ in range(2)]
        li = 0
        for cc in range(4):
            for rg in range(NLR):
                t0 = rg * TL
                b = li % 2
                hp, hc = hh[li % 2], hh[(li + 1) % 2]
                hres_p, hres_c = ("hh", li % 2), ("hh", (li + 1) % 2)
                li += 1
                xc, xcb, rr, ii, aa, a2, gq = xc_l[b], xcb_l[b], rr_l[b], ii_l[b], aa_l[b], a2_l[b], gq_l[b]
                if rg == 0:
                    OP("pool", "memset", [], [("xb", b, "h")], ap=xb[b][:, 0:3], constant=0.0)
                    dma("sp", xb[b][:, 3:3 + TL], xbT[cc][:, 0:TL], [], [("xb", b)])
                else:
                    dma("sp", xb[b][:, 0:3 + TL], xbT[cc][:, t0 - 3:t0 + TL], [], [("xb", b), ("xb", b, "h")])
                dma("sp", gb[b], gbT[cc][:, t0:t0 + TL], [], [("gb", b)])
                xr = [("xb", b), ("xb", b, "h"), "pv"]
                cwc = PV_CW + 4 * cc
                OP("dve", "tensor_scalar", xr, [("xc", b)], out=xc, in0=xb[b][:, 3:3 + TL], scalar1=pv[:, cwc + 3:cwc + 4],
                   scalar2=pv[:, PV_CB + cc:PV_CB + cc + 1], op0=ALU.mult, op1=ALU.add)
                for j in range(3):
                    OP("dve", "scalar_tensor_tensor", xr + [("xc", b)], [("xc", b)], out=xc, in0=xb[b][:, j:j + TL],
                       scalar=pv[:, cwc + j:cwc + j + 1], in1=xc, op0=ALU.mult, op1=ALU.add)
                OP("act", "copy", [("xc", b)], [("xcb", b)], out=xcb, in_=xc)
                for sb in range(NSB):
                    sl = slice(sb * 512, (sb + 1) * 512)
                    b1, b2 = next_bank(), next_bank()
                    mm_group(ps[b1][:, :], [(wab[:, cc, :], xcb[:, sl])], ("ps", b1),
                             [("xcb", b), ("wab", 2 * cc), ("wab", 2 * cc + 1), "wab"])
                    mm_group(ps[b2][:, :], [(wxb[:, cc, :], xcb[:, sl])], ("ps", b2),
                             [("xcb", b), ("wxb", 2 * cc), ("wxb", 2 * cc + 1), "wxb"])
                    OP("act", "activation", [("ps", b1), "pv"], [("rr", b, sb)], out=rr[:, sl], in_=ps[b1][:, :],
                       func=AF.Sigmoid, bias=pv[:, PV_BA + cc:PV_BA + cc + 1])
                    OP("act", "activation", [("ps", b2), "pv"], [("ii", b, sb)], out=ii[:, sl], in_=ps[b2][:, :],
                       func=AF.Sigmoid, bias=pv[:, PV_BX + cc:PV_BX + cc + 1])
                rrr = [("rr", b, sb) for sb in range(NSB)]
                iir = [("ii", b, sb) for sb in range(NSB)]
                OP("act", "activation", rrr + ["nsp8"], [("aa", b)], out=aa, in_=rr, func=AF.Exp, scale=nsp[:, 4 + cc:5 + cc])
                OP("act", "activation", rrr + ["nsp16"], [("a2", b)], out=a2, in_=rr, func=AF.Exp, scale=nsp[:, 8 + cc:9 + cc])
                OP("pool", "tensor_scalar", [("a2", b)], [("a2", b)], out=a2, in0=a2, scalar1=-1.0, scalar2=1.0, op0=ALU.mult,
                   op1=ALU.add)
                OP("act", "activation", [("a2", b)], [("a2", b)], out=a2, in_=a2, func=AF.Sqrt)
                OP("pool", "tensor_tensor", iir + [("xc", b)], iir, out=ii, in0=ii, in1=xc, op=ALU.mult)
                OP("pool", "tensor_tensor", iir + [("a2", b)], iir, out=ii, in0=ii, in1=a2, op=ALU.mult)
                if rg == 0:
                    OP("dve", "tensor_tensor_scan", iir + [("aa", b)], [hres_c], out=hc, data0=aa, data1=ii, initial=0.0,
                       op0=ALU.mult, op1=ALU.add)
                else:
                    OP("dve", "tensor_tensor_scan", iir + [("aa", b), hres_p], [hres_c], out=hc, data0=aa, data1=ii,
                       initial=hp[:, TL - 1:TL], op0=ALU.mult, op1=ALU.add)
                OP("act", "activation", [("gb", b)], [("gq", b)], out=gq, in_=gb[b], func=AF.Square)
                OP("pool", "tensor_scalar", [("gq", b)], [("gq", b)], out=gq, in0=gq, scalar1=0.044715, scalar2=1.0,
                   op0=ALU.mult, op1=ALU.add)
                OP("pool", "tensor_tensor", [("gq", b), ("gb", b)], [("gq", b)], out=gq, in0=gq, in1=gb[b], op=ALU.mult)
                OP("act", "activation", [("gq", b)], [("gq", b)], out=gq, in_=gq, func=AF.Sigmoid, scale=1.5957691216057308)
                OP("pool", "tensor_tensor", [("gq", b), ("gb", b)], [("gq", b)], out=gq, in0=gq, in1=gb[b], op=ALU.mult)
                OP("dve", "tensor_tensor", [("gq", b), hres_c], [("ybs", b)], out=ybs[b], in0=gq, in1=hc, op=ALU.mult)
                dma("sp", yall[4 + cc][:, t0:t0 + TL], ybs[b], [("ybs", b)], [("yall", 4 + cc, rg)])
        P.barrier()

        AR.off = layer_base
        Wb = AR.alloc([16, D], BF16)
        wbv = w_branch[l].rearrange("(k p) n -> p k n", p=128)
        for q in range(4):
            dma("pool", Wb[:, 4 * q:4 * q + 4, :], wbv[:, 4 * q:4 * q + 4, :], [], [("Wb", q)])
        Wb_r = [("Wb", q) for q in range(4)]
        ysb = [AR.alloc([16, 512], BF16) for _ in range(2)]
        gsb = [AR.alloc([4, 512], BF16) for _ in range(2)]
        mixT = [AR.alloc([KC, 512], BF16) for _ in range(2)]
        mtmp = [AR.alloc([512], F32) for _ in range(3)]
        macc = AR.alloc([512], F32)
        gview = gatesD.rearrange("(n c) p s -> c p n s", n=4)
        gi = 0
        for tr in range(NR):
            t0 = tr * 512
            yb_ = tr % 2
            dma("sp", ysb[yb_], yall[:, :, t0:t0 + 512].rearrange("k p t -> p k t"), [], [("ysb", yb_)])
            warm_pe(lo=0, hi=8)
            for cc in range(16):
                gbf = gi % 2
                gi += 1
                dma("sp", gsb[gbf], gview[cc][:, :, t0:t0 + 512], [], [("gsb", gbf)])
                for n in range(4):
                    bk = next_bank()
                    pairs = [(Wb[:, 4 * n + k, cc * 128:(cc + 1) * 128], ysb[yb_][:, 4 * n + k, :]) for k in range(4)]
                    mm_group(ps[bk][:, :], pairs, ("ps", bk), [("ysb", yb_)] + Wb_r)
                    if n == 0:
                        OP("dve", "tensor_tensor", [("ps", bk), ("gsb", gbf)], ["macc"], out=macc, in0=ps[bk][:, :],
                           in1=gsb[gbf][:, 0, :], op=ALU.mult)
                    else:
                        mi = n - 1
                        OP("dve", "tensor_tensor", [("ps", bk), ("gsb", gbf)], [("mtmp", mi)], out=mtmp[mi],
                           in0=ps[bk][:, :], in1=gsb[gbf][:, n, :], op=ALU.mult)
                        if n < 3:
                            OP("pool", "tensor_tensor", ["macc", ("mtmp", mi)], ["macc"], out=macc, in0=macc,
                               in1=mtmp[mi], op=ALU.add)
                        else:
                            OP("pool", "tensor_tensor", ["macc", ("mtmp", mi)], [("mixT", yb_, cc)],
                               out=mixT[yb_][:, cc, :], in0=macc, in1=mtmp[mi], op=ALU.add)
            dma("sp", mixD[:, :, t0:t0 + 512].rearrange("k p t -> p k t"), mixT[yb_],
                [("mixT", yb_, cc) for cc in range(16)], [("mixD", tr)])
        P.barrier()

        AR.off = layer_base
        Wo = AR.alloc([KC, D], BF16)
        wov = w_out[l].rearrange("(k p) n -> p k n", p=128)
        for q in range(4):
            dma("pool", Wo[:, 4 * q:4 * q + 4, :], wov[:, 4 * q:4 * q + 4, :], [], [("Wo", q)])
        Wo_r = [("Wo", q) for q in range(4)]
        mx = [AR.alloc([KC, 512], BF16) for _ in range(2)]
        hxm = [AR.alloc([D], F32) for _ in range(2)]
        hi = 0
        for tr in range(NR):
            t0 = tr * 512
            mb_ = tr % 2
            dma("sp", mx[mb_], mixD[:, :, t0:t0 + 512].rearrange("k p t -> p k t"), [], [("mx", mb_)])
            if tr == 0:
                warm_pe(lo=0, hi=8)
            for ts in range(4):
                r0 = t0 + ts * 128
                hb = hi % 2
                hi += 1
                dma("sp", hxm[hb], src[r0:r0 + 128, :], [], [("hxm", hb)])
                for cb in range(4):
                    bk = next_bank()
                    pairs = [(mx[mb_][:, k, ts * 128:(ts + 1) * 128], Wo[:, k, cb * 512:(cb + 1) * 512]) for k in range(KC)]
                    mm_group(ps[bk][:, :], pairs, ("ps", bk), [("mx", mb_)] + Wo_r)
                    OP("dve", "tensor_tensor", [("ps", bk), ("hxm", hb)], [("hxm", hb)],
                       out=hxm[hb][:, cb * 512:(cb + 1) * 512], in0=hxm[hb][:, cb * 512:(cb + 1) * 512],
                       in1=ps[bk][:, :], op=ALU.add)
                dma("sp", hres[r0:r0 + 128, :], hxm[hb], [("hxm", hb)], [("hres", r0)])
        P.barrier()

        wgv = w_fg[l].rearrange("(k p) n -> p k n", p=128)
        wuv = w_fu[l].rearrange("(k p) n -> p k n", p=128)
        wdv = w_fd[l].rearrange("(f p) n -> p f n", p=128)
        for g in range(S // TGF):
            tok0 = g * TGF
            AR.off = layer_base
            aT = AR.alloc([FC, TGF], BF16)
            mark = AR.off
            uT2 = AR.alloc([KC, TGF], BF16)
            mark2 = AR.off
            norm_transpose(hres, tok0, TGF, 2 * l + 1, uT2, "uT2")
            P.barrier()
            AR.off = mark2
            wg = [AR.alloc([KC, 256], BF16) for _ in range(2)]
            wu = [AR.alloc([KC, 256], BF16) for _ in range(2)]
            sg = [AR.alloc([512], F32) for _ in range(2)]
            u_reads = [("uT2", i, hf) for i in range(TGF // 128) for hf in range(2)]
            si = 0
            for fb in range(DFF // 256):
                wbuf = fb % 2
                dma("pool", wg[wbuf], wgv[:, :, fb * 256:(fb + 1) * 256], [], [("wg", wbuf)])
                dma("pool", wu[wbuf], wuv[:, :, fb * 256:(fb + 1) * 256], [], [("wu", wbuf)])
                for sub in range(2):
                    f = fb * 2 + sub
                    for r in range(TGF // 512):
                        b1, b2 = next_bank(), next_bank()
                        rs = slice(r * 512, (r + 1) * 512)
                        mm_group(ps[b1][:, :], [(wg[wbuf][:, k, sub * 128:(sub + 1) * 128], uT2[:, k, rs]) for k in range(KC)],
                                 ("ps", b1), [("wg", wbuf)] + u_reads)
                        mm_group(ps[b2][:, :], [(wu[wbuf][:, k, sub * 128:(sub + 1) * 128], uT2[:, k, rs]) for k in range(KC)],
                                 ("ps", b2), [("wu", wbuf)] + u_reads)
                        sj = si % 2
                        si += 1
                        OP("act", "activation", [("ps", b1)], [("sg", sj)], out=sg[sj], in_=ps[b1][:, :], func=AF.Silu)
                        OP("dve", "tensor_tensor", [("ps", b2), ("sg", sj)], [("aT", f, r)], out=aT[:, f, rs],
                           in0=sg[sj], in1=ps[b2][:, :], op=ALU.mult)
            P.barrier()
            AR.off = mark
            wd = [AR.alloc([FC, 256], BF16) for _ in range(2)]
            hxs = [AR.alloc([256], F32) for _ in range(3)]
            a_reads = [("aT", f, r) for f in range(FC) for r in range(TGF // 512)]
            hi = 0
            for cb in range(8):
                wbuf = cb % 2
                cs = slice(cb * 256, (cb + 1) * 256)
                for q in range(4):
                    dma("pool", wd[wbuf][:, 11 * q:11 * q + 11, :], wdv[:, 11 * q:11 * q + 11, cs],
                        [], [("wd", wbuf, q)])
                wd_r = [("wd", wbuf, q) for q in range(4)]
                for i in range(TGF // 128):
                    r0 = tok0 + i * 128
                    hb = hi % 3
                    hi += 1
                    dma("sp", hxs[hb], hres[r0:r0 + 128, cs], [("hres", r0)], [("hxs", hb)])
                    bk = next_bank()
                    pairs = [(aT[:, f, i * 128:(i + 1) * 128], wd[wbuf][:, f, :]) for f in range(FC)]
                    mm_group(ps[bk][:, 0:256], pairs, ("ps", bk), a_reads + wd_r)
                    OP("dve", "tensor_tensor", [("ps", bk), ("hxs", hb)], [("hxs", hb)], out=hxs[hb], in0=hxs[hb],
                       in1=ps[bk][:, 0:256], op=ALU.add)
                    dma("sp", hres[r0:r0 + 128, cs], hxs[hb], [("hxs", hb)], [("hres", r0)])
            P.barrier()

    AR.reset()
    gbcf = AR.alloc([D], F32)
    dma("sp", gbcf, norms[2 * depth:2 * depth + 1, :].partition_broadcast(128), [], ["gbcf"])
    hxf = [AR.alloc([D], F32) for _ in range(2)]
    junkf = AR.alloc([D], BF16)
    stf = [AR.alloc([4], F32) for _ in range(2)]
    final_toks = []
    for i in range(NT):
        b = i % 2
        r0 = i * 128
        dma("sp", hxf[b], hres[r0:r0 + 128, :], [("hres", r0)], [("hxf", b)])
        OP("act", "activation", [("hxf", b)], ["junkf", ("stf", b, 0)], out=junkf, in_=hxf[b], func=AF.Square,
           accum_out=stf[b][:, 0:1])
        OP("act", "activation", [("stf", b, 0), "eps_col"], [("stf", b, 1)], out=stf[b][:, 1:2], in_=stf[b][:, 0:1],
           func=AF.Sqrt, scale=1.0 / D, bias=eps_col)
        OP("dve", "reciprocal", [("stf", b, 1)], [("stf", b, 2)], out=stf[b][:, 2:3], in_=stf[b][:, 1:2])
        OP("dve", "scalar_tensor_tensor", [("hxf", b), ("stf", b, 2), "gbcf"], [("hxf", b)], out=hxf[b], in0=hxf[b],
           scalar=stf[b][:, 2:3], in1=gbcf, op0=ALU.mult, op1=ALU.mult)
        final_toks.append(dma("sp", out[r0:r0 + 128, :], hxf[b], [("hxf", b)], [("out", i)]))
    P.emit(final_toks)
    return nc


_NC_CACHE = {}


def _pack_small(inp, depth):
    f = np.float32
    pvec = np.zeros((depth, 128, NPV), f)
    rvec = np.zeros((depth, NRV), f)
    for l in range(depth):
        pvec[l, :, PV_BG:PV_BG + 64] = inp["b_gate"][l].reshape(64, 128).T
        pvec[l, :, PV_CW:PV_CW + 16] = inp["lru_conv_w"][l].reshape(4, 4, 128).transpose(2, 1, 0).reshape(128, 16)
        pvec[l, :, PV_CB:PV_CB + 4] = inp["lru_conv_b"][l].reshape(4, 128).T
        pvec[l, :, PV_BA:PV_BA + 4] = inp["lru_ba"][l].reshape(4, 128).T
        pvec[l, :, PV_BX:PV_BX + 4] = inp["lru_bx"][l].reshape(4, 128).T
        pvec[l, :, PV_LAM:PV_LAM + 4] = inp["lru_lambda"][l].reshape(4, 128).T
        pvec[l, :, PV_SUB] = inp["diff_subln"][l]
        rvec[l, RV_LQ1:RV_LQ1 + 64] = inp["diff_lq1"][l]
        rvec[l, RV_LK1:RV_LK1 + 64] = inp["diff_lk1"][l]
        rvec[l, RV_LQ2:RV_LQ2 + 64] = inp["diff_lq2"][l]
        rvec[l, RV_LK2:RV_LK2 + 64] = inp["diff_lk2"][l]
        rvec[l, RV_SINK:RV_SINK + 8] = inp["swa_sinks"][l]
        rvec[l, RV_BF:RV_BF + 4] = inp["fox_b_f"][l]
    norms = np.zeros((2 * depth + 1, D), f)
    for l in range(depth):
        norms[2 * l] = inp["norm_mix"][l]
        norms[2 * l + 1] = inp["norm_ffn"][l]
    norms[2 * depth] = inp["norm_final"]
    return pvec, rvec, norms


def run(inputs, S, depth, n_cores=8):
    inp = {k: np.asarray(v, dtype=np.float32) for k, v in inputs.items()}
    B = inp["x"].shape[0]
    key = (S, depth)
    if key not in _NC_CACHE:
        _NC_CACHE[key] = build_program(S, depth)
    nc = _NC_CACHE[key]
    pvec, rvec, norms = _pack_small(inp, depth)
    shared = {
        "w_in": np.ascontiguousarray(inp["w_in"]),
        "w_branch": np.ascontiguousarray(inp["w_branch"].reshape(depth, 4 * BW, D)),
        "w_out": np.ascontiguousarray(inp["w_out"]),
        "w_ffn_gate": np.ascontiguousarray(inp["w_ffn_gate"]),
        "w_ffn_up": np.ascontiguousarray(inp["w_ffn_up"]),
        "w_ffn_down": np.ascontiguousarray(inp["w_ffn_down"]),
        "lru_wa": np.ascontiguousarray(inp["lru_wa"]),
        "lru_wx": np.ascontiguousarray(inp["lru_wx"]),
        "norms": norms, "pvec": pvec, "rvec": rvec,
    }
    active = [0, 1, 4, 5]
    zeros = np.zeros((S, D), np.float32)
    in_maps = []
    for c in range(n_cores):
        m = dict(shared)
        m["x"] = np.ascontiguousarray(inp["x"][active.index(c)]) if (c in active and active.index(c) < B) else zeros
        in_maps.append(m)
    res = run_bass_kernel_spmd(nc, in_maps, core_ids=list(range(n_cores)))
    return np.stack([res.results[active[b]]["out"] for b in range(B)], axis=0)


def kernel(**inputs):
    return run(inputs, 4096, 4)
```

```python
import math
from contextlib import ExitStack

import numpy as np
import concourse.bass as bass
import concourse.mybir as mybir
from concourse.bass_utils import run_bass_kernel_spmd

F32 = mybir.dt.float32
BF16 = mybir.dt.bfloat16
U8 = mybir.dt.uint8
AF = mybir.ActivationFunctionType
ALU = mybir.AluOpType
AX = mybir.AxisListType

D = 2048
KC = D // 128
DFF = 5632
FC = DFF // 128
BW = 512
NCOLS_IN = 13060
RMS_EPS = 1e-6
NEG = -1.0e30
SLOPES_A = [2.0 ** (-8.0 * i / 4) for i in range(1, 5)]
SLOPES_D = [2.0 ** (-8.0 * i / 8) for i in range(1, 9)]
C_AQ, C_AK, C_AV, C_BX, C_BG, C_CQ, C_CK, C_CV, C_CF, C_DQ, C_DK, C_DV, C_GZ = (
    0, 512, 1024, 1536, 2048, 2560, 3072, 3584, 4096, 4100, 4612, 4740, 4868)
PV_BG, PV_CW, PV_CB, PV_BA, PV_BX, PV_LAM, PV_SUB = 0, 64, 80, 84, 88, 92, 96
NPV = 97
RV_LQ1, RV_LK1, RV_LQ2, RV_LK2, RV_SINK, RV_BF = 0, 64, 128, 192, 256, 264
NRV = 268

SAME_ENGINE_SYNC = True


class Prog:
    ENGS = ["pe", "act", "dve", "pool", "sp"]

    def __init__(self, nc, n_sp=40, n_pool=16):
        self.nc = nc
        self.ops = {e: [] for e in self.ENGS}
        self.sem_names = []
        self.eng_sem = {}
        for e in self.ENGS:
            self.eng_sem[e] = self._new_sem("c_" + e)
        self.count = {e: 0 for e in self.ENGS}
        self.dma_slots = {"sp": [self._new_sem(f"d_sp{i}") for i in range(n_sp)],
                          "pool": [self._new_sem(f"d_pl{i}") for i in range(n_pool)]}
        self.dma_next = {"sp": 0, "pool": 0}
        self.sem_val = {}
        self.last_write = {}
        self.readers = {}

    def _new_sem(self, name):
        self.sem_names.append(name)
        return len(self.sem_names) - 1

    def add(self, eng, fn, reads=(), writes=(), dma=False):
        deps = {}

        def dep(tok):
            s, v = tok
            if deps.get(s, 0) < v:
                deps[s] = v
        for r in reads:
            t = self.last_write.get(r)
            if t is not None:
                dep(t)
        for w in writes:
            t = self.last_write.get(w)
            if t is not None:
                dep(t)
            for s, v in self.readers.get(w, {}).items():
                dep((s, v))
        if dma:
            slots = self.dma_slots[eng]
            j = self.dma_next[eng]
            self.dma_next[eng] = (j + 1) % len(slots)
            sem = slots[j]
            prev = self.sem_val.get(sem, 0)
            if prev:
                dep((sem, prev))
            tok = (sem, prev + 16)
            inc = 16
        else:
            sem = self.eng_sem[eng]
            self.count[eng] += 1
            tok = (sem, self.count[eng])
            inc = 1
        self.sem_val[sem] = tok[1]
        for r in reads:
            d = self.readers.setdefault(r, {})
            if d.get(tok[0], 0) < tok[1]:
                d[tok[0]] = tok[1]
        for w in writes:
            self.last_write[w] = tok
            self.readers[w] = {}
        self.ops[eng].append((fn, deps, tok, inc))
        return tok

    def wait_all(self, eng, toks):
        deps = {}
        for s, v in toks:
            if deps.get(s, 0) < v:
                deps[s] = v
        self.ops[eng].append((None, deps, None, 0))

    def barrier(self):
        frontier = [(s, v) for s, v in self.sem_val.items()]
        for e in self.ENGS:
            self.wait_all(e, frontier)

    def emit(self, final_toks):
        nc = self.nc
        with ExitStack() as st:
            sems = [st.enter_context(nc.semaphore(n)) for n in self.sem_names]
            self.wait_all("sp", final_toks)
            block = st.enter_context(nc.Block())

            def run(engname):
                def body(eng):
                    known = {}
                    own = self.eng_sem[engname]
                    for fn, deps, tok, inc in self.ops[engname]:
                        for s in sorted(deps):
                            v = deps[s]
                            if s == own and (engname == "pe" or not SAME_ENGINE_SYNC):
                                continue
                            if known.get(s, 0) >= v:
                                continue
                            eng.wait_ge(sems[s], v)
                            known[s] = v
                        if fn is not None:
                            inst = fn(eng)
                            inst.then_inc(sems[tok[0]], inc)
                return body
            block.tensor(run("pe"))
            block.scalar(run("act"))
            block.vector(run("dve"))
            block.gpsimd(run("pool"))
            block.sync(run("sp"))


class Arena:
    def __init__(self, nc, nbytes):
        self.t = nc.alloc_sbuf_tensor("arena", [128, nbytes], U8)
        self.ap = self.t.ap() if hasattr(self.t, "ap") else self.t[:]
        self.nbytes = nbytes
        self.off = 0
        self.base = 0

    def set_base(self):
        self.base = self.off

    def reset(self):
        self.off = self.base

    def alloc(self, shape, dt, parts=128):
        esz = 4 if dt == F32 else 2
        n = int(np.prod(shape)) * esz
        assert self.off + n <= self.nbytes, f"SBUF arena overflow {self.off}+{n}>{self.nbytes}"
        a = self.ap[0:parts, self.off:self.off + n].bitcast(dt)
        self.off += (n + 63) // 64 * 64
        if len(shape) == 2:
            a = a.rearrange("p (a b) -> p a b", b=shape[1])
        elif len(shape) == 3:
            a = a.rearrange("p (a b c) -> p a b c", b=shape[1], c=shape[2])
        return a


def build_program(S, depth):
    assert S % 512 == 0
    NT = S // 128
    NR = S // 512
    TGI = min(2048, S)
    TGF = min(1024, S)
    TL = min(1024, S)
    nc = bass.Bass("TRN2", target_bir_lowering=False)

    def din(name, shape):
        return nc.dram_tensor(name, shape, F32, kind="ExternalInput").ap()
    x_in = din("x", [S, D])
    w_in = din("w_in", [depth, D, NCOLS_IN])
    w_branch = din("w_branch", [depth, 4 * BW, D])
    w_out = din("w_out", [depth, D, D])
    w_fg = din("w_ffn_gate", [depth, D, DFF])
    w_fu = din("w_ffn_up", [depth, D, DFF])
    w_fd = din("w_ffn_down", [depth, DFF, D])
    lru_wa = din("lru_wa", [depth, 8, 64, 64])
    lru_wx = din("lru_wx", [depth, 8, 64, 64])
    norms = din("norms", [2 * depth + 1, D])
    pvec = din("pvec", [depth, 128, NPV])
    rvec = din("rvec", [depth, NRV])
    out = nc.dram_tensor("out", [S, D], F32, kind="ExternalOutput").ap()

    def dscr(name, shape, dt):
        return nc.dram_tensor(name, shape, dt).ap()
    hres = dscr("hres", [S, D], F32)
    qkA = dscr("qkA", [8, 128, S], BF16)
    vA = dscr("vA", [S, 512], BF16)
    qkC = dscr("qkC", [8, 128, S], BF16)
    vC = dscr("vC", [S, 512], BF16)
    cfT = dscr("cfT", [4, S], F32)
    cumD = dscr("cumD", [4, S], F32)
    qDd = dscr("qD", [8, 64, S], BF16)
    kDd = dscr("kD", [2, 64, S], BF16)
    vDd = dscr("vD", [S, 128], BF16)
    xbT = dscr("xbT", [4, 128, S], F32)
    gbT = dscr("gbT", [4, 128, S], F32)
    gatesD = dscr("gatesD", [64, 128, S], BF16)
    yall = dscr("yall", [16, 128, S], BF16)
    mixD = dscr("mixD", [16, 128, S], BF16)

    P = Prog(nc)
    AR = Arena(nc, 207 * 1024)
    ps = [nc.alloc_psum_tensor(f"ps{i}", [128, 512], F32) for i in range(8)]

    bank_ctr = {}

    def next_bank(lo=0, hi=8):
        c = bank_ctr.get((lo, hi), 0)
        bank_ctr[(lo, hi)] = c + 1
        return lo + c % (hi - lo)

    def OP(eng, method, reads, writes, **kw):
        return P.add(eng, lambda e: getattr(e, method)(**kw), reads=reads, writes=writes)

    def dma(q, out_ap, in_ap, reads, writes):
        return P.add(q, lambda e: e.dma_start(out=out_ap, in_=in_ap), reads=reads, writes=writes, dma=True)

    def mm_group(outp, pairs, bank_res, reads, first=True, last=True):
        def fn(e):
            n = len(pairs)
            for i, (l_, r_) in enumerate(pairs):
                ins = e.matmul(outp, lhsT=l_, rhs=r_, start=(first and i == 0), stop=(last and i == n - 1))
            return ins
        return P.add("pe", fn, reads=reads, writes=[bank_res])

    evac_rr = [0]

    def evac_copy(out_ap, in_ap, reads, writes):
        evac_rr[0] ^= 1
        if evac_rr[0]:
            return OP("act", "copy", reads, writes, out=out_ap, in_=in_ap)
        return OP("dve", "tensor_copy", reads, writes, out=out_ap, in_=in_ap)

    ident_f = AR.alloc([128], F32)
    ones_f = AR.alloc([128], F32)
    ident = AR.alloc([128], BF16)
    ones_b = AR.alloc([128], BF16)
    maskneg = AR.alloc([128], F32)
    rowbA = AR.alloc([4, 512], F32)
    colbA = AR.alloc([4, NT + 4], F32)
    mbD = AR.alloc([8, 256], F32)
    iot = AR.alloc([512], F32)
    iotc = AR.alloc([NT + 4], F32)
    tmpc = AR.alloc([256], F32)
    eps_col = AR.alloc([1], F32)
    warm_rhs = AR.alloc([512], BF16)
    AR.set_base()

    OP("pool", "memset", [], ["ones_f"], ap=ones_f, constant=1.0)
    OP("pool", "memset", [], ["eps_col"], ap=eps_col, constant=RMS_EPS)
    OP("pool", "memset", [], ["warm_rhs"], ap=warm_rhs, constant=1.0)
    OP("pool", "affine_select", ["ones_f"], ["ident_f"], out=ident_f, in_=ones_f, pattern=[[-1, 128]],
       compare_op=ALU.is_equal, fill=0.0, base=0, channel_multiplier=1)
    OP("pool", "memset", [], ["tmpc"], ap=tmpc[:, 0:128], constant=0.0)
    OP("pool", "affine_select", ["tmpc"], ["maskneg"], out=maskneg, in_=tmpc[:, 0:128], pattern=[[1, 128]],
       compare_op=ALU.is_ge, fill=NEG, base=0, channel_multiplier=-1)
    OP("dve", "tensor_copy", ["ident_f"], ["ident"], out=ident, in_=ident_f)
    OP("dve", "tensor_copy", ["ones_f"], ["ones_b"], out=ones_b, in_=ones_f)
    OP("pool", "iota", [], ["iot"], out=iot, pattern=[[1, 512]], base=0, channel_multiplier=0,
       allow_small_or_imprecise_dtypes=True)
    OP("pool", "iota", [], ["iotc"], out=iotc, pattern=[[128, NT + 4]], base=-128 * NT, channel_multiplier=1,
       allow_small_or_imprecise_dtypes=True)
    for h in range(4):
        OP("dve", "tensor_scalar", ["iot"], [("rowbA", h)], out=rowbA[:, h, :], in0=iot, scalar1=-SLOPES_A[h],
           scalar2=None, op0=ALU.mult)
        OP("dve", "tensor_scalar", ["iotc"], [("colbA", h)], out=colbA[:, h, :], in0=iotc, scalar1=SLOPES_A[h],
           scalar2=None, op0=ALU.mult)
    OP("pool", "iota", ["maskneg"], ["tmpc"], out=tmpc, pattern=[[1, 256]], base=0, channel_multiplier=-1,
       allow_small_or_imprecise_dtypes=True)
    for h in range(8):
        OP("dve", "tensor_scalar", ["tmpc"], [("mbD", h)], out=mbD[:, h, :], in0=tmpc, scalar1=-SLOPES_D[h],
           scalar2=None, op0=ALU.mult)
        OP("pool", "affine_select", [("mbD", h)], [("mbD", h)], out=mbD[:, h, :], in_=mbD[:, h, :],
           pattern=[[1, 256]], compare_op=ALU.is_ge, fill=NEG, base=0, channel_multiplier=-1)
        OP("pool", "affine_select", [("mbD", h)], [("mbD", h)], out=mbD[:, h, :], in_=mbD[:, h, :],
           pattern=[[-1, 256]], compare_op=ALU.is_ge, fill=NEG, base=127, channel_multiplier=1)
    P.barrier()

    def warm_pe(n=20, lo=4, hi=8):
        bkw = next_bank(lo, hi)

        def fn(e):
            for _ in range(n):
                ins = e.matmul(ps[bkw][:, :], lhsT=ones_b, rhs=warm_rhs, start=True, stop=True)
            return ins
        P.add("pe", fn, reads=["ones_b", "warm_rhs"], writes=[("ps", bkw)])

    def norm_transpose(src, row0, ntok, gain_row, uT, uT_res):
        gbc = AR.alloc([D], F32)
        dma("sp", gbc, norms[gain_row:gain_row + 1, :].partition_broadcast(128), [], ["gbc"])
        hx = [AR.alloc([D], F32) for _ in range(2)]
        un = [AR.alloc([D], BF16) for _ in range(2)]
        junk = AR.alloc([D], BF16)
        st = [AR.alloc([4], F32) for _ in range(2)]
        for i in range(ntok // 128):
            b = i % 2
            r0 = row0 + i * 128
            dma("sp", hx[b], src[r0:r0 + 128, :], [], [("hx", b)])
            OP("act", "activation", [("hx", b)], ["junk", ("st", b, 0)], out=junk, in_=hx[b], func=AF.Square,
               accum_out=st[b][:, 0:1])
            OP("act", "activation", [("st", b, 0), "eps_col"], [("st", b, 1)], out=st[b][:, 1:2], in_=st[b][:, 0:1],
               func=AF.Sqrt, scale=1.0 / D, bias=eps_col)
            OP("dve", "reciprocal", [("st", b, 1)], [("st", b, 2)], out=st[b][:, 2:3], in_=st[b][:, 1:2])
            OP("dve", "scalar_tensor_tensor", [("hx", b), ("st", b, 2), "gbc"], [("un", b)], out=un[b], in0=hx[b],
               scalar=st[b][:, 2:3], in1=gbc, op0=ALU.mult, op1=ALU.mult)
            for half in range(2):
                bk = next_bank()
                pvw = ps[bk][:, :].bitcast(BF16)

                def fn(e, b=b, half=half, pvw=pvw):
                    for j in range(8):
                        k = half * 8 + j
                        ins = e.transpose(out=pvw[:, 128 * j:128 * j + 128], in_=un[b][:, 128 * k:128 * k + 128],
                                          identity=ident)
                    return ins
                P.add("pe", fn, reads=[("un", b), "ident"], writes=[("ps", bk)])
                evac_copy(uT[:, half * 8:half * 8 + 8, i * 128:i * 128 + 128],
                          pvw.rearrange("p (k t) -> p k t", t=128),
                          [("ps", bk)], [(uT_res, i, half)])

    for l in range(depth):
        src = x_in if l == 0 else hres
        lam_init = 0.8 - 0.6 * math.exp(-0.3 * l)
        AR.reset()
        pv = AR.alloc([NPV], F32)
        rv = AR.alloc([NRV], F32)
        lw = AR.alloc([16], F32)
        wab = AR.alloc([4, 128], BF16)
        wxb = AR.alloc([4, 128], BF16)
        prod = AR.alloc([128], F32)
        nsp = AR.alloc([12], F32)
        bfc = AR.alloc([1], F32)
        layer_base = AR.off
        dma("sp", pv, pvec[l], [], ["pv"])
        dma("sp", rv, rvec[l:l + 1, :].partition_broadcast(128), [], ["rv"])
        dma("sp", bfc[0:4, :], rvec[l, RV_BF:RV_BF + 4].rearrange("(a b) -> a b", b=1), [], ["bfc"])
        OP("pool", "memset", [], ["wab"], ap=wab, constant=0.0)
        OP("pool", "memset", [], ["wxb"], ap=wxb, constant=0.0)
        for bi in range(8):
            cc, hb = bi // 2, bi % 2
            dma("pool", wab[64 * hb:64 * hb + 64, cc, 64 * hb:64 * hb + 64], lru_wa[l, bi], ["wab"], [("wab", bi)])
            dma("pool", wxb[64 * hb:64 * hb + 64, cc, 64 * hb:64 * hb + 64], lru_wx[l, bi], ["wxb"], [("wxb", bi)])
        OP("dve", "tensor_tensor", ["rv"], ["prod0"], out=prod[:, 0:64], in0=rv[:, RV_LQ1:RV_LQ1 + 64],
           in1=rv[:, RV_LK1:RV_LK1 + 64], op=ALU.mult)
        OP("dve", "tensor_tensor", ["rv"], ["prod1"], out=prod[:, 64:128], in0=rv[:, RV_LQ2:RV_LQ2 + 64],
           in1=rv[:, RV_LK2:RV_LK2 + 64], op=ALU.mult)
        OP("dve", "reduce_sum", ["prod0"], [("lw", 2)], out=lw[:, 2:3], in_=prod[:, 0:64], axis=AX.X)
        OP("dve", "reduce_sum", ["prod1"], [("lw", 3)], out=lw[:, 3:4], in_=prod[:, 64:128], axis=AX.X)
        OP("act", "activation", [("lw", 2), ("lw", 3)], [("lw", 4)], out=lw[:, 4:6], in_=lw[:, 2:4], func=AF.Exp)
        OP("dve", "scalar_tensor_tensor", [("lw", 4)], [("lw", 0)], out=lw[:, 0:1], in0=lw[:, 5:6], scalar=-lam_init,
           in1=lw[:, 4:5], op0=ALU.add, op1=ALU.subtract)
        OP("dve", "tensor_scalar", ["pv"], [("lw", 1)], out=lw[:, 1:2], in0=pv[:, PV_SUB:PV_SUB + 1],
           scalar1=1.0 - lam_init, scalar2=None, op0=ALU.mult)
        OP("act", "activation", ["rv"], [("lw", 8)], out=lw[:, 8:16], in_=rv[:, RV_SINK:RV_SINK + 8], func=AF.Exp)
        OP("act", "activation", ["pv"], ["nsp0"], out=nsp[:, 0:4], in_=pv[:, PV_LAM:PV_LAM + 4], func=AF.Exp, scale=-1.0)
        OP("dve", "tensor_scalar", ["nsp0"], ["nsp0"], out=nsp[:, 0:4], in0=nsp[:, 0:4], scalar1=1.0, scalar2=None,
           op0=ALU.add)
        OP("act", "activation", ["nsp0"], ["nsp0"], out=nsp[:, 0:4], in_=nsp[:, 0:4], func=AF.Ln)
        OP("dve", "tensor_scalar", ["nsp0"], ["nsp8"], out=nsp[:, 4:8], in0=nsp[:, 0:4], scalar1=-8.0, scalar2=None,
           op0=ALU.mult)
        OP("dve", "tensor_scalar", ["nsp0"], ["nsp16"], out=nsp[:, 8:12], in0=nsp[:, 0:4], scalar1=-16.0, scalar2=None,
           op0=ALU.mult)
        P.barrier()

        blocks = [("aq", C_AQ, 512), ("ak", C_AK, 512), ("av", C_AV, 512), ("bx", C_BX, 512), ("bg", C_BG, 512),
                  ("cq", C_CQ, 512), ("ck", C_CK, 512), ("cvf", C_CV, 516), ("dq", C_DQ, 512), ("dkv", C_DK, 256)]
        blocks += [("gz", C_GZ + 512 * i, 512) for i in range(16)]
        wv_in = w_in[l].rearrange("(k p) n -> p k n", p=128)
        for g in range(S // TGI):
            AR.off = layer_base
            uT = AR.alloc([KC, TGI], BF16)
            mark = AR.off
            norm_transpose(src, g * TGI, TGI, 2 * l, uT, "uT")
            P.barrier()
            AR.off = mark
            wblk = [AR.alloc([KC, 516], BF16) for _ in range(2)]
            stg = [AR.alloc([TGI], F32) for _ in range(3)]
            stg_i = [0]
            tok0 = g * TGI
            uT_reads = [("uT", i, hf) for i in range(TGI // 128) for hf in range(2)]

            def fm_chunk(wb, wres, c0, m, dst, dst_res, dt, func=None, bias=None, bias_res=None):
                si = stg_i[0] % 3
                stg_i[0] += 1
                sview = stg[si] if dt == F32 else stg[si].bitcast(BF16)[:, 0:TGI]
                for r in range(TGI // 512):
                    bk = next_bank()
                    pairs = [(wb[:, k, c0:c0 + m], uT[:, k, r * 512:(r + 1) * 512]) for k in range(KC)]
                    mm_group(ps[bk][0:m, :], pairs, ("ps", bk), [wres] + uT_reads)
                    o = sview[0:m, r * 512:(r + 1) * 512]
                    if func is not None:
                        OP("act", "activation", [("ps", bk)] + ([bias_res] if bias_res else []), [("stg", si, r)],
                           out=o, in_=ps[bk][0:m, :], func=func, bias=bias)
                    else:
                        evac_copy(o, ps[bk][0:m, :], [("ps", bk)], [("stg", si, r)])
                dma("sp", dst[:, tok0:tok0 + TGI], sview[0:m, :], [("stg", si, r) for r in range(TGI // 512)],
                    [dst_res])

            def tm_block(wb, wres, c0, n, dst, dst_res):
                for i in range(TGI // 128):
                    si = stg_i[0] % 3
                    stg_i[0] += 1
                    sview = stg[si].bitcast(BF16)[:, 0:n]
                    bk = next_bank()
                    pairs = [(uT[:, k, i * 128:(i + 1) * 128], wb[:, k, c0:c0 + n]) for k in range(KC)]
                    mm_group(ps[bk][:, 0:n], pairs, ("ps", bk), [wres] + uT_reads)
                    evac_copy(sview, ps[bk][:, 0:n], [("ps", bk)], [("stg", si, 0)])
                    dma("sp", dst[tok0 + i * 128:tok0 + (i + 1) * 128, :], sview, [("stg", si, 0)], [(dst_res, i)])

            for bi, (name, c0, wcols) in enumerate(blocks):
                wb = wblk[bi % 2]
                wres = ("wblk", bi % 2)
                dma("pool", wb[:, :, 0:wcols], wv_in[:, :, c0:c0 + wcols], [], [wres])
                if name in ("aq", "ak"):
                    base = 0 if name == "aq" else 4
                    for c in range(4):
                        fm_chunk(wb, wres, 128 * c, 128, qkA[base + c], ("qkA", base + c, g), BF16)
                elif name in ("cq", "ck"):
                    base = 0 if name == "cq" else 4
                    for c in range(4):
                        fm_chunk(wb, wres, 128 * c, 128, qkC[base + c], ("qkC", base + c, g), BF16)
                elif name == "av":
                    tm_block(wb, wres, 0, 512, vA, ("vA", g))
                elif name == "cvf":
                    tm_block(wb, wres, 0, 512, vC, ("vC", g))
                    fm_chunk(wb, wres, 512, 4, cfT, ("cfT", g), F32)
                elif name == "bx":
                    for c in range(4):
                        fm_chunk(wb, wres, 128 * c, 128, xbT[c], ("xbT", c, g), F32)
                elif name == "bg":
                    for c in range(4):
                        fm_chunk(wb, wres, 128 * c, 128, gbT[c], ("gbT", c, g), F32)
                elif name == "dq":
                    for h in range(8):
                        fm_chunk(wb, wres, 64 * h, 64, qDd[h], ("qD", h, g), BF16)
                elif name == "dkv":
                    for kv in range(2):
                        fm_chunk(wb, wres, 64 * kv, 64, kDd[kv], ("kD", kv, g), BF16)
                    tm_block(wb, wres, 128, 128, vDd, ("vD", g))
                else:
                    gi = (c0 - C_GZ) // 128
                    for c in range(4):
                        fm_chunk(wb, wres, 128 * c, 128, gatesD[gi + c], ("gates", gi + c, g), BF16,
                                 func=AF.Sigmoid, bias=pv[:, PV_BG + gi + c:PV_BG + gi + c + 1], bias_res="pv")
            P.barrier()

        AR.off = layer_base
        cf_sb = AR.alloc([S], F32)
        cum_sb = AR.alloc([S], F32)
        ones_row = AR.alloc([S], F32)
        dma("sp", cf_sb[0:4, :], cfT[:, :], [], ["cf_sb"])
        OP("act", "activation", ["cf_sb", "bfc"], ["cf_sb"], out=cf_sb[0:4, :], in_=cf_sb[0:4, :], func=AF.Sigmoid,
           bias=bfc[0:4, :])
        OP("act", "activation", ["cf_sb"], ["cf_sb"], out=cf_sb[0:4, :], in_=cf_sb[0:4, :], func=AF.Ln)
        OP("pool", "memset", [], ["ones_row"], ap=ones_row[0:4, :], constant=1.0)
        OP("dve", "tensor_tensor_scan", ["cf_sb", "ones_row"], ["cum_sb"], out=cum_sb[0:4, :], data0=ones_row[0:4, :],
           data1=cf_sb[0:4, :], initial=0.0, op0=ALU.mult, op1=ALU.add)
        dma("sp", cumD[:, :], cum_sb[0:4, :], ["cum_sb"], ["cumD"])
        P.barrier()

        def attention(kind):
            AR.off = layer_base
            qk = qkA if kind == "A" else qkC
            vsrc = vA if kind == "A" else vC
            NB = 4
            LA = 3
            V = AR.alloc([NT, 512], BF16)
            dma("sp", V, vsrc.rearrange("(c p) n -> p c n", p=128), [], ["V"])
            qT = [AR.alloc([S], BF16) for _ in range(2)]
            kT = [AR.alloc([S], BF16) for _ in range(2)]
            tmp = [AR.alloc([512], F32) for _ in range(NB)]
            eT = [AR.alloc([512], BF16) for _ in range(NB)]
            fin = [AR.alloc([512], F32) for _ in range(6)]
            sqb = AR.alloc([512], BF16)
            yst = [AR.alloc([512], BF16) for _ in range(2)]
            if kind == "C":
                cumT = AR.alloc([NT, 4], F32)
                Rb = [AR.alloc([512], F32) for _ in range(2)]
                rowbC = [AR.alloc([512], F32) for _ in range(2)]
                colbC = [AR.alloc([NT], F32) for _ in range(2)]
                cum4 = AR.alloc([S], F32)
                dma("sp", cum4[0:4, :], cumD[:, :], [], ["cum4"])
                bkc = next_bank(4, 8)

                def fnc(e):
                    for c in range(NT):
                        ins = e.transpose(out=ps[bkc][:, 4 * c:4 * c + 4], in_=cum4[0:4, 128 * c:128 * c + 128],
                                          identity=ident_f[0:4, 0:4])
                    return ins
                P.add("pe", fnc, reads=["cum4", "ident_f"], writes=[("ps", bkc)])
                OP("dve", "tensor_copy", [("ps", bkc)], ["cumT"], out=cumT,
                   in_=ps[bkc][:, 0:4 * NT].rearrange("p (c h) -> p c h", h=4))
            pend = []

            def push(front, back):
                front()
                pend.append(back)
                if len(pend) > LA:
                    pend.pop(0)()

            def finalize(h, tr, banks, ysi):
                t0 = tr * 512
                bo, bn = banks[0]
                OP("act", "activation", [("ps", bn)], [("fin", 0)], out=fin[0], in_=ps[bn][:, :], func=AF.Ln)
                OP("act", "activation", [("fin", 0)], [("fin", 0)], out=fin[0], in_=fin[0], func=AF.Exp, scale=-1.0)
                if kind == "C":
                    OP("dve", "tensor_tensor", [("ps", bo), ("fin", 0)], [("yst", ysi)], out=yst[ysi],
                       in0=ps[bo][:, :], in1=fin[0], op=ALU.mult)
                    dma("sp", yall[8 + h][:, t0:t0 + 512], yst[ysi], [("yst", ysi)], [("yall", 8 + h, tr)])
                    return
                OP("dve", "tensor_tensor", [("ps", bo), ("fin", 0)], [("fin", 1)], out=fin[1], in0=ps[bo][:, :],
                   in1=fin[0], op=ALU.mult)
                bo1, bn1 = banks[1]
                OP("act", "activation", [("ps", bn1)], [("fin", 2)], out=fin[2], in_=ps[bn1][:, :], func=AF.Ln)
                OP("act", "activation", [("fin", 2)], [("fin", 2)], out=fin[2], in_=fin[2], func=AF.Exp, scale=-1.0)
                OP("dve", "tensor_tensor", [("ps", bo1), ("fin", 2)], [("fin", 3)], out=fin[3], in0=ps[bo1][:, :],
                   in1=fin[2], op=ALU.mult)
                OP("dve", "scalar_tensor_tensor", [("fin", 3), ("fin", 1), ("lw", 0)], [("fin", 4)], out=fin[4],
                   in0=fin[3], scalar=lw[:, 0:1], in1=fin[1], op0=ALU.mult, op1=ALU.add)
                OP("act", "activation", [("fin", 4)], ["sqb"], out=sqb, in_=fin[4], func=AF.Square)
                bq = next_bank(4, 8)
                mm_group(ps[bq][:, :], [(ones_b, sqb)], ("ps", bq), ["sqb", "ones_b"])
                OP("act", "activation", [("ps", bq), "eps_col"], [("fin", 5)], out=fin[5], in_=ps[bq][:, :],
                   func=AF.Ln, scale=1.0 / 128, bias=eps_col)
                OP("act", "activation", [("fin", 5)], [("fin", 5)], out=fin[5], in_=fin[5], func=AF.Exp, scale=-0.5)
                OP("dve", "scalar_tensor_tensor", [("fin", 4), ("fin", 5), ("lw", 1)], [("yst", ysi)], out=yst[ysi],
                   in0=fin[4], scalar=lw[:, 1:2], in1=fin[5], op0=ALU.mult, op1=ALU.mult)
                dma("sp", yall[h][:, t0:t0 + 512], yst[ysi], [("yst", ysi)], [("yall", h, tr)])

            def make_tile(h, hb, m, tr, sc, nsc, rb, bo, bn, ti, fin_args):
                t0 = tr * 512
                di = sc - 4 * tr
                ta = t0 + 128 * di if di > 0 else t0
                n = t0 + 512 - ta
                off = ta - t0
                if kind == "A":
                    lhsT = kT[hb][64 * m:64 * m + 64, sc * 128:(sc + 1) * 128]
                    rhs = qT[hb][64 * m:64 * m + 64, ta:t0 + 512]
                    scale = 0.125
                    rowb = rowbA[:, h, off:512]
                    rowres = ("rowbA", h)
                    ci = NT + sc - 4 * tr
                    colb = colbA[:, h, ci:ci + 1]
                    colres = ("colbA", h)
                else:
                    lhsT = kT[hb][:, sc * 128:(sc + 1) * 128]
                    rhs = qT[hb][:, ta:t0 + 512]
                    scale = 128.0 ** -0.5
                    rowb = rowbC[rb][:, off:512]
                    rowres = ("rowbC", rb)
                    colb = colbC[rb][:, sc:sc + 1]
                    colres = ("colbC", rb)

                def front():
                    bs = next_bank(4, 8)
                    mm_group(ps[bs][:, 0:n], [(lhsT, rhs)], ("ps", bs), [("qT", hb), ("kT", hb)])
                    OP("dve", "scalar_tensor_tensor", [("ps", bs), rowres], [("tmp", ti)], out=tmp[ti][:, 0:n],
                       in0=ps[bs][:, 0:n], scalar=scale, in1=rowb, op0=ALU.mult, op1=ALU.add)
                    if di >= 0:
                        OP("pool", "tensor_tensor", [("tmp", ti), "maskneg"], [("tmp", ti)],
                           out=tmp[ti][:, 0:128], in0=tmp[ti][:, 0:128], in1=maskneg, op=ALU.add)
                    OP("act", "activation", [("tmp", ti), colres], [("eT", ti)], out=eT[ti][:, 0:n],
                       in_=tmp[ti][:, 0:n], func=AF.Exp, bias=colb)

                def back():
                    first, last = sc == 0, sc == nsc - 1
                    mm_group(ps[bo][:, off:512], [(V[:, sc, 128 * h:128 * h + 128], eT[ti][:, 0:n])],
                             ("ps", bo), [("eT", ti), "V"], first=first, last=last)
                    mm_group(ps[bn][:, off:512], [(ones_b, eT[ti][:, 0:n])],
                             ("ps", bn), [("eT", ti), "ones_b"], first=first, last=last)
                    if fin_args is not None:
                        finalize(*fin_args)
                return front, back

            it = 0
            yi = 0
            ri = 0
            for h in range(4):
                hb = h % 2
                dma("sp", qT[hb], qk[h], [], [("qT", hb)])
                dma("sp", kT[hb], qk[4 + h], [], [("kT", hb)])
                warm_pe()
                for tr in range(NR):
                    t0 = tr * 512
                    nsc = 4 * tr + 4
                    rb = 0
                    if kind == "C":
                        rb = ri % 2
                        dma("sp", Rb[rb], cumD[h:h + 1, t0:t0 + 512].partition_broadcast(128), [], [("Rb", rb)])
                        OP("dve", "tensor_scalar", [("Rb", rb)], [("rowbC", rb)], out=rowbC[rb], in0=Rb[rb],
                           scalar1=Rb[rb][:, 0:1], scalar2=None, op0=ALU.subtract)
                        OP("dve", "tensor_scalar", [("Rb", rb), "cumT"], [("colbC", rb)], out=colbC[rb][:, 0:nsc],
                           in0=cumT[:, 0:nsc, h], scalar1=-1.0, scalar2=Rb[rb][:, 0:1], op0=ALU.mult, op1=ALU.add)
                    if kind == "A":
                        maps = 2
                        banks = [(0, 1), (2, 3)]
                    else:
                        maps = 1
                        banks = [(0, 1)] if ri % 2 == 0 else [(2, 3)]
                    ri += 1
                    ysi = yi % 2
                    yi += 1
                    for m in range(maps):
                        bo, bn = banks[m]
                        for sc in range(nsc):
                            lastt = (m == maps - 1 and sc == nsc - 1)
                            fr, bk_ = make_tile(h, hb, m, tr, sc, nsc, rb, bo, bn, it % NB,
                                                (h, tr, banks, ysi) if lastt else None)
                            it += 1
                            push(fr, bk_)
            while pend:
                pend.pop(0)()
            P.barrier()

        attention("A")
        attention("C")

        AR.off = layer_base
        Vd = AR.alloc([NT, 128], BF16)
        dma("sp", Vd, vDd.rearrange("(c p) n -> p c n", p=128), [], ["Vd"])
        kTd = AR.alloc([2, S], BF16)
        for kv in range(2):
            dma("sp", kTd[0:64, kv, :], kDd[kv], [], [("kTd", kv)])
        qTd = [AR.alloc([S], BF16) for _ in range(2)]
        NBD = 10
        tmpd = [AR.alloc([256], F32) for _ in range(NBD)]
        eTd = [AR.alloc([256], BF16) for _ in range(NBD)]
        find = [AR.alloc([512], F32) for _ in range(2)]
        ystd = [AR.alloc([512], BF16) for _ in range(2)]
        itd = [0]
        pendd = []

        def make_group(h, hb, kv, tr, gi_):
            t0 = tr * 512
            bo, bn = (0, 1) if gi_ % 2 == 0 else (2, 3)
            etiles = {}

            def front():
                for c in range(4 * tr - 1, 4 * tr + 4):
                    if c < 0:
                        continue
                    tca = max(c, 4 * tr)
                    tcb = min(c + 1, 4 * tr + 3)
                    ta, n = tca * 128, (tcb - tca + 1) * 128
                    moff = 0 if tca == c else 128
                    bs = next_bank(4, 8)
                    ti = itd[0] % NBD
                    itd[0] += 1
                    mm_group(ps[bs][:, 0:n], [(kTd[0:64, kv, c * 128:(c + 1) * 128], qTd[hb][0:64, ta:ta + n])],
                             ("ps", bs), [("qTd", hb), ("kTd", kv)])
                    OP("dve", "scalar_tensor_tensor", [("ps", bs), ("mbD", h)], [("tmpd", ti)], out=tmpd[ti][:, 0:n],
                       in0=ps[bs][:, 0:n], scalar=0.125, in1=mbD[:, h, moff:moff + n], op0=ALU.mult, op1=ALU.add)
                    OP("act", "activation", [("tmpd", ti)], [("eTd", ti)], out=eTd[ti][:, 0:n], in_=tmpd[ti][:, 0:n],
                       func=AF.Exp)
                    etiles[c] = (ti, tca)

            def back():
                for bank, is_o in ((bo, True), (bn, False)):
                    for tc in range(4 * tr, 4 * tr + 4):
                        srcs = [c for c in (tc - 1, tc) if c >= 0]
                        for j, c in enumerate(srcs):
                            ti, tca = etiles[c]
                            eo = (tc - tca) * 128
                            lhs = Vd[:, c, 64 * kv:64 * kv + 64] if is_o else ones_b[:, 0:64]
                            col = (tc - 4 * tr) * 128
                            mm_group(ps[bank][0:64, col:col + 128], [(lhs, eTd[ti][:, eo:eo + 128])], ("ps", bank),
                                     [("eTd", ti), "Vd", "ones_b"], first=(j == 0), last=(j == len(srcs) - 1))
                fi = gi_ % 2
                OP("act", "activation", [("ps", bn), ("lw", 8)], [("find", fi)], out=find[fi][0:64, :],
                   in_=ps[bn][0:64, :], func=AF.Ln, bias=lw[0:64, 8 + h:9 + h])
                OP("act", "activation", [("find", fi)], [("find", fi)], out=find[fi][0:64, :], in_=find[fi][0:64, :],
                   func=AF.Exp, scale=-1.0)
                OP("dve", "tensor_tensor", [("ps", bo), ("find", fi)], [("ystd", fi)], out=ystd[fi][0:64, :],
                   in0=ps[bo][0:64, :], in1=find[fi][0:64, :], op=ALU.mult)
                dma("sp", yall[12 + h // 2][64 * (h % 2):64 * (h % 2) + 64, t0:t0 + 512], ystd[fi][0:64, :],
                    [("ystd", fi)], [("yall", 12 + h // 2, tr, h % 2)])
            return front, back

        gi_ = 0
        for h in range(8):
            hb = h % 2
            kv = h // 4
            dma("sp", qTd[hb][0:64, :], qDd[h], [], [("qTd", hb)])
            if h % 2 == 0:
                warm_pe()
            for tr in range(NR):
                fr, bk_ = make_group(h, hb, kv, tr, gi_)
                gi_ += 1
                fr()
                pendd.append(bk_)
                if len(pendd) > 1:
                    pendd.pop(0)()
        while pendd:
            pendd.pop(0)()
        P.barrier()

        AR.off = layer_base
        NLR = S // TL
        NSB = TL // 512
        xb = [AR.alloc([TL + 3], F32) for _ in range(2)]
        gb = [AR.alloc([TL], F32) for _ in range(2)]
        xc_l = [AR.alloc([TL], F32) for _ in range(2)]
        xcb_l = [AR.alloc([TL], BF16) for _ in range(2)]
        rr_l = [AR.alloc([TL], F32) for _ in range(2)]
        ii_l = [AR.alloc([TL], F32) for _ in range(2)]
        aa_l = [AR.alloc([TL], F32) for _ in range(2)]
        a2_l = [AR.alloc([TL], F32) for _ in range(2)]
        hh = [AR.alloc([TL], F32) for _ in range(2)]
        gq_l = [AR.alloc([TL], F32) for _ in range(2)]
        ybs = [AR.alloc([TL], BF16) for _ in range(2)]
        li = 0
        for cc in range(4):
            for rg in range(NLR):
                t0 = rg * TL
                b = li % 2
                hp, hc = hh[li % 2], hh[(li + 1) % 2]
                hres_p, hres_c = ("hh", li % 2), ("hh", (li + 1) % 2)
                li += 1
                xc, xcb, rr, ii, aa, a2, gq = xc_l[b], xcb_l[b], rr_l[b], ii_l[b], aa_l[b], a2_l[b], gq_l[b]
                if rg == 0:
                    OP("pool", "memset", [], [("xb", b, "h")], ap=xb[b][:, 0:3], constant=0.0)
                    dma("sp", xb[b][:, 3:3 + TL], xbT[cc][:, 0:TL], [], [("xb", b)])
                else:
                    dma("sp", xb[b][:, 0:3 + TL], xbT[cc][:, t0 - 3:t0 + TL], [], [("xb", b), ("xb", b, "h")])
                dma("sp", gb[b], gbT[cc][:, t0:t0 + TL], [], [("gb", b)])
                xr = [("xb", b), ("xb", b, "h"), "pv"]
                cwc = PV_CW + 4 * cc
                OP("dve", "tensor_scalar", xr, [("xc", b)], out=xc, in0=xb[b][:, 3:3 + TL], scalar1=pv[:, cwc + 3:cwc + 4],
                   scalar2=pv[:, PV_CB + cc:PV_CB + cc + 1], op0=ALU.mult, op1=ALU.add)
                for j in range(3):
                    OP("dve", "scalar_tensor_tensor", xr + [("xc", b)], [("xc", b)], out=xc, in0=xb[b][:, j:j + TL],
                       scalar=pv[:, cwc + j:cwc + j + 1], in1=xc, op0=ALU.mult, op1=ALU.add)
                OP("act", "copy", [("xc", b)], [("xcb", b)], out=xcb, in_=xc)
                for sb in range(NSB):
                    sl = slice(sb * 512, (sb + 1) * 512)
                    b1, b2 = next_bank(), next_bank()
                    mm_group(ps[b1][:, :], [(wab[:, cc, :], xcb[:, sl])], ("ps", b1),
                             [("xcb", b), ("wab", 2 * cc), ("wab", 2 * cc + 1), "wab"])
                    mm_group(ps[b2][:, :], [(wxb[:, cc, :], xcb[:, sl])], ("ps", b2),
                             [("xcb", b), ("wxb", 2 * cc), ("wxb", 2 * cc + 1), "wxb"])
                    OP("act", "activation", [("ps", b1), "pv"], [("rr", b, sb)], out=rr[:, sl], in_=ps[b1][:, :],
                       func=AF.Sigmoid, bias=pv[:, PV_BA + cc:PV_BA + cc + 1])
                    OP("act", "activation", [("ps", b2), "pv"], [("ii", b, sb)], out=ii[:, sl], in_=ps[b2][:, :],
                       func=AF.Sigmoid, bias=pv[:, PV_BX + cc:PV_BX + cc + 1])
                rrr = [("rr", b, sb) for sb in range(NSB)]
                iir = [("ii", b, sb) for sb in range(NSB)]
                OP("act", "activation", rrr + ["nsp8"], [("aa", b)], out=aa, in_=rr, func=AF.Exp, scale=nsp[:, 4 + cc:5 + cc])
                OP("act", "activation", rrr + ["nsp16"], [("a2", b)], out=a2, in_=rr, func=AF.Exp, scale=nsp[:, 8 + cc:9 + cc])
                OP("pool", "tensor_scalar", [("a2", b)], [("a2", b)], out=a2, in0=a2, scalar1=-1.0, scalar2=1.0, op0=ALU.mult,
                   op1=ALU.add)
                OP("act", "activation", [("a2", b)], [("a2", b)], out=a2, in_=a2, func=AF.Sqrt)
                OP("pool", "tensor_tensor", iir + [("xc", b)], iir, out=ii, in0=ii, in1=xc, op=ALU.mult)
                OP("pool", "tensor_tensor", iir + [("a2", b)], iir, out=ii, in0=ii, in1=a2, op=ALU.mult)
                if rg == 0:
                    OP("dve", "tensor_tensor_scan", iir + [("aa", b)], [hres_c], out=hc, data0=aa, data1=ii, initial=0.0,
                       op0=ALU.mult, op1=ALU.add)
                else:
                    OP("dve", "tensor_tensor_scan", iir + [("aa", b), hres_p], [hres_c], out=hc, data0=aa, data1=ii,
                       initial=hp[:, TL - 1:TL], op0=ALU.mult, op1=ALU.add)
                OP("act", "activation", [("gb", b)], [("gq", b)], out=gq, in_=gb[b], func=AF.Square)
                OP("pool", "tensor_scalar", [("gq", b)], [("gq", b)], out=gq, in0=gq, scalar1=0.044715, scalar2=1.0,
                   op0=ALU.mult, op1=ALU.add)
                OP("pool", "tensor_tensor", [("gq", b), ("gb", b)], [("gq", b)], out=gq, in0=gq, in1=gb[b], op=ALU.mult)
                OP("act", "activation", [("gq", b)], [("gq", b)], out=gq, in_=gq, func=AF.Sigmoid, scale=1.5957691216057308)
                OP("pool", "tensor_tensor", [("gq", b), ("gb", b)], [("gq", b)], out=gq, in0=gq, in1=gb[b], op=ALU.mult)
                OP("dve", "tensor_tensor", [("gq", b), hres_c], [("ybs", b)], out=ybs[b], in0=gq, in1=hc, op=ALU.mult)
                dma("sp", yall[4 + cc][:, t0:t0 + TL], ybs[b], [("ybs", b)], [("yall", 4 + cc, rg)])
        P.barrier()

        AR.off = layer_base
        Wb = AR.alloc([16, D], BF16)
        wbv = w_branch[l].rearrange("(k p) n -> p k n", p=128)
        for q in range(4):
            dma("pool", Wb[:, 4 * q:4 * q + 4, :], wbv[:, 4 * q:4 * q + 4, :], [], [("Wb", q)])
        Wb_r = [("Wb", q) for q in range(4)]
        ysb = [AR.alloc([16, 512], BF16) for _ in range(2)]
        gsb = [AR.alloc([4, 512], BF16) for _ in range(2)]
        mixT = [AR.alloc([KC, 512], BF16) for _ in range(2)]
        mtmp_l = [AR.alloc([512], F32) for _ in range(6)]
        macc_l = [AR.alloc([512], F32) for _ in range(2)]
        gview = gatesD.rearrange("(n c) p s -> c p n s", n=4)
        gi = 0
        for tr in range(NR):
            t0 = tr * 512
            yb_ = tr % 2
            dma("sp", ysb[yb_], yall[:, :, t0:t0 + 512].rearrange("k p t -> p k t"), [], [("ysb", yb_)])
            warm_pe(lo=0, hi=8)
            for cc in range(16):
                gbf = gi % 2
                macc = macc_l[gbf]
                mtmp = mtmp_l[3 * gbf:3 * gbf + 3]
                gi += 1
                dma("sp", gsb[gbf], gview[cc][:, :, t0:t0 + 512], [], [("gsb", gbf)])
                for n in range(4):
                    bk = next_bank()
                    pairs = [(Wb[:, 4 * n + k, cc * 128:(cc + 1) * 128], ysb[yb_][:, 4 * n + k, :]) for k in range(4)]
                    mm_group(ps[bk][:, :], pairs, ("ps", bk), [("ysb", yb_)] + Wb_r)
                    if n == 0:
                        OP("dve", "tensor_tensor", [("ps", bk), ("gsb", gbf)], [("macc", gbf)], out=macc, in0=ps[bk][:, :],
                           in1=gsb[gbf][:, 0, :], op=ALU.mult)
                    else:
                        mi = n - 1
                        OP("dve", "tensor_tensor", [("ps", bk), ("gsb", gbf)], [("mtmp", gbf, mi)], out=mtmp[mi],
                           in0=ps[bk][:, :], in1=gsb[gbf][:, n, :], op=ALU.mult)
                        if n < 3:
                            OP("pool", "tensor_tensor", [("macc", gbf), ("mtmp", gbf, mi)], [("macc", gbf)], out=macc, in0=macc,
                               in1=mtmp[mi], op=ALU.add)
                        else:
                            OP("pool", "tensor_tensor", [("macc", gbf), ("mtmp", gbf, mi)], [("mixT", yb_, cc)],
                               out=mixT[yb_][:, cc, :], in0=macc, in1=mtmp[mi], op=ALU.add)
            dma("sp", mixD[:, :, t0:t0 + 512].rearrange("k p t -> p k t"), mixT[yb_],
                [("mixT", yb_, cc) for cc in range(16)], [("mixD", tr)])
        P.barrier()

        AR.off = layer_base
        Wo = AR.alloc([KC, D], BF16)
        wov = w_out[l].rearrange("(k p) n -> p k n", p=128)
        for q in range(4):
            dma("pool", Wo[:, 4 * q:4 * q + 4, :], wov[:, 4 * q:4 * q + 4, :], [], [("Wo", q)])
        Wo_r = [("Wo", q) for q in range(4)]
        mx = [AR.alloc([KC, 512], BF16) for _ in range(2)]
        hxm = [AR.alloc([D], F32) for _ in range(2)]
        hi = 0
        for tr in range(NR):
            t0 = tr * 512
            mb_ = tr % 2
            dma("sp", mx[mb_], mixD[:, :, t0:t0 + 512].rearrange("k p t -> p k t"), [], [("mx", mb_)])
            if tr == 0:
                warm_pe(lo=0, hi=8)
            for ts in range(4):
                r0 = t0 + ts * 128
                hb = hi % 2
                hi += 1
                dma("sp", hxm[hb], src[r0:r0 + 128, :], [], [("hxm", hb)])
                for cb in range(4):
                    bk = next_bank()
                    pairs = [(mx[mb_][:, k, ts * 128:(ts + 1) * 128], Wo[:, k, cb * 512:(cb + 1) * 512]) for k in range(KC)]
                    mm_group(ps[bk][:, :], pairs, ("ps", bk), [("mx", mb_)] + Wo_r)
                    OP("dve", "tensor_tensor", [("ps", bk), ("hxm", hb)], [("hxm", hb)],
                       out=hxm[hb][:, cb * 512:(cb + 1) * 512], in0=hxm[hb][:, cb * 512:(cb + 1) * 512],
                       in1=ps[bk][:, :], op=ALU.add)
                dma("sp", hres[r0:r0 + 128, :], hxm[hb], [("hxm", hb)], [("hres", r0)])
        P.barrier()

        wgv = w_fg[l].rearrange("(k p) n -> p k n", p=128)
        wuv = w_fu[l].rearrange("(k p) n -> p k n", p=128)
        wdv = w_fd[l].rearrange("(f p) n -> p f n", p=128)
        for g in range(S // TGF):
            tok0 = g * TGF
            AR.off = layer_base
            aT = AR.alloc([FC, TGF], BF16)
            mark = AR.off
            uT2 = AR.alloc([KC, TGF], BF16)
            mark2 = AR.off
            norm_transpose(hres, tok0, TGF, 2 * l + 1, uT2, "uT2")
            P.barrier()
            AR.off = mark2
            wg = [AR.alloc([KC, 256], BF16) for _ in range(2)]
            wu = [AR.alloc([KC, 256], BF16) for _ in range(2)]
            sg = [AR.alloc([512], F32) for _ in range(2)]
            u_reads = [("uT2", i, hf) for i in range(TGF // 128) for hf in range(2)]
            si = 0
            for fb in range(DFF // 256):
                wbuf = fb % 2
                dma("pool", wg[wbuf], wgv[:, :, fb * 256:(fb + 1) * 256], [], [("wg", wbuf)])
                dma("pool", wu[wbuf], wuv[:, :, fb * 256:(fb + 1) * 256], [], [("wu", wbuf)])
                for sub in range(2):
                    f = fb * 2 + sub
                    for r in range(TGF // 512):
                        b1, b2 = next_bank(), next_bank()
                        rs = slice(r * 512, (r + 1) * 512)
                        mm_group(ps[b1][:, :], [(wg[wbuf][:, k, sub * 128:(sub + 1) * 128], uT2[:, k, rs]) for k in range(KC)],
                                 ("ps", b1), [("wg", wbuf)] + u_reads)
                        mm_group(ps[b2][:, :], [(wu[wbuf][:, k, sub * 128:(sub + 1) * 128], uT2[:, k, rs]) for k in range(KC)],
                                 ("ps", b2), [("wu", wbuf)] + u_reads)
                        sj = si % 2
                        si += 1
                        OP("act", "activation", [("ps", b1)], [("sg", sj)], out=sg[sj], in_=ps[b1][:, :], func=AF.Silu)
                        OP("dve", "tensor_tensor", [("ps", b2), ("sg", sj)], [("aT", f, r)], out=aT[:, f, rs],
                           in0=sg[sj], in1=ps[b2][:, :], op=ALU.mult)
            P.barrier()
            AR.off = mark
            wd = [AR.alloc([FC, 256], BF16) for _ in range(2)]
            hxs = [AR.alloc([256], F32) for _ in range(3)]
            a_reads = [("aT", f, r) for f in range(FC) for r in range(TGF // 512)]
            hi = 0
            for cb in range(8):
                wbuf = cb % 2
                cs = slice(cb * 256, (cb + 1) * 256)
                for q in range(4):
                    dma("pool", wd[wbuf][:, 11 * q:11 * q + 11, :], wdv[:, 11 * q:11 * q + 11, cs],
                        [], [("wd", wbuf, q)])
                wd_r = [("wd", wbuf, q) for q in range(4)]
                for i in range(TGF // 128):
                    r0 = tok0 + i * 128
                    hb = hi % 3
                    hi += 1
                    dma("sp", hxs[hb], hres[r0:r0 + 128, cs], [("hres", r0)], [("hxs", hb)])
                    bk = next_bank()
                    pairs = [(aT[:, f, i * 128:(i + 1) * 128], wd[wbuf][:, f, :]) for f in range(FC)]
                    mm_group(ps[bk][:, 0:256], pairs, ("ps", bk), a_reads + wd_r)
                    OP("dve", "tensor_tensor", [("ps", bk), ("hxs", hb)], [("hxs", hb)], out=hxs[hb], in0=hxs[hb],
                       in1=ps[bk][:, 0:256], op=ALU.add)
                    dma("sp", hres[r0:r0 + 128, cs], hxs[hb], [("hxs", hb)], [("hres", r0)])
            P.barrier()

    AR.reset()
    gbcf = AR.alloc([D], F32)
    dma("sp", gbcf, norms[2 * depth:2 * depth + 1, :].partition_broadcast(128), [], ["gbcf"])
    hxf = [AR.alloc([D], F32) for _ in range(2)]
    junkf = AR.alloc([D], BF16)
    stf = [AR.alloc([4], F32) for _ in range(2)]
    final_toks = []
    for i in range(NT):
        b = i % 2
        r0 = i * 128
        dma("sp", hxf[b], hres[r0:r0 + 128, :], [("hres", r0)], [("hxf", b)])
        OP("act", "activation", [("hxf", b)], ["junkf", ("stf", b, 0)], out=junkf, in_=hxf[b], func=AF.Square,
           accum_out=stf[b][:, 0:1])
        OP("act", "activation", [("stf", b, 0), "eps_col"], [("stf", b, 1)], out=stf[b][:, 1:2], in_=stf[b][:, 0:1],
           func=AF.Sqrt, scale=1.0 / D, bias=eps_col)
        OP("dve", "reciprocal", [("stf", b, 1)], [("stf", b, 2)], out=stf[b][:, 2:3], in_=stf[b][:, 1:2])
        OP("dve", "scalar_tensor_tensor", [("hxf", b), ("stf", b, 2), "gbcf"], [("hxf", b)], out=hxf[b], in0=hxf[b],
           scalar=stf[b][:, 2:3], in1=gbcf, op0=ALU.mult, op1=ALU.mult)
        final_toks.append(dma("sp", out[r0:r0 + 128, :], hxf[b], [("hxf", b)], [("out", i)]))
    P.emit(final_toks)
    return nc


_NC_CACHE = {}


def _pack_small(inp, depth):
    f = np.float32
    pvec = np.zeros((depth, 128, NPV), f)
    rvec = np.zeros((depth, NRV), f)
    for l in range(depth):
        pvec[l, :, PV_BG:PV_BG + 64] = inp["b_gate"][l].reshape(64, 128).T
        pvec[l, :, PV_CW:PV_CW + 16] = inp["lru_conv_w"][l].reshape(4, 4, 128).transpose(2, 1, 0).reshape(128, 16)
        pvec[l, :, PV_CB:PV_CB + 4] = inp["lru_conv_b"][l].reshape(4, 128).T
        pvec[l, :, PV_BA:PV_BA + 4] = inp["lru_ba"][l].reshape(4, 128).T
        pvec[l, :, PV_BX:PV_BX + 4] = inp["lru_bx"][l].reshape(4, 128).T
        pvec[l, :, PV_LAM:PV_LAM + 4] = inp["lru_lambda"][l].reshape(4, 128).T
        pvec[l, :, PV_SUB] = inp["diff_subln"][l]
        rvec[l, RV_LQ1:RV_LQ1 + 64] = inp["diff_lq1"][l]
        rvec[l, RV_LK1:RV_LK1 + 64] = inp["diff_lk1"][l]
        rvec[l, RV_LQ2:RV_LQ2 + 64] = inp["diff_lq2"][l]
        rvec[l, RV_LK2:RV_LK2 + 64] = inp["diff_lk2"][l]
        rvec[l, RV_SINK:RV_SINK + 8] = inp["swa_sinks"][l]
        rvec[l, RV_BF:RV_BF + 4] = inp["fox_b_f"][l]
    norms = np.zeros((2 * depth + 1, D), f)
    for l in range(depth):
        norms[2 * l] = inp["norm_mix"][l]
        norms[2 * l + 1] = inp["norm_ffn"][l]
    norms[2 * depth] = inp["norm_final"]
    return pvec, rvec, norms


def run(inputs, S, depth, n_cores=8):
    inp = {k: np.asarray(v, dtype=np.float32) for k, v in inputs.items()}
    B = inp["x"].shape[0]
    key = (S, depth)
    if key not in _NC_CACHE:
        _NC_CACHE[key] = build_program(S, depth)
    nc = _NC_CACHE[key]
    pvec, rvec, norms = _pack_small(inp, depth)
    shared = {
        "w_in": np.ascontiguousarray(inp["w_in"]),
        "w_branch": np.ascontiguousarray(inp["w_branch"].reshape(depth, 4 * BW, D)),
        "w_out": np.ascontiguousarray(inp["w_out"]),
        "w_ffn_gate": np.ascontiguousarray(inp["w_ffn_gate"]),
        "w_ffn_up": np.ascontiguousarray(inp["w_ffn_up"]),
        "w_ffn_down": np.ascontiguousarray(inp["w_ffn_down"]),
        "lru_wa": np.ascontiguousarray(inp["lru_wa"]),
        "lru_wx": np.ascontiguousarray(inp["lru_wx"]),
        "norms": norms, "pvec": pvec, "rvec": rvec,
    }
    active = [0, 1, 4, 5]
    zeros = np.zeros((S, D), np.float32)
    in_maps = []
    for c in range(n_cores):
        m = dict(shared)
        m["x"] = np.ascontiguousarray(inp["x"][active.index(c)]) if (c in active and active.index(c) < B) else zeros
        in_maps.append(m)
    res = run_bass_kernel_spmd(nc, in_maps, core_ids=list(range(n_cores)))
    return np.stack([res.results[active[b]]["out"] for b in range(B)], axis=0)


def kernel(**inputs):
    return run(inputs, 4096, 4)
```
